# Optimizing a Trainium2 kernel written in Bass

```python
import math
import jax, jax.numpy as jnp
from jax import lax
import numpy as np

D_MODEL = 2048
BATCH = 2
SEQ = 4096
DEPTH = 2

HEAD_DIM = 128
N_GROUPS = 4
GROUP_HEADS = (D_MODEL // HEAD_DIM) // N_GROUPS
GROUP_WIDTH = GROUP_HEADS * HEAD_DIM
MIX_WIDTH = N_GROUPS * GROUP_WIDTH
DIFF_HEADS = GROUP_HEADS
DIFF_QK_DIM = HEAD_DIM // 2
CONV_CH = GROUP_WIDTH
CONV_WIDTH = 31
GQA_Q_HEADS = GROUP_HEADS
GQA_KV_HEADS = GROUP_HEADS // 2
NA_HEADS = GROUP_HEADS
NA_KH_MAX = 8
NA_KW = 16
GRID_W = 64
Q_BLOCK = 128
ROPE_THETA = 10000.0
FFN_HIDDEN = ((8 * D_MODEL + 3 * 256 - 1) // (3 * 256)) * 256
EPS = 1e-6

SPLIT_SIZES = (
    DIFF_HEADS * 2 * DIFF_QK_DIM, DIFF_HEADS * 2 * DIFF_QK_DIM, DIFF_HEADS * HEAD_DIM,
    2 * CONV_CH,
    GQA_Q_HEADS * HEAD_DIM, GQA_KV_HEADS * HEAD_DIM, GQA_KV_HEADS * HEAD_DIM,
    NA_HEADS * HEAD_DIM, NA_HEADS * HEAD_DIM, NA_HEADS * HEAD_DIM,
)
IN_COLS = sum(SPLIT_SIZES)

kernel_name = "hybrid_parallel_heads_encoder"


def rmsnorm(x, g):
    xf = x.astype(jnp.float32)
    y = xf * lax.rsqrt(jnp.mean(xf * xf, axis=-1, keepdims=True) + EPS)
    return (y * g.astype(jnp.float32)).astype(x.dtype)


def layernorm(x, g, b):
    xf = x.astype(jnp.float32)
    mu = jnp.mean(xf, axis=-1, keepdims=True)
    var = jnp.mean(jnp.square(xf - mu), axis=-1, keepdims=True)
    y = (xf - mu) * lax.rsqrt(var + EPS)
    return (y * g.astype(jnp.float32) + b.astype(jnp.float32)).astype(x.dtype)


def rope(x, pos):
    d = x.shape[-1]
    half = d // 2
    inv = jnp.power(ROPE_THETA, -jnp.arange(0, d, 2, dtype=jnp.float32) / d)
    ang = pos[:, None] * inv[None, :]
    cos, sin = jnp.cos(ang), jnp.sin(ang)
    xf = x.astype(jnp.float32)
    x1, x2 = xf[..., :half], xf[..., half:]
    return jnp.concatenate([x1 * cos - x2 * sin, x1 * sin + x2 * cos], axis=-1).astype(x.dtype)


def axial_rope(x, row, col):
    half = x.shape[-1] // 2
    return jnp.concatenate([rope(x[..., :half], row), rope(x[..., half:], col)], axis=-1)


def diff_attention(q, k, v, lam_params, subln_g, layer_idx):
    B, S, _ = q.shape
    H, DK = DIFF_HEADS, DIFF_QK_DIM
    pos = jnp.arange(S, dtype=jnp.float32)
    q = rope(q.reshape(B, S, H, 2, DK).transpose(0, 2, 3, 1, 4), pos)
    k = rope(k.reshape(B, S, H, 2, DK).transpose(0, 2, 3, 1, 4), pos)
    v = v.reshape(B, S, H, HEAD_DIM).transpose(0, 2, 1, 3)
    lam_init = 0.8 - 0.6 * math.exp(-0.3 * layer_idx)
    lp = lam_params.astype(jnp.float32)
    lam = jnp.exp(jnp.sum(lp[0] * lp[1])) - jnp.exp(jnp.sum(lp[2] * lp[3])) + lam_init
    scale = DK ** -0.5
    nb = S // Q_BLOCK
    qb = q.reshape(B, H, 2, nb, Q_BLOCK, DK).transpose(3, 0, 1, 2, 4, 5)

    def block(qi):
        s = jnp.einsum('bhmqd,bhmkd->bhmqk', qi, k).astype(jnp.float32) * scale
        p = jax.nn.softmax(s, axis=-1)
        w = (p[:, :, 0] - lam * p[:, :, 1]).astype(v.dtype)
        return jnp.einsum('bhqk,bhkd->bhqd', w, v)

    o = lax.map(block, qb)
    o = rmsnorm(o, subln_g) * (1.0 - lam_init)
    return o.transpose(1, 0, 3, 2, 4).reshape(B, S, H * HEAD_DIM)


def conformer_conv(h, dw, dw_b, ln_g, ln_b, pw, pw_b):
    a, g = jnp.split(h, 2, axis=-1)
    u = a * jax.nn.sigmoid(g)
    pad = CONV_WIDTH // 2
    u = lax.conv_general_dilated(u, dw[:, None, :], window_strides=(1,), padding=[(pad, pad)],
                                 dimension_numbers=('NWC', 'WIO', 'NWC'),
                                 feature_group_count=CONV_CH) + dw_b
    u = jax.nn.silu(layernorm(u, ln_g, ln_b))
    return u @ pw + pw_b


def gqa_axial_attention(q, k, v, q_norm, k_norm):
    B, S, _ = q.shape
    HQ, HKV, R = GQA_Q_HEADS, GQA_KV_HEADS, GQA_Q_HEADS // GQA_KV_HEADS
    t = jnp.arange(S)
    row = (t // GRID_W).astype(jnp.float32)
    col = (t % GRID_W).astype(jnp.float32)
    q = axial_rope(rmsnorm(q.reshape(B, S, HQ, HEAD_DIM), q_norm).transpose(0, 2, 1, 3), row, col)
    k = axial_rope(rmsnorm(k.reshape(B, S, HKV, HEAD_DIM), k_norm).transpose(0, 2, 1, 3), row, col)
    v = v.reshape(B, S, HKV, HEAD_DIM).transpose(0, 2, 1, 3)
    scale = HEAD_DIM ** -0.5
    nb = S // Q_BLOCK
    qb = q.reshape(B, HKV, R, nb, Q_BLOCK, HEAD_DIM).transpose(3, 0, 1, 2, 4, 5)

    def block(qi):
        s = jnp.einsum('bgrqd,bgkd->bgrqk', qi, k).astype(jnp.float32) * scale
        p = jax.nn.softmax(s, axis=-1).astype(v.dtype)
        return jnp.einsum('bgrqk,bgkd->bgrqd', p, v)

    o = lax.map(block, qb)
    return o.transpose(1, 0, 4, 2, 3, 5).reshape(B, S, HQ * HEAD_DIM)


def neighbourhood_attention(q, k, v, rpb):
    B, S, _ = q.shape
    H = NA_HEADS
    rows = S // GRID_W
    kh = min(NA_KH_MAX, rows)
    kw = min(NA_KW, GRID_W)
    qr_blk = Q_BLOCK // GRID_W
    nb = S // Q_BLOCK
    band = min(kh + qr_blk - 1, rows)
    q = q.reshape(B, S, H, HEAD_DIM).transpose(0, 2, 1, 3).reshape(B, H, nb, Q_BLOCK, HEAD_DIM)
    k = k.reshape(B, S, H, HEAD_DIM).transpose(0, 2, 1, 3).reshape(B, H, rows, GRID_W, HEAD_DIM)
    v = v.reshape(B, S, H, HEAD_DIM).transpose(0, 2, 1, 3).reshape(B, H, rows, GRID_W, HEAD_DIM)
    blk = jnp.arange(nb)
    band_start = jnp.clip(blk * qr_blk - kh // 2, 0, rows - band)
    key_rows = band_start[:, None] + jnp.arange(band)
    kb = k[:, :, key_rows].reshape(B, H, nb, band * GRID_W, HEAD_DIM)
    vb = v[:, :, key_rows].reshape(B, H, nb, band * GRID_W, HEAD_DIM)
    q_local = jnp.arange(Q_BLOCK)
    q_row = blk[:, None] * qr_blk + (q_local // GRID_W)[None, :]
    q_col = jnp.broadcast_to((q_local % GRID_W)[None, :], (nb, Q_BLOCK))
    k_row = jnp.repeat(key_rows, GRID_W, axis=1)[:, None, :]
    k_col = jnp.tile(jnp.arange(GRID_W), band)[None, None, :]
    win_r = jnp.clip(q_row - kh // 2, 0, rows - kh)[:, :, None]
    win_c = jnp.clip(q_col - kw // 2, 0, GRID_W - kw)[:, :, None]
    mask = (k_row >= win_r) & (k_row < win_r + kh) & (k_col >= win_c) & (k_col < win_c + kw)
    ir = jnp.clip(k_row - q_row[:, :, None] + NA_KH_MAX - 1, 0, 2 * NA_KH_MAX - 2)
    ic = jnp.clip(k_col - q_col[:, :, None] + NA_KW - 1, 0, 2 * NA_KW - 2)
    bias = rpb[:, ir, ic].astype(jnp.float32)
    s = jnp.einsum('bhnqd,bhnkd->bhnqk', q, kb).astype(jnp.float32) * (HEAD_DIM ** -0.5) + bias[None]
    s = jnp.where(mask[None, None], s, -1e30)
    p = jax.nn.softmax(s, axis=-1).astype(vb.dtype)
    o = jnp.einsum('bhnqk,bhnkd->bhnqd', p, vb)
    return o.transpose(0, 2, 3, 1, 4).reshape(B, S, H * HEAD_DIM)


def setup_inputs(seed: int = 0) -> dict:
    key = jax.random.key(seed)
    ks = jax.random.split(key, 24)
    L, D, F = DEPTH, D_MODEL, FFN_HIDDEN
    f32 = jnp.float32

    def nrm(k, shape, scale):
        return jax.random.normal(k, shape, f32) * scale

    def gain(k, shape):
        return 1.0 + 0.05 * jax.random.normal(k, shape, f32)

    return {
        "x": nrm(ks[0], (BATCH, SEQ, D), 1.0),
        "norm_mix_pre": gain(ks[1], (L, D)),
        "norm_mix_post": gain(ks[2], (L, D)),
        "norm_ffn_pre": gain(ks[3], (L, D)),
        "norm_ffn_post": gain(ks[4], (L, D)),
        "w_in": nrm(ks[5], (L, D, IN_COLS), D ** -0.5),
        "w_out": nrm(ks[6], (L, MIX_WIDTH, D), MIX_WIDTH ** -0.5),
        "diff_lambda": nrm(ks[7], (L, 4, DIFF_QK_DIM), 0.1),
        "diff_subln": gain(ks[8], (L, HEAD_DIM)),
        "conv_dw": nrm(ks[9], (L, CONV_WIDTH, CONV_CH), CONV_WIDTH ** -0.5),
        "conv_dw_b": nrm(ks[10], (L, CONV_CH), 0.02),
        "conv_ln_g": gain(ks[11], (L, CONV_CH)),
        "conv_ln_b": nrm(ks[12], (L, CONV_CH), 0.02),
        "conv_pw": nrm(ks[13], (L, CONV_CH, CONV_CH), CONV_CH ** -0.5),
        "conv_pw_b": nrm(ks[14], (L, CONV_CH), 0.02),
        "gqa_q_norm": gain(ks[15], (L, HEAD_DIM)),
        "gqa_k_norm": gain(ks[16], (L, HEAD_DIM)),
        "na_rpb": nrm(ks[17], (L, NA_HEADS, 2 * NA_KH_MAX - 1, 2 * NA_KW - 1), 0.1),
        "ffn_gate": nrm(ks[18], (L, D, F), D ** -0.5),
        "ffn_up": nrm(ks[19], (L, D, F), D ** -0.5),
        "ffn_down": nrm(ks[20], (L, F, D), F ** -0.5),
    }


def reference(x, norm_mix_pre, norm_mix_post, norm_ffn_pre, norm_ffn_post, w_in, w_out,
              diff_lambda, diff_subln, conv_dw, conv_dw_b, conv_ln_g, conv_ln_b, conv_pw,
              conv_pw_b, gqa_q_norm, gqa_k_norm, na_rpb, ffn_gate, ffn_up, ffn_down):
    split_at = np.cumsum(SPLIT_SIZES)[:-1].tolist()
    for l in range(DEPTH):
        h = rmsnorm(x, norm_mix_pre[l])
        proj = h @ w_in[l]
        a_q, a_k, a_v, b_glu, c_q, c_k, c_v, d_q, d_k, d_v = jnp.split(proj, split_at, axis=-1)
        out_a = diff_attention(a_q, a_k, a_v, diff_lambda[l], diff_subln[l], l)
        out_b = conformer_conv(b_glu, conv_dw[l], conv_dw_b[l], conv_ln_g[l], conv_ln_b[l],
                               conv_pw[l], conv_pw_b[l])
        out_c = gqa_axial_attention(c_q, c_k, c_v, gqa_q_norm[l], gqa_k_norm[l])
        out_d = neighbourhood_attention(d_q, d_k, d_v, na_rpb[l])
        mixed = jnp.concatenate([out_a, out_b, out_c, out_d], axis=-1) @ w_out[l]
        x = x + rmsnorm(mixed, norm_mix_post[l])
        h = rmsnorm(x, norm_ffn_pre[l])
        f = (jax.nn.silu(h @ ffn_gate[l]) * (h @ ffn_up[l])) @ ffn_down[l]
        x = x + rmsnorm(f, norm_ffn_post[l])
    return x
```

```python
import contextlib
import numpy as np
import ml_dtypes
import concourse.bass as bass
import concourse.mybir as mybir
from concourse.bass_utils import run_bass_kernel_spmd

F32 = mybir.dt.float32
BF16 = mybir.dt.bfloat16
AF = mybir.ActivationFunctionType
ALU = mybir.AluOpType

L = 2
D = 2048
KC = 16
T = 1024
TG = 512
NTG = 2
FF = 5632
FC = 44
INC = 5120
EPS = 1e-6
NV = 208
KROWS = 1792
VCOLS = 1280
ENGS = ("pe", "act", "dve", "pool", "sp")
FUSED = True
KCUT = 0


class Buf:
    __slots__ = ("name", "ws", "rs", "sem", "cnt", "arena", "psum")

    def __init__(self, name, arena=False, psum=False):
        self.psum = psum
        self.name = name
        self.ws = {}
        self.rs = {}
        self.sem = None
        self.cnt = 0
        self.arena = arena


class Op:
    __slots__ = ("eng", "fn", "deps", "dma", "cc", "sig", "val", "dbuf", "ninc")


class Prog:
    def __init__(self):
        self.ops = {e: [] for e in ENGS}
        self.last_compute = {}
        self.arena_dmas = []
        self.fence_ops = []
        self.dma_bufs = []
        self.cc_cnt = 0

    def add(self, eng, fn, reads=(), writes=(), pw=(), dma=False, cc=False, ninc=1, sigbuf=None):
        op = Op()
        op.eng = eng
        op.fn = fn
        op.dma = dma
        op.cc = cc
        op.sig = False
        op.val = None
        op.dbuf = None
        op.ninc = ninc
        deps = {}
        for b in reads:
            for w in b.ws.values():
                deps[w] = True
            if b.psum:
                for r in b.rs.values():
                    if r.eng != eng and r not in deps:
                        deps[r] = False
        for b in list(writes) + list(pw):
            for r in b.rs.values():
                if r not in deps:
                    deps[r] = False
        for b in writes:
            for w in b.ws.values():
                if w not in deps:
                    deps[w] = False
        op.deps = deps
        if dma:
            db = sigbuf if sigbuf is not None else (list(writes) + list(pw))[0]
            if db.sem is None:
                self.dma_bufs.append(db)
                db.sem = True
            db.cnt += 16 * ninc
            op.val = db.cnt
            op.dbuf = db
            if any(b.arena for b in list(reads) + list(writes) + list(pw)):
                self.arena_dmas.append(op)
        elif cc:
            self.cc_cnt += 1
            op.val = self.cc_cnt
        else:
            self.last_compute[eng] = op
        wkey = ("dma", id(op.dbuf)) if dma else ("cc" if cc else eng)
        for b in reads:
            if dma or cc:
                b.rs[("dma", id(op.dbuf) if dma else "cc")] = op
            else:
                b.rs[eng] = op
        for b in writes:
            b.ws = {wkey: op}
            b.rs = {}
        for b in pw:
            b.ws[wkey] = op
        self.ops[eng].append(op)
        return op

    def new_phase(self):
        self.fence_ops = list(self.last_compute.values()) + list(self.arena_dmas)
        self.arena_dmas = []

    def abuf(self, name):
        if not hasattr(self, "_ab"):
            self._ab = {}
        if name not in self._ab:
            self._ab[name] = Buf(name, arena=True)
        return self.fence(self._ab[name])

    def fence(self, buf):
        buf.ws = {}
        buf.rs = {("f", i): o for i, o in enumerate(self.fence_ops)}
        return buf

    def finalize(self, nc, stack):
        engsem = {e: stack.enter_context(nc.semaphore("s_" + e)) for e in ENGS}
        ccsem = stack.enter_context(nc.semaphore("s_cc"))
        for i, b in enumerate(self.dma_bufs):
            b.sem = stack.enter_context(nc.semaphore("d%d" % i))
        for e in ENGS:
            for op in self.ops[e]:
                for d, raw in op.deps.items():
                    if d.dma or d.cc:
                        continue
                    if d.eng == op.eng and not op.dma and not op.cc:
                        if e == "pe":
                            continue
                    d.sig = True
        for e in ENGS:
            c = 0
            for op in self.ops[e]:
                if not op.dma and not op.cc and op.sig:
                    c += 1
                    op.val = c
        block = stack.enter_context(nc.Block())

        def emit(e, eng):
            waited = {}
            for op in self.ops[e]:
                for d, raw in op.deps.items():
                    if d.dma:
                        sem, key, v = d.dbuf.sem, ("d", id(d.dbuf)), d.val
                    elif d.cc:
                        sem, key, v = ccsem, "cc", d.val
                    else:
                        if d.eng == e and not op.dma and not op.cc and e == "pe":
                            continue
                        sem, key, v = engsem[d.eng], d.eng, d.val
                    if waited.get(key, 0) >= v:
                        continue
                    waited[key] = v
                    eng.wait_ge(sem, v)
                if op.fn is None:
                    continue
                res = op.fn(eng)
                if op.dma:
                    if not isinstance(res, (list, tuple)):
                        res = [res]
                    assert len(res) == op.ninc
                    for r in res:
                        r.then_inc(op.dbuf.sem, 16)
                elif op.cc:
                    res.then_inc(ccsem, 1)
                elif op.sig:
                    res.then_inc(engsem[e], 1)

        @block.tensor
        def _(eng):
            emit("pe", eng)

        @block.scalar
        def _(eng):
            emit("act", eng)

        @block.vector
        def _(eng):
            emit("dve", eng)

        @block.gpsimd
        def _(eng):
            emit("pool", eng)

        @block.sync
        def _(eng):
            emit("sp", eng)


def _mk_raw(fn):
    op = Op()
    op.eng = "pool"
    op.fn = fn
    op.deps = {}
    op.dma = False
    op.cc = False
    op.sig = False
    op.val = None
    op.dbuf = None
    op.ninc = 0
    return op


class Ring:
    def __init__(self, items):
        self.items = items
        self.i = 0

    def next(self):
        it = self.items[self.i % len(self.items)]
        self.i += 1
        return it


def build_program(seg_lo, seg_hi, fused, debug=False):
    nc = bass.Bass("TRN2", target_bir_lowering=False)
    P = Prog()
    DYN = {}
    stack = contextlib.ExitStack()
    first_seg, last_seg = seg_lo == 0, seg_hi == 2
    layers = sorted({0 if s <= 1 else 1 for s in range(seg_lo, seg_hi + 1)} | {1 if s >= 1 else 0 for s in range(seg_lo, seg_hi + 1)})

    in_names = []

    def din(name, shape, dt=F32):
        in_names.append(name)
        return nc.dram_tensor(name, list(shape), dt, kind="ExternalInput").ap()

    def dout(name, shape, dt=F32):
        return nc.dram_tensor(name, list(shape), dt, kind="ExternalOutput").ap()

    class _LazyW(dict):
        def __init__(self, l):
            super().__init__()
            self.l = l

        def __missing__(self, key):
            shp = {"w_in": [D, INC], "w_out": [D, D], "gate": [D, FF], "up": [D, FF], "down": [FF, D], "pw": [512, 512],
                   "vec": [128, NV], "lamb": [128, 256], "bias": [128, 28 * 128]}[key]
            v = din("%s%d" % (key, self.l), shp)
            self[key] = v
            return v
    W = {l: _LazyW(l) for l in layers}
    tabs_d = din("tabs", [128, 4 * T])
    dmask_d = din("dmask", [128, 8 * 7 * 128])
    perm_d = din("perm", [128, 128])
    halo_d = din("halo", [128, 2])
    if first_seg:
        xT_d = din("xT", [D, T])
    else:
        stx_in = din("stx_in", [128, KC * T])
        stq_in = din("stq_in", [128, 12 * T], BF16)
    if last_seg:
        out_d = dout("outT", [D, T])
    KSEG = [("kA", 0, 512), ("kC", 512, 256), ("kD", 768, 512), ("uu", 1280, 512)]
    VSEG = [("vA", 0, 512), ("vC", 512, 256), ("vD", 768, 512)]
    NAMES = [k[0] for k in KSEG] + [v[0] for v in VSEG]

    def kloc(row):
        for name, r0, n in KSEG:
            if r0 <= row < r0 + n:
                return name, row - r0, n
        raise ValueError(row)

    def vloc(col):
        for name, c0, n in VSEG:
            if c0 <= col < c0 + n:
                return name, col - c0, n
        raise ValueError(col)

    def snd_shape(name):
        for nm, _, n in KSEG:
            if nm == name:
                return [n, T]
        for nm, _, n in VSEG:
            if nm == name:
                return [T, n]

    def gth_shape(name):
        sh = snd_shape(name)
        return [4 * sh[0], sh[1]]
    SND, GTH, SNDH, GTHH = {}, {}, {}, {}
    if not fused:
        if not last_seg:
            stx_out = dout("stx_out", [128, KC * T])
            stq_out = dout("stq_out", [128, 12 * T], BF16)
            SND[seg_lo] = {n_: dout("snd_" + n_, snd_shape(n_), BF16) for n_ in NAMES}
        if not first_seg:
            GTH[seg_lo - 1] = {n_: din("gth_" + n_, gth_shape(n_), BF16) for n_ in NAMES}
    else:
        for s_ in (0, 1):
            SNDH[s_] = {n_: nc.dram_tensor("snd%d_%s" % (s_, n_), snd_shape(n_), BF16) for n_ in NAMES}
            GTHH[s_] = {n_: nc.dram_tensor("gth%d_%s" % (s_, n_), gth_shape(n_), BF16) for n_ in NAMES}
            SND[s_] = {n_: SNDH[s_][n_].ap() for n_ in NAMES}
            GTH[s_] = {n_: GTHH[s_][n_].ap() for n_ in NAMES}
    B_snd = {s_: {n_: Buf("snd%d%s" % (s_, n_)) for n_ in NAMES} for s_ in (0, 1)}
    B_gth = {s_: {n_: Buf("gth%d%s" % (s_, n_)) for n_ in NAMES} for s_ in (0, 1)}
    B_out = Buf("out")
    B_st = Buf("stout")
    if debug:
        dbg_cat = dout("dbg_cat", [NTG, 128, KC * TG], BF16)
        dbg_xmid = dout("dbg_xmid", [128, KC * T])
        dbg_y = dout("dbg_y", [128, 4 * TG])
        dbg_zs = dout("dbg_zs", [128, 4 * TG], BF16)
    B_dbg = Buf("dbg")
    winK = {s_: nc.dram_tensor("winK%d" % s_, [1024, 1792], BF16) for s_ in (0, 1)}
    winV = {s_: nc.dram_tensor("winV%d" % s_, [1792, 512], BF16) for s_ in (0, 1)}
    B_winK = {s_: Buf("winK%d" % s_) for s_ in (0, 1)}
    B_winV = {s_: Buf("winV%d" % s_) for s_ in (0, 1)}

    def sb(name, shape, dt):
        return stack.enter_context(nc.sbuf_tensor("t_" + name, list(shape), dt))

    xT = sb("xT", [128, KC, T], F32)
    qT = sb("qT", [128, 12, T], BF16)
    ain = sb("ain", [128, KC, TG], BF16)
    B_x = [Buf("x%d" % i) for i in range(NTG)]
    B_q = [Buf("q%d" % i) for i in range(NTG)]
    B_ain = Buf("ain")
    wbufs = Ring([(sb("wb%d" % i, [128, 4096], BF16), Buf("wb%d" % i)) for i in range(2)])
    ptp = Ring([(sb("pt%d" % i, [128, TG], BF16), Buf("pt%d" % i)) for i in range(3)])
    sqp = Ring([(sb("sq%d" % i, [128, TG], BF16), Buf("sq%d" % i)) for i in range(3)])
    f32p = Ring([(sb("f%d" % i, [128, TG], F32), Buf("f%d" % i)) for i in range(4)])
    stgp = Ring([(sb("stg%d" % i, [128, TG], BF16), Buf("stg%d" % i)) for i in range(2)])
    vec = {l: sb("vec%d" % l, [128, NV], F32) for l in layers}
    lamb = {l: sb("lamb%d" % l, [128, 256], F32) for l in layers}
    lamv = {l: sb("lamv%d" % l, [128, 8], F32) for l in layers}
    B_vec = {l: Buf("vec%d" % l) for l in layers}
    B_lamv = {l: Buf("lamv%d" % l) for l in layers}
    perm = sb("perm", [128, 128], BF16)
    onesD = sb("onesD", [128, 128], BF16)
    ones128 = sb("ones128", [128, 128], BF16)
    ones512 = sb("ones512", [128, 128], BF16)
    ones1 = sb("ones1", [128, 128], BF16)
    halo = sb("halo", [128, 2], F32)
    epsc = sb("epsc", [128, 1], F32)
    B_const = Buf("const")
    B_perm = Buf("perm")
    B_halo = Buf("halo")
    ARENA_F32 = 13824
    arena = sb("arena", [128, ARENA_F32], F32)

    def a32(off, n):
        return arena[:, off:off + n]

    def a16(off, n):
        return arena[:, off:off + n].bitcast(BF16)

    ps = [stack.enter_context(nc.psum_tensor("ps%d" % i, [128, 512], F32)) for i in range(8)]
    B_ps = [Buf("ps%d" % i, psum=True) for i in range(8)]

    def mm(out, pairs, first=True, last=True):
        def fn(e):
            n = len(pairs)
            ins = None
            for i, (l_, r_) in enumerate(pairs):
                ins = e.matmul(out, l_, r_, start=(first and i == 0), stop=(last and i == n - 1))
            return ins
        return fn

    def dma(out, in_):
        return lambda e: e.dma_start(out=out, in_=in_)

    def act(out, in_, func, scale=None, bias=None):
        kw = {}
        if scale is not None:
            kw["scale"] = scale
        if bias is not None:
            kw["bias"] = bias
        return lambda e: e.activation(out=out, in_=in_, func=func, **kw)

    def tt(out, a, b, op):
        return lambda e: e.tensor_tensor(out=out, in0=a, in1=b, op=op)

    def ts(out, a, s1, s2, op0, op1):
        return lambda e: e.tensor_scalar(out=out, in0=a, scalar1=s1, scalar2=s2, op0=op0, op1=op1)

    def stt(out, a, s, b, op0, op1):
        return lambda e: e.scalar_tensor_tensor(out=out, in0=a, scalar=s, in1=b, op0=op0, op1=op1)

    def tgs(tg):
        return slice(tg * TG, (tg + 1) * TG)

    def recip(out, bout, in_, bin_):
        P.add("act", act(out, in_, AF.Ln), reads=[bin_], writes=[bout])
        P.add("act", act(out, out, AF.Exp, scale=-1.0), reads=[bout], writes=[bout])

    def rsqrt(out, bout, in_, bin_):
        P.add("act", act(out, in_, AF.Ln, bias=epsc[:, 0:1]), reads=[bin_, B_const], writes=[bout])
        P.add("act", act(out, out, AF.Exp, scale=-0.5), reads=[bout], writes=[bout])

    def setup_consts():
        for t_, v in ((onesD, 1.0 / D), (ones128, 1.0 / 128), (ones512, 1.0 / 512), (ones1, 1.0)):
            P.add("dve", (lambda e, t_=t_, v=v: e.memset(t_[:], v)), pw=[B_const])
        P.add("dve", (lambda e: e.memset(epsc[:], EPS)), pw=[B_const])
        P.add("pool", dma(perm[:], perm_d), writes=[B_perm], dma=True)
        P.add("sp", dma(halo[:], halo_d), writes=[B_halo], dma=True)
        for l in layers:
            P.add("sp", dma(vec[l][:], W[l]["vec"]), writes=[B_vec[l]], dma=True)
            bl = Buf("lamb")
            P.add("sp", dma(lamb[l][:], W[l]["lamb"]), writes=[bl], dma=True)
            lv = lamv[l]
            lam_init = 0.8 - 0.6 * float(np.exp(-0.3 * l))
            tmpf, btmp = f32p.next()
            P.add("dve", tt(tmpf[:, 0:64], lamb[l][:, 0:64], lamb[l][:, 64:128], ALU.mult), reads=[bl], writes=[btmp])
            P.add("dve", (lambda e, lv=lv, tmpf=tmpf: e.reduce_sum(out=lv[:, 0:1], in_=tmpf[:, 0:64], axis=mybir.AxisListType.X)),
                  reads=[btmp], pw=[B_lamv[l]])
            tmpf2, btmp2 = f32p.next()
            P.add("dve", tt(tmpf2[:, 0:64], lamb[l][:, 128:192], lamb[l][:, 192:256], ALU.mult), reads=[bl], writes=[btmp2])
            P.add("dve", (lambda e, lv=lv, tmpf2=tmpf2: e.reduce_sum(out=lv[:, 1:2], in_=tmpf2[:, 0:64], axis=mybir.AxisListType.X)),
                  reads=[btmp2], pw=[B_lamv[l]])
            P.add("act", act(lv[:, 2:4], lv[:, 0:2], AF.Exp), reads=[B_lamv[l]], pw=[B_lamv[l]])
            P.add("dve", stt(lv[:, 4:5], lv[:, 3:4], -lam_init, lv[:, 2:3], ALU.add, ALU.subtract), reads=[B_lamv[l]], pw=[B_lamv[l]])
            P.add("dve", (lambda e, lv=lv, l=l, li=lam_init: e.tensor_scalar_mul(out=lv[:, 5:6], in0=vec[l][:, 204:205], scalar1=1.0 - li)),
                  reads=[B_vec[l]], pw=[B_lamv[l]])

    def rmsnorm_to_ain(l, tg, gcol):
        for c in range(KC):
            sq, bsq = sqp.next()
            P.add("act", act(sq[:], xT[:, c, tgs(tg)], AF.Square), reads=[B_x[tg]], writes=[bsq])
            P.add("pe", mm(ps[6][:], [(onesD[:], sq[:])], first=(c == 0), last=(c == KC - 1)), reads=[bsq, B_const],
                  writes=[B_ps[6]] if c == 0 else [], pw=[] if c == 0 else [B_ps[6]])
        rstd, brstd = f32p.next()
        rsqrt(rstd[:], brstd, ps[6][:], B_ps[6])
        for c in range(KC):
            P.add("dve", stt(ain[:, c, :], xT[:, c, tgs(tg)], vec[l][:, gcol + c:gcol + c + 1], rstd[:], ALU.mult, ALU.mult),
                  reads=[B_x[tg], brstd, B_vec[l]], pw=[B_ain])

    def load_w(segs, kchunks, krow0=0):
        wt, bw = wbufs.next()
        tot = sum(s[2] for s in segs)
        assert kchunks * tot <= 4096
        view = wt[:, 0:kchunks * tot].rearrange("p (k c) -> p k c", c=tot)
        off = 0
        for i, (Wd, c0, n) in enumerate(segs):
            src = Wd[krow0:krow0 + kchunks * 128, c0:c0 + n].rearrange("(k p) c -> p k c", p=128)
            P.add("pool", dma(view[:, :, off:off + n], src), writes=[bw] if i == 0 else [], pw=[] if i == 0 else [bw], dma=True)
            off += n
        return view, bw

    gen_ps = Ring([0, 1, 2, 3])

    def phase1(l, seg, tg):
        P.new_phase()
        tabs = a32(0, 4 * TG).rearrange("p (f t) -> p f t", f=4)
        B_tabs = P.abuf("tabs")
        P.add("sp", dma(tabs, tabs_d.rearrange("p (f t) -> p f t", f=4)[:, :, tgs(tg)]), writes=[B_tabs], dma=True)
        if KCUT == 1:
            return
        rmsnorm_to_ain(l, tg, 0)
        if KCUT == 2:
            return
        w_in = W[l]["w_in"]

        def rope(psb, kind, dest, dest_bufs_pw, gcol=None):
            if kind == "plain":
                P.add("act", act(dest, ps[psb][:], AF.Copy), reads=[B_ps[psb]], pw=dest_bufs_pw)
                return
            ci, si = (0, 1) if kind == "A" else (2, 3)
            if kind == "C":
                sq, bsq = sqp.next()
                P.add("act", act(sq[:], ps[psb][:], AF.Square), reads=[B_ps[psb]], writes=[bsq])
                P.add("pe", mm(ps[7][:], [(ones128[:], sq[:])]), reads=[bsq, B_const], writes=[B_ps[7]])
                rstd2, brstd2 = f32p.next()
                rsqrt(rstd2[:], brstd2, ps[7][:], B_ps[7])
            xb, bxb = sqp.next()
            if kind == "C":
                P.add("act", act(xb[:], ps[psb][:], AF.Copy, scale=vec[l][:, gcol:gcol + 1]), reads=[B_ps[psb], B_vec[l]], writes=[bxb])
            else:
                P.add("act", act(xb[:], ps[psb][:], AF.Copy), reads=[B_ps[psb]], writes=[bxb])
            P.add("pe", mm(ps[5][:], [(perm[:], xb[:])]), reads=[bxb, B_perm], writes=[B_ps[5]])
            t1, bt1 = f32p.next()
            if kind == "C":
                P.add("dve", stt(t1[:], ps[psb][:], vec[l][:, gcol:gcol + 1], tabs[:, ci, :], ALU.mult, ALU.mult),
                      reads=[B_ps[psb], B_vec[l], B_tabs], writes=[bt1])
            else:
                P.add("dve", tt(t1[:], ps[psb][:], tabs[:, ci, :], ALU.mult), reads=[B_ps[psb], B_tabs], writes=[bt1])
            t2, bt2 = f32p.next()
            P.add("dve", tt(t2[:], ps[5][:], tabs[:, si, :], ALU.mult), reads=[B_ps[5], B_tabs], writes=[bt2])
            if kind == "C":
                P.add("dve", tt(t1[:], t1[:], t2[:], ALU.add), reads=[bt1, bt2], writes=[bt1])
                P.add("dve", tt(dest, t1[:], rstd2[:], ALU.mult), reads=[bt1, brstd2], pw=dest_bufs_pw)
            else:
                P.add("dve", tt(dest, t1[:], t2[:], ALU.add), reads=[bt1, bt2], pw=dest_bufs_pw)

        def to_sendK(kind, psb, row0, gcol=None):
            stg, bstg = stgp.next()
            rope(psb, kind, stg[:], [bstg], gcol)
            nm_, lr_, _n = kloc(row0)
            P.add("sp", dma(SND[seg][nm_][lr_:lr_ + 128, tgs(tg)], stg[:]), reads=[bstg], pw=[B_snd[seg][nm_]], dma=True, sigbuf=bstg)

        groups = [
            ((0, 1), ("qA", 0)), ((2, 3), ("qA", 2)), ((4, 5), ("kA", 0)), ((6, 7), ("kA", 256)),
            ((12, 16), ("glu", 0)), ((13, 17), ("glu", 1)), ((14, 18), ("glu", 2)), ((15, 19), ("glu", 3)),
            ((20, 21), ("qC", 4)), ((22, 23), ("qC", 6)), ((24, 25), ("kC", 512)),
            ((28, 29), ("qD", 8)), ((30, 31), ("qD", 10)), ((32, 33), ("kD", 768)), ((34, 35), ("kD", 1024)),
        ]
        if KCUT == 3:
            groups = groups[:1]
        if KCUT == 4:
            groups = groups[:5]
        for (c0, c1), (kind, arg) in groups:
            if c1 == c0 + 1:
                view, bw = load_w([(w_in, c0 * 128, 256)], KC)
            else:
                view, bw = load_w([(w_in, c0 * 128, 128), (w_in, c1 * 128, 128)], KC)
            banks = []
            for j in range(2):
                pb = gen_ps.next()
                banks.append(pb)
                P.add("pe", mm(ps[pb][:], [(view[:, kc, j * 128:(j + 1) * 128], ain[:, kc, :]) for kc in range(KC)]),
                      reads=[bw, B_ain], writes=[B_ps[pb]])
            if kind == "glu":
                sg, bsg = f32p.next()
                P.add("act", act(sg[:], ps[banks[1]][:], AF.Sigmoid), reads=[B_ps[banks[1]]], writes=[bsg])
                stg, bstg = stgp.next()
                P.add("dve", tt(stg[:], ps[banks[0]][:], sg[:], ALU.mult), reads=[B_ps[banks[0]], bsg], writes=[bstg])
                P.add("sp", dma(SND[seg]["uu"][arg * 128:(arg + 1) * 128, tgs(tg)], stg[:]), reads=[bstg], pw=[B_snd[seg]["uu"]], dma=True, sigbuf=bstg)
                continue
            for j in range(2):
                pb = banks[j]
                if kind == "qA":
                    rope(pb, "A", qT[:, arg + j, tgs(tg)], [B_q[tg]])
                elif kind == "qC":
                    rope(pb, "C", qT[:, arg + j, tgs(tg)], [B_q[tg]], gcol=205)
                elif kind == "qD":
                    rope(pb, "plain", qT[:, arg + j, tgs(tg)], [B_q[tg]])
                elif kind == "kA":
                    to_sendK("A", pb, arg + j * 128)
                elif kind == "kC":
                    to_sendK("C", pb, arg + j * 128, gcol=206)
                elif kind == "kD":
                    to_sendK("plain", pb, arg + j * 128)
        if KCUT in (3, 4, 5):
            return
        for wc0, vc0 in ((1024, 0), (1280, 256), (3328, 512), (4608, 768), (4864, 1024)):
            view, bw = load_w([(w_in, wc0, 256)], KC)
            for t4 in range(4):
                pb = gen_ps.next()
                P.add("pe", mm(ps[pb][:, 0:256], [(ain[:, kc, t4 * 128:(t4 + 1) * 128], view[:, kc, :]) for kc in range(KC)]),
                      reads=[bw, B_ain], writes=[B_ps[pb]])
                stg, bstg = stgp.next()
                P.add("act", act(stg[:, 0:256], ps[pb][:, 0:256], AF.Copy), reads=[B_ps[pb]], writes=[bstg])
                r0 = tg * TG + t4 * 128
                nm_, lc_, _n = vloc(vc0)
                P.add("sp", dma(SND[seg][nm_][r0:r0 + 128, lc_:lc_ + 256], stg[:, 0:256]), reads=[bstg], pw=[B_snd[seg][nm_]], dma=True, sigbuf=bstg)

    def exchange(seg):
        if not fused:
            return
        for n_ in NAMES:
            P.add("pool", (lambda e, n_=n_: e.collective_compute("AllGather", ALU.bypass, replica_groups=[[0, 1, 2, 3], [4, 5, 6, 7]],
                                                                 ins=[SNDH[seg][n_].ap().opt()], outs=[GTHH[seg][n_].ap().opt()])),
                  reads=[B_snd[seg][n_]], writes=[B_gth[seg][n_]], cc=True)

    def make_windows(seg):
        wK, wV = winK[seg], winV[seg]
        gkD = GTH[seg]["kD"].rearrange("(r k) t -> r k t", r=4)
        guu = GTH[seg]["uu"].rearrange("(r k) t -> r k t", r=4)
        gvD = GTH[seg]["vD"].rearrange("(r k) t -> r k t", r=4)

        def kcopies(e):
            return [
                e.dma_start(out=wK[0:512, 0:384], in_=gkD[bass.ds(DYN["lb"], 1), :, 640:1024].rearrange("o k t -> (o k) t")),
                e.dma_start(out=wK[0:512, 1408:1792], in_=gkD[bass.ds(DYN["rb"], 1), :, 0:384].rearrange("o k t -> (o k) t")),
                e.dma_start(out=wK[512:1024, 0:384], in_=guu[bass.ds(DYN["lb"], 1), :, 640:1024].rearrange("o k t -> (o k) t")),
                e.dma_start(out=wK[512:1024, 1408:1792], in_=guu[bass.ds(DYN["rb"], 1), :, 0:384].rearrange("o k t -> (o k) t")),
            ]
        P.add("pool", kcopies, reads=[B_gth[seg]["kD"], B_gth[seg]["uu"]], writes=[B_winK[seg]], dma=True, ninc=4)
        if fused:
            P.add("sp", dma(wK[0:512, 384:1408], SND[seg]["kD"]), reads=[B_snd[seg]["kD"]], pw=[B_winK[seg]], dma=True)
            P.add("sp", dma(wK[512:1024, 384:1408], SND[seg]["uu"]), reads=[B_snd[seg]["uu"]], pw=[B_winK[seg]], dma=True)
        else:
            def kown(e):
                return [
                    e.dma_start(out=wK[0:512, 384:1408], in_=gkD[bass.ds(DYN["rk"], 1), :, :].rearrange("o k t -> (o k) t")),
                    e.dma_start(out=wK[512:1024, 384:1408], in_=guu[bass.ds(DYN["rk"], 1), :, :].rearrange("o k t -> (o k) t")),
                ]
            P.add("pool", kown, reads=[B_gth[seg]["kD"], B_gth[seg]["uu"]], pw=[B_winK[seg]], dma=True, ninc=2)

        def vcopies(e):
            return [
                e.dma_start(out=wV[0:384, :], in_=gvD[bass.ds(DYN["lb"], 1), 640:1024, :].rearrange("o k t -> (o k) t")),
                e.dma_start(out=wV[1408:1792, :], in_=gvD[bass.ds(DYN["rb"], 1), 0:384, :].rearrange("o k t -> (o k) t")),
            ]
        P.add("pool", vcopies, reads=[B_gth[seg]["vD"]], writes=[B_winV[seg]], dma=True, ninc=2)
        if fused:
            P.add("sp", dma(wV[384:1408, :], SND[seg]["vD"]), reads=[B_snd[seg]["vD"]], pw=[B_winV[seg]], dma=True)
        else:
            P.add("pool", (lambda e: e.dma_start(out=wV[384:1408, :], in_=gvD[bass.ds(DYN["rk"], 1), :, :].rearrange("o k t -> (o k) t"))),
                  reads=[B_gth[seg]["vD"]], pw=[B_winV[seg]], dma=True)

    def phase2a(l, seg, tg):
        cat = ain
        P.new_phase()
        UW = 542
        uw = a16(0, 4 * UW // 2).rearrange("p (c t) -> p c t", c=4)
        y = a32(1088, 4 * TG).rearrange("p (c t) -> p c t", c=4)
        acc1 = a32(3136, TG)
        zs = a16(3648, 4 * TG // 2).rearrange("p (c t) -> p c t", c=4)
        mean = a32(4672, TG)
        B_uw, B_y, B_acc1, B_zs, B_mean = (P.abuf(n) for n in ("uw", "y", "acc1", "zs", "mean"))
        U0 = 1280

        c0w = 369 if tg == 0 else 881
        P.add("sp", dma(uw, winK[seg][512:1024, c0w:c0w + 542].rearrange("(c p) t -> p c t", p=128)), reads=[B_winK[seg]], writes=[B_uw], dma=True)
        hs = (slice(0, 15), 0) if tg == 0 else (slice(527, 542), 1)
        P.add("dve", (lambda e: e.tensor_scalar_mul(out=uw[:, :, hs[0]], in0=uw[:, :, hs[0]], scalar1=halo[:, hs[1]:hs[1] + 1])),
              reads=[B_uw, B_halo], writes=[B_uw])
        V = vec[l]
        for c in range(4):
            P.add("dve", (lambda e, c=c: e.tensor_scalar_mul(out=y[:, c, :], in0=uw[:, c, 0:TG], scalar1=V[:, 64 + c * 31:65 + c * 31])),
                  reads=[B_uw, B_vec[l]], pw=[B_y])
            P.add("dve", (lambda e, c=c: e.tensor_scalar_mul(out=acc1, in0=uw[:, c, 1:1 + TG], scalar1=V[:, 65 + c * 31:66 + c * 31])),
                  reads=[B_uw, B_vec[l]], writes=[B_acc1])
            for j in range(2, 31):
                dst, bd = (y[:, c, :], B_y) if j % 2 == 0 else (acc1, B_acc1)
                P.add("dve", stt(dst, uw[:, c, j:j + TG], V[:, 64 + c * 31 + j:65 + c * 31 + j], dst, ALU.mult, ALU.add),
                      reads=[B_uw, B_vec[l], bd], pw=[bd])
            P.add("dve", stt(y[:, c, :], acc1, V[:, 188 + c:189 + c], y[:, c, :], ALU.add, ALU.add), reads=[B_acc1, B_y, B_vec[l]], pw=[B_y])
            yb, byb = sqp.next()
            P.add("act", act(yb[:], y[:, c, :], AF.Copy), reads=[B_y], writes=[byb])
            P.add("pe", mm(ps[4][:], [(ones512[:], yb[:])], first=(c == 0), last=(c == 3)), reads=[byb, B_const],
                  writes=[B_ps[4]] if c == 0 else [], pw=[] if c == 0 else [B_ps[4]])
            ysq, bysq = sqp.next()
            P.add("act", act(ysq[:], y[:, c, :], AF.Square), reads=[B_y], writes=[bysq])
            P.add("pe", mm(ps[5][:], [(ones512[:], ysq[:])], first=(c == 0), last=(c == 3)), reads=[bysq, B_const],
                  writes=[B_ps[5]] if c == 0 else [], pw=[] if c == 0 else [B_ps[5]])
        if debug and seg == 0 and tg == 0:
            P.add("sp", dma(dbg_y.rearrange("p (c t) -> p c t", c=4), y), reads=[B_y], pw=[B_dbg], dma=True, sigbuf=B_y)
        P.add("act", act(mean, ps[4][:], AF.Copy), reads=[B_ps[4]], writes=[B_mean])
        var, bvar = a32(5184, TG), P.abuf("cvar")
        P.add("dve", stt(var[:], mean, -1.0, mean, ALU.mult, ALU.mult), reads=[B_mean], writes=[bvar])
        P.add("dve", tt(var[:], var[:], ps[5][:], ALU.add), reads=[bvar, B_ps[5]], writes=[bvar])
        rsqrt(var[:], bvar, var[:], bvar)
        for c in range(4):
            t1, bt1 = f32p.next()
            P.add("dve", tt(t1[:], y[:, c, :], mean, ALU.subtract), reads=[B_y, B_mean], writes=[bt1])
            P.add("dve", tt(t1[:], t1[:], var[:], ALU.mult), reads=[bt1, bvar], writes=[bt1])
            P.add("act", act(zs[:, c, :], t1[:], AF.Silu, scale=V[:, 192 + c:193 + c], bias=V[:, 196 + c:197 + c]),
                  reads=[bt1, B_vec[l]], pw=[B_zs])
        if debug and seg == 0 and tg == 0:
            P.add("sp", dma(dbg_zs.rearrange("p (c t) -> p c t", c=4), zs), reads=[B_zs], pw=[B_dbg], dma=True, sigbuf=B_zs)
        view, bw = load_w([(W[l]["pw"], 0, 512)], 4)
        for oc in range(4):
            pb = gen_ps.next()
            P.add("pe", mm(ps[pb][:], [(view[:, kc, oc * 128:(oc + 1) * 128], zs[:, kc, :]) for kc in range(4)]),
                  reads=[bw, B_zs], writes=[B_ps[pb]])
            P.add("act", act(cat[:, 4 + oc, :], ps[pb][:], AF.Identity, bias=V[:, 200 + oc:201 + oc]), reads=[B_ps[pb], B_vec[l]], pw=[B_ain])

        P.new_phase()
        kq = [a16(r * 512, 512) for r in range(4)]
        vq = [a16(2048 + r * 512, 512).rearrange("p (t d) -> p t d", d=128) for r in range(4)]
        bkq = [P.abuf("kq%d" % r) for r in range(4)]
        bvq = [P.abuf("vq%d" % r) for r in range(4)]
        biasT = a32(4096, 28 * 128).rearrange("p (n q) -> p n q", q=128)
        dm = a16(7680, 4 * 7 * 64).rearrange("p (i j q) -> p i j q", i=4, j=7)
        B_bias = P.abuf("biasT")
        B_dm = P.abuf("dm")
        P.add("sp", dma(biasT, W[l]["bias"].rearrange("p (n q) -> p n q", q=128)), writes=[B_bias], dma=True)
        P.add("pool", dma(dm, dmask_d.rearrange("p (i j q) -> p i j q", i=8, j=7)[:, tg * 4:(tg + 1) * 4]), writes=[B_dm], dma=True)
        st_ring = Ring([0, 1])

        def load_kv(krow0, vcol0):
            for r in range(4):
                kn_, lr_, kn = kloc(krow0)
                vn_, lc_, _vn = vloc(vcol0)
                P.add("sp", dma(kq[r], GTH[seg][kn_][r * kn + lr_:r * kn + lr_ + 128, :]), reads=[B_gth[seg][kn_]], writes=[bkq[r]], dma=True)
                P.add("sp", dma(vq[r], GTH[seg][vn_][r * T:(r + 1) * T, lc_:lc_ + 128].rearrange("(t p) c -> p t c", p=128)),
                      reads=[B_gth[seg][vn_]], writes=[bvq[r]], dma=True)

        def attn(q_ap, kpart, scale, ob, sb_):
            for kt in range(32):
                r, kl = kt // 8, kt % 8
                sbk = st_ring.next()
                P.add("pe", mm(ps[sbk][:], [(kq[r][kpart, kl * 128:(kl + 1) * 128], q_ap)]), reads=[bkq[r], B_q[tg]], writes=[B_ps[sbk]])
                pT, bpt = ptp.next()
                P.add("act", act(pT[:], ps[sbk][:], AF.Exp, scale=scale), reads=[B_ps[sbk]], writes=[bpt])
                w_, p_ = ([B_ps[ob]], []) if kt == 0 else ([], [B_ps[ob]])
                P.add("pe", mm(ps[ob][:], [(vq[r][:, kl, :], pT[:])], first=(kt == 0), last=(kt == 31)), reads=[bvq[r], bpt], writes=w_, pw=p_)
                w_, p_ = ([B_ps[sb_]], []) if kt == 0 else ([], [B_ps[sb_]])
                P.add("pe", mm(ps[sb_][:], [(ones1[:], pT[:])], first=(kt == 0), last=(kt == 31)), reads=[bpt, B_const], writes=w_, pw=p_)

        lv = lamv[l]
        for h in range(4):
            load_kv(h * 128, h * 128)
            for m in range(2):
                kp = slice(m * 64, (m + 1) * 64)
                attn(qT[kp, h, tgs(tg)], kp, 0.125, 2 + m, 4 + m)
            r0, br0 = f32p.next()
            recip(r0[:], br0, ps[4][:], B_ps[4])
            P.add("dve", tt(r0[:], ps[2][:], r0[:], ALU.mult), reads=[B_ps[2], br0], writes=[br0])
            r1, br1 = f32p.next()
            recip(r1[:], br1, ps[5][:], B_ps[5])
            P.add("dve", tt(r1[:], ps[3][:], r1[:], ALU.mult), reads=[B_ps[3], br1], writes=[br1])
            P.add("dve", stt(r1[:], r1[:], lv[:, 4:5], r0[:], ALU.mult, ALU.add), reads=[br0, br1, B_lamv[l]], writes=[br1])
            sq, bsq = sqp.next()
            P.add("act", act(sq[:], r1[:], AF.Square), reads=[br1], writes=[bsq])
            P.add("pe", mm(ps[6][:], [(ones128[:], sq[:])]), reads=[bsq, B_const], writes=[B_ps[6]])
            rs_, brs = f32p.next()
            rsqrt(rs_[:], brs, ps[6][:], B_ps[6])
            P.add("dve", stt(cat[:, h, :], r1[:], lv[:, 5:6], rs_[:], ALU.mult, ALU.mult), reads=[br1, brs, B_lamv[l]], pw=[B_ain])
        sC = 128 ** -0.5
        for g in range(2):
            load_kv(512 + g * 128, 512 + g * 128)
            for rr in range(2):
                hq = 2 * g + rr
                attn(qT[:, 4 + hq, tgs(tg)], slice(0, 128), sC, 2, 4)
                rc, brc = f32p.next()
                recip(rc[:], brc, ps[4][:], B_ps[4])
                P.add("dve", tt(cat[:, 8 + hq, :], ps[2][:], rc[:], ALU.mult), reads=[B_ps[2], brc], pw=[B_ain])
        for h in range(4):
            wK, wV = winK[seg], winV[seg]
            for part, (c_lo, c_hi) in enumerate(((0, 384), (384, 1408), (1408, 1792))):
                n_ = c_hi - c_lo
                P.add("sp", dma(kq[part][:, 0:n_], wK[h * 128:(h + 1) * 128, c_lo:c_hi]), reads=[B_winK[seg]], writes=[bkq[part]], dma=True)
                P.add("sp", dma(vq[part][:, 0:n_ // 128, :], wV[c_lo:c_hi, h * 128:(h + 1) * 128].rearrange("(t p) c -> p t c", p=128)),
                      reads=[B_winV[seg]], writes=[bvq[part]], dma=True)
            def kwin(wt):
                if wt < 3:
                    return kq[0][:, wt * 128:(wt + 1) * 128]
                if wt < 11:
                    return kq[1][:, (wt - 3) * 128:(wt - 2) * 128]
                return kq[2][:, (wt - 11) * 128:(wt - 10) * 128]

            def vwin(wt):
                if wt < 3:
                    return vq[0][:, wt, :]
                if wt < 11:
                    return vq[1][:, wt - 3, :]
                return vq[2][:, wt - 11, :]
            for i in range(4):
                lt = tg * 4 + i
                for j in range(7):
                    wt = lt + j
                    sbk = st_ring.next()
                    P.add("pe", mm(ps[sbk][:, 0:128], [(kwin(wt), qT[:, 8 + h, tg * TG + i * 128:tg * TG + (i + 1) * 128])]),
                          reads=[bkq[0], bkq[1], bkq[2], B_q[tg]], writes=[B_ps[sbk]])
                    t_, bt_ = f32p.next()
                    P.add("dve", stt(t_[:, 0:128], ps[sbk][:, 0:128], sC, biasT[:, h * 7 + j, :], ALU.mult, ALU.add),
                          reads=[B_ps[sbk], B_bias], writes=[bt_])
                    e_, be_ = sqp.next()
                    P.add("act", act(e_[:, 0:128], t_[:, 0:128], AF.Exp), reads=[bt_], writes=[be_])
                    pT, bpt = ptp.next()
                    P.add("dve", tt(pT[:, 0:128], e_[:, 0:128], dm[:, i, j, :], ALU.mult), reads=[be_, B_dm], writes=[bpt])
                    w_, p_ = ([B_ps[3]], []) if j == 0 else ([], [B_ps[3]])
                    P.add("pe", mm(ps[3][:, 0:128], [(vwin(wt), pT[:, 0:128])], first=(j == 0), last=(j == 6)), reads=[bvq[0], bvq[1], bvq[2], bpt], writes=w_, pw=p_)
                    w_, p_ = ([B_ps[5]], []) if j == 0 else ([], [B_ps[5]])
                    P.add("pe", mm(ps[5][:, 0:128], [(ones1[:], pT[:, 0:128])], first=(j == 0), last=(j == 6)), reads=[bpt, B_const], writes=w_, pw=p_)
                rc, brc = f32p.next()
                recip(rc[:, 0:128], brc, ps[5][:, 0:128], B_ps[5])
                P.add("dve", tt(cat[:, 12 + h, i * 128:(i + 1) * 128], ps[3][:, 0:128], rc[:, 0:128], ALU.mult), reads=[B_ps[3], brc], pw=[B_ain])

    def phase2b(l, tg):
        P.new_phase()
        mixed = a32(0, KC * TG).rearrange("p (c t) -> p c t", c=KC)
        B_mixed = P.abuf("mixed")
        cat = ain
        for g in range(8):
            view, bw = load_w([(W[l]["w_out"], g * 256, 256)], KC)
            for j in range(2):
                oc = g * 2 + j
                pb = gen_ps.next()
                P.add("pe", mm(ps[pb][:], [(view[:, kc, j * 128:(j + 1) * 128], cat[:, kc, :]) for kc in range(KC)]),
                      reads=[bw, B_ain], writes=[B_ps[pb]])
                P.add("dve", (lambda e, oc=oc, pb=pb: e.tensor_copy(out=mixed[:, oc, :], in_=ps[pb][:])), reads=[B_ps[pb]], pw=[B_mixed])
                sq, bsq = sqp.next()
                P.add("act", act(sq[:], ps[pb][:], AF.Square), reads=[B_ps[pb]], writes=[bsq])
                P.add("pe", mm(ps[6][:], [(onesD[:], sq[:])], first=(oc == 0), last=(oc == KC - 1)), reads=[bsq, B_const],
                      writes=[B_ps[6]] if oc == 0 else [], pw=[] if oc == 0 else [B_ps[6]])
        rstd, brstd = f32p.next()
        rsqrt(rstd[:], brstd, ps[6][:], B_ps[6])
        for oc in range(KC):
            P.add("dve", stt(mixed[:, oc, :], mixed[:, oc, :], vec[l][:, 16 + oc:17 + oc], rstd[:], ALU.mult, ALU.mult),
                  reads=[B_mixed, brstd, B_vec[l]], pw=[B_mixed])
            P.add("dve", tt(xT[:, oc, tgs(tg)], xT[:, oc, tgs(tg)], mixed[:, oc, :], ALU.add), reads=[B_mixed, B_x[tg]], pw=[B_x[tg]])

    def phase2c(l, tg):
        P.new_phase()
        fT = a32(0, KC * TG).rearrange("p (c t) -> p c t", c=KC)
        actT = a16(8192, 22 * TG // 2).rearrange("p (c t) -> p c t", c=22)
        B_fT = P.abuf("fT")
        B_act = P.abuf("act")
        rmsnorm_to_ain(l, tg, 32)
        for hf in range(2):
            for fl in range(22):
                fc = hf * 22 + fl
                view, bw = load_w([(W[l]["gate"], fc * 128, 128), (W[l]["up"], fc * 128, 128)], KC)
                pg, pu = gen_ps.next(), gen_ps.next()
                P.add("pe", mm(ps[pg][:], [(view[:, kc, 0:128], ain[:, kc, :]) for kc in range(KC)]), reads=[bw, B_ain], writes=[B_ps[pg]])
                P.add("pe", mm(ps[pu][:], [(view[:, kc, 128:256], ain[:, kc, :]) for kc in range(KC)]), reads=[bw, B_ain], writes=[B_ps[pu]])
                sg, bsg = f32p.next()
                P.add("act", act(sg[:], ps[pg][:], AF.Silu), reads=[B_ps[pg]], writes=[bsg])
                P.add("dve", tt(actT[:, fl, :], ps[pu][:], sg[:], ALU.mult), reads=[B_ps[pu], bsg], pw=[B_act])
            for oc in range(KC):
                view, bw = load_w([(W[l]["down"], oc * 128, 128)], 22, krow0=hf * 22 * 128)
                pb = gen_ps.next()
                P.add("pe", mm(ps[pb][:], [(view[:, fl, :], actT[:, fl, :]) for fl in range(22)]), reads=[bw, B_act], writes=[B_ps[pb]])
                if hf == 0:
                    P.add("act", act(fT[:, oc, :], ps[pb][:], AF.Copy), reads=[B_ps[pb]], pw=[B_fT])
                else:
                    P.add("dve", tt(fT[:, oc, :], fT[:, oc, :], ps[pb][:], ALU.add), reads=[B_ps[pb], B_fT], pw=[B_fT])
                    sq, bsq = sqp.next()
                    P.add("act", act(sq[:], fT[:, oc, :], AF.Square), reads=[B_fT], writes=[bsq])
                    P.add("pe", mm(ps[6][:], [(onesD[:], sq[:])], first=(oc == 0), last=(oc == KC - 1)), reads=[bsq, B_const],
                          writes=[B_ps[6]] if oc == 0 else [], pw=[] if oc == 0 else [B_ps[6]])
        rstd, brstd = f32p.next()
        rsqrt(rstd[:], brstd, ps[6][:], B_ps[6])
        for oc in range(KC):
            P.add("dve", stt(fT[:, oc, :], fT[:, oc, :], vec[l][:, 48 + oc:49 + oc], rstd[:], ALU.mult, ALU.mult),
                  reads=[B_fT, brstd, B_vec[l]], pw=[B_fT])
            P.add("dve", tt(xT[:, oc, tgs(tg)], xT[:, oc, tgs(tg)], fT[:, oc, :], ALU.add), reads=[B_fT, B_x[tg]], pw=[B_x[tg]])

    def dyn_init(e):
        pid = e.partition_id()
        rk = e.snap(pid % 4)
        DYN["rk"] = rk
        DYN["lb"] = e.snap((rk + 3) % 4)
        DYN["rb"] = e.snap((rk + 1) % 4)
        return None
    P.ops["pool"].append(_mk_raw(dyn_init))
    setup_consts()
    if first_seg:
        for tg in range(NTG):
            P.add("sp", dma(xT[:, :, tgs(tg)], xT_d.rearrange("(c p) t -> p c t", p=128)[:, :, tgs(tg)]), writes=[B_x[tg]], dma=True)
    else:
        for tg in range(NTG):
            P.add("sp", dma(xT[:, :, tgs(tg)], stx_in.rearrange("p (c t) -> p c t", c=KC)[:, :, tgs(tg)]), writes=[B_x[tg]], dma=True)
            P.add("sp", dma(qT[:, :, tgs(tg)], stq_in.rearrange("p (c t) -> p c t", c=12)[:, :, tgs(tg)]), writes=[B_q[tg]], dma=True)
    for seg in range(seg_lo, seg_hi + 1):
        if seg >= 1:
            lprev = seg - 1
            make_windows(seg - 1)
            for tg in range(NTG):
                phase2a(lprev, seg - 1, tg)
                if debug and seg == 1:
                    P.add("sp", dma(dbg_cat[tg], ain[:].rearrange("p c t -> p (c t)")), reads=[B_ain], pw=[B_dbg], dma=True, sigbuf=B_ain)
                phase2b(lprev, tg)
                if debug and seg == 1:
                    P.add("sp", dma(dbg_xmid.rearrange("p (c t) -> p c t", c=KC)[:, :, tgs(tg)], xT[:, :, tgs(tg)]), reads=[B_x[tg]], pw=[B_dbg], dma=True, sigbuf=B_x[tg])
                phase2c(lprev, tg)
        if seg <= 1:
            for tg in range(NTG):
                phase1(seg, seg, tg)
            exchange(seg)
    finals = []
    if last_seg:
        for tg in range(NTG):
            P.add("sp", dma(out_d.rearrange("(c p) t -> p c t", p=128)[:, :, tgs(tg)], xT[:, :, tgs(tg)]), reads=[B_x[tg]], pw=[B_out], dma=True, sigbuf=B_x[tg])
        finals = [B_out]
    elif not fused:
        for tg in range(NTG):
            P.add("sp", dma(stx_out.rearrange("p (c t) -> p c t", c=KC)[:, :, tgs(tg)], xT[:, :, tgs(tg)]), reads=[B_x[tg]], pw=[B_st], dma=True, sigbuf=B_x[tg])
            P.add("sp", dma(stq_out.rearrange("p (c t) -> p c t", c=12)[:, :, tgs(tg)], qT[:, :, tgs(tg)]), reads=[B_q[tg]], pw=[B_st], dma=True, sigbuf=B_q[tg])
        finals = [B_st, B_dbg] + list(B_snd[seg_lo].values())
    P.add("sp", None, reads=finals)
    P.finalize(nc, stack)
    stack.close()
    nc._in_names = in_names
    return nc


def _host_consts():
    theta = 10000.0
    inv = np.power(theta, -np.arange(0, 64, 2, dtype=np.float32) / 64).astype(np.float32)
    p = np.arange(128)
    j = p % 32
    sign = np.where((p % 64) < 32, -1.0, 1.0).astype(np.float32)
    tabs, dmasks, halos = [], [], []
    for rank in range(4):
        s = (rank * T + np.arange(T))
        posA = s.astype(np.float32)
        angA = posA[None, :] * inv[j][:, None]
        row = (s // 64).astype(np.float32)
        col = (s % 64).astype(np.float32)
        posC = np.where((p < 64)[:, None], row[None, :], col[None, :]).astype(np.float32)
        angC = posC * inv[j][:, None]
        tab = np.stack([np.cos(angA), np.sin(angA) * sign[:, None], np.cos(angC), np.sin(angC) * sign[:, None]], axis=1)
        tabs.append(np.ascontiguousarray(tab.reshape(128, 4 * T).astype(np.float32)))
        dmk = np.zeros((128, 8, 7, 128), np.float32)
        kk = np.arange(128)
        kr_par, kc = kk // 64, kk % 64
        qq = np.arange(128)
        qr_l, qc = qq // 64, qq % 64
        for lt in range(8):
            b = rank * 8 + lt
            qr = 2 * b + qr_l
            win_r = np.clip(qr - 4, 0, 56)
            win_c = np.clip(qc - 8, 0, 48)
            for jj in range(7):
                kr = 2 * b + 2 * (jj - 3) + kr_par
                ok = ((kr[:, None] >= 0) & (kr[:, None] < 64) & (kr[:, None] >= win_r[None, :]) & (kr[:, None] < win_r[None, :] + 8)
                      & (kc[:, None] >= win_c[None, :]) & (kc[:, None] < win_c[None, :] + 16))
                dmk[:, lt, jj, :] = ok
        dmasks.append(np.ascontiguousarray(dmk.reshape(128, 8 * 7 * 128)))
        hl = np.zeros((128, 2), np.float32)
        hl[:, 0] = 0.0 if rank == 0 else 1.0
        hl[:, 1] = 0.0 if rank == 3 else 1.0
        halos.append(hl)
    m = np.arange(128)
    perm = np.zeros((128, 128), np.float32)
    perm[m ^ 32, m] = 1.0
    return tabs, dmasks, halos, perm


def _layer_inputs(inp, l):
    def pc(v, n):
        return np.ascontiguousarray(np.asarray(v, np.float32).reshape(n, 128).T)
    vec = np.zeros((128, NV), np.float32)
    vec[:, 0:16] = pc(inp["norm_mix_pre"][l], 16)
    vec[:, 16:32] = pc(inp["norm_mix_post"][l], 16)
    vec[:, 32:48] = pc(inp["norm_ffn_pre"][l], 16)
    vec[:, 48:64] = pc(inp["norm_ffn_post"][l], 16)
    dw = np.asarray(inp["conv_dw"][l], np.float32)
    vec[:, 64:188] = dw.reshape(31, 4, 128).transpose(2, 1, 0).reshape(128, 124)
    vec[:, 188:192] = pc(inp["conv_dw_b"][l], 4)
    vec[:, 192:196] = pc(inp["conv_ln_g"][l], 4)
    vec[:, 196:200] = pc(inp["conv_ln_b"][l], 4)
    vec[:, 200:204] = pc(inp["conv_pw_b"][l], 4)
    vec[:, 204] = np.asarray(inp["diff_subln"][l], np.float32)
    vec[:, 205] = np.asarray(inp["gqa_q_norm"][l], np.float32)
    vec[:, 206] = np.asarray(inp["gqa_k_norm"][l], np.float32)
    lamb = np.ascontiguousarray(np.broadcast_to(np.asarray(inp["diff_lambda"][l], np.float32).reshape(1, 256), (128, 256)))
    rpb = np.asarray(inp["na_rpb"][l], np.float32)
    kk = np.arange(128)
    kr_par, kc = kk // 64, kk % 64
    qq = np.arange(128)
    qr_l, qc = qq // 64, qq % 64
    bias = np.zeros((128, 4, 7, 128), np.float32)
    for jj in range(7):
        dr = 2 * (jj - 3) + kr_par[:, None] - qr_l[None, :]
        ir = np.clip(dr + 7, 0, 14)
        ic = np.clip(kc[:, None] - qc[None, :] + 15, 0, 30)
        for h in range(4):
            bias[:, h, jj, :] = rpb[h][ir, ic]
    d = {
        "w_in%d" % l: np.ascontiguousarray(inp["w_in"][l], np.float32), "w_out%d" % l: np.ascontiguousarray(inp["w_out"][l], np.float32),
        "gate%d" % l: np.ascontiguousarray(inp["ffn_gate"][l], np.float32), "up%d" % l: np.ascontiguousarray(inp["ffn_up"][l], np.float32),
        "down%d" % l: np.ascontiguousarray(inp["ffn_down"][l], np.float32), "pw%d" % l: np.ascontiguousarray(inp["conv_pw"][l], np.float32),
        "vec%d" % l: vec, "lamb%d" % l: lamb, "bias%d" % l: np.ascontiguousarray(bias.reshape(128, 28 * 128)),
    }
    return d


_NC_CACHE = {}


def _get_nc(lo, hi, fused):
    key = (lo, hi, fused)
    if key not in _NC_CACHE:
        _NC_CACHE[key] = build_program(lo, hi, fused)
    return _NC_CACHE[key]


def kernel(**inp):
    inp = {k: np.asarray(v) for k, v in inp.items()}
    x = inp["x"].astype(np.float32, copy=False)
    tabs, dmasks, halos, perm = _host_consts()
    lay = {l: _layer_inputs(inp, l) for l in range(L)}
    cores = list(range(8))

    def common(c):
        r = c % 4
        return {"tabs": tabs[r], "dmask": dmasks[r], "halo": halos[r], "perm": perm}

    def xT_of(c):
        b, r = c // 4, c % 4
        return np.ascontiguousarray(x[b, r * T:(r + 1) * T, :].T)

    if FUSED:
        nc = _get_nc(0, 2, True)
        maps = []
        for c in cores:
            m = common(c)
            m.update(lay[0])
            m.update(lay[1])
            m["xT"] = xT_of(c)
            maps.append(m)
        maps = [{k: m[k] for k in nc._in_names} for m in maps]
        res = run_bass_kernel_spmd(nc, maps, core_ids=cores)
        outs = [np.asarray(res.results[c]["outT"]) for c in cores]
    else:
        state = None
        outs = None
        for seg in range(3):
            nc = _get_nc(seg, seg, False)
            maps = []
            for c in cores:
                m = common(c)
                for l in sorted({0 if seg <= 1 else 1, 1 if seg >= 1 else 0}):
                    m.update(lay[l])
                if seg == 0:
                    m["xT"] = xT_of(c)
                else:
                    g0 = (c // 4) * 4
                    m["stx_in"] = state[c]["stx_out"]
                    m["stq_in"] = state[c]["stq_out"]
                    for n_ in ("kA", "kC", "kD", "uu", "vA", "vC", "vD"):
                        m["gth_" + n_] = np.concatenate([state[g0 + r]["snd_" + n_] for r in range(4)], axis=0)
                maps.append(m)
            maps = [{k: m[k] for k in nc._in_names} for m in maps]
            res = run_bass_kernel_spmd(nc, maps, core_ids=cores)
            if seg < 2:
                state = [{k: np.asarray(v) for k, v in res.results[c].items()} for c in cores]
            else:
                outs = [np.asarray(res.results[c]["outT"]) for c in cores]
    out = np.zeros((2, 4096, D), np.float32)
    for c in cores:
        b, r = c // 4, c % 4
        out[b, r * T:(r + 1) * T, :] = outs[c].T
    return out
```

```python
import contextlib
import numpy as np
import ml_dtypes
import concourse.bass as bass
import concourse.mybir as mybir
from concourse.bass_utils import run_bass_kernel_spmd

F32 = mybir.dt.float32
BF16 = mybir.dt.bfloat16
AF = mybir.ActivationFunctionType
ALU = mybir.AluOpType

L = 2
D = 2048
KC = 16
T = 1024
TG = 512
NTG = 2
FF = 5632
FC = 44
INC = 5120
EPS = 1e-6
NV = 208
KROWS = 1792
VCOLS = 1280
ENGS = ("pe", "act", "dve", "pool", "sp")
FUSED = True
KCUT = 0


class Buf:
    __slots__ = ("name", "ws", "rs", "sem", "cnt", "arena", "psum")

    def __init__(self, name, arena=False, psum=False):
        self.psum = psum
        self.name = name
        self.ws = {}
        self.rs = {}
        self.sem = None
        self.cnt = 0
        self.arena = arena


class Op:
    __slots__ = ("eng", "fn", "deps", "dma", "cc", "sig", "val", "dbuf", "ninc")


class Prog:
    def __init__(self):
        self.ops = {e: [] for e in ENGS}
        self.last_compute = {}
        self.arena_dmas = []
        self.fence_ops = []
        self.dma_bufs = []
        self.cc_cnt = 0

    def add(self, eng, fn, reads=(), writes=(), pw=(), dma=False, cc=False, ninc=1, sigbuf=None):
        op = Op()
        op.eng = eng
        op.fn = fn
        op.dma = dma
        op.cc = cc
        op.sig = False
        op.val = None
        op.dbuf = None
        op.ninc = ninc
        deps = {}
        for b in reads:
            for w in b.ws.values():
                deps[w] = True
            if b.psum:
                for r in b.rs.values():
                    if r.eng != eng and r not in deps:
                        deps[r] = False
        for b in list(writes) + list(pw):
            for r in b.rs.values():
                if r not in deps:
                    deps[r] = False
        for b in writes:
            for w in b.ws.values():
                if w not in deps:
                    deps[w] = False
        op.deps = deps
        if dma:
            db = sigbuf if sigbuf is not None else (list(writes) + list(pw))[0]
            if db.sem is None:
                self.dma_bufs.append(db)
                db.sem = True
            db.cnt += 16 * ninc
            op.val = db.cnt
            op.dbuf = db
            if any(b.arena for b in list(reads) + list(writes) + list(pw)):
                self.arena_dmas.append(op)
        elif cc:
            self.cc_cnt += 1
            op.val = self.cc_cnt
        else:
            self.last_compute[eng] = op
        wkey = ("dma", id(op.dbuf)) if dma else ("cc" if cc else eng)
        for b in reads:
            if dma or cc:
                b.rs[("dma", id(op.dbuf) if dma else "cc")] = op
            else:
                b.rs[eng] = op
        for b in writes:
            b.ws = {wkey: op}
            b.rs = {}
        for b in pw:
            b.ws[wkey] = op
        self.ops[eng].append(op)
        return op

    def new_phase(self):
        self.fence_ops = list(self.last_compute.values()) + list(self.arena_dmas)
        self.arena_dmas = []

    def abuf(self, name):
        if not hasattr(self, "_ab"):
            self._ab = {}
        if name not in self._ab:
            self._ab[name] = Buf(name, arena=True)
        return self.fence(self._ab[name])

    def fence(self, buf):
        buf.ws = {}
        buf.rs = {("f", i): o for i, o in enumerate(self.fence_ops)}
        return buf

    def finalize(self, nc, stack):
        engsem = {e: stack.enter_context(nc.semaphore("s_" + e)) for e in ENGS}
        ccsem = stack.enter_context(nc.semaphore("s_cc"))
        for i, b in enumerate(self.dma_bufs):
            b.sem = stack.enter_context(nc.semaphore("d%d" % i))
        for e in ENGS:
            for op in self.ops[e]:
                for d, raw in op.deps.items():
                    if d.dma or d.cc:
                        continue
                    if d.eng == op.eng and not op.dma and not op.cc:
                        if e == "pe":
                            continue
                    d.sig = True
        for e in ENGS:
            c = 0
            for op in self.ops[e]:
                if not op.dma and not op.cc and op.sig:
                    c += 1
                    op.val = c
        block = stack.enter_context(nc.Block())

        def emit(e, eng):
            waited = {}
            for op in self.ops[e]:
                for d, raw in op.deps.items():
                    if d.dma:
                        sem, key, v = d.dbuf.sem, ("d", id(d.dbuf)), d.val
                    elif d.cc:
                        sem, key, v = ccsem, "cc", d.val
                    else:
                        if d.eng == e and not op.dma and not op.cc and e == "pe":
                            continue
                        sem, key, v = engsem[d.eng], d.eng, d.val
                    if waited.get(key, 0) >= v:
                        continue
                    waited[key] = v
                    eng.wait_ge(sem, v)
                if op.fn is None:
                    continue
                res = op.fn(eng)
                if op.dma:
                    if not isinstance(res, (list, tuple)):
                        res = [res]
                    assert len(res) == op.ninc
                    for r in res:
                        r.then_inc(op.dbuf.sem, 16)
                elif op.cc:
                    res.then_inc(ccsem, 1)
                elif op.sig:
                    res.then_inc(engsem[e], 1)

        @block.tensor
        def _(eng):
            emit("pe", eng)

        @block.scalar
        def _(eng):
            emit("act", eng)

        @block.vector
        def _(eng):
            emit("dve", eng)

        @block.gpsimd
        def _(eng):
            emit("pool", eng)

        @block.sync
        def _(eng):
            emit("sp", eng)


def _mk_raw(fn):
    op = Op()
    op.eng = "pool"
    op.fn = fn
    op.deps = {}
    op.dma = False
    op.cc = False
    op.sig = False
    op.val = None
    op.dbuf = None
    op.ninc = 0
    return op


class Ring:
    def __init__(self, items):
        self.items = items
        self.i = 0

    def next(self):
        it = self.items[self.i % len(self.items)]
        self.i += 1
        return it


def build_program(seg_lo, seg_hi, fused, debug=False):
    nc = bass.Bass("TRN2", target_bir_lowering=False)
    P = Prog()
    DYN = {}
    stack = contextlib.ExitStack()
    first_seg, last_seg = seg_lo == 0, seg_hi == 2
    layers = sorted({0 if s <= 1 else 1 for s in range(seg_lo, seg_hi + 1)} | {1 if s >= 1 else 0 for s in range(seg_lo, seg_hi + 1)})

    in_names = []

    def din(name, shape, dt=F32):
        in_names.append(name)
        return nc.dram_tensor(name, list(shape), dt, kind="ExternalInput").ap()

    def dout(name, shape, dt=F32):
        return nc.dram_tensor(name, list(shape), dt, kind="ExternalOutput").ap()

    class _LazyW(dict):
        def __init__(self, l):
            super().__init__()
            self.l = l

        def __missing__(self, key):
            shp = {"w_in": [D, INC], "w_out": [D, D], "gate": [D, FF], "up": [D, FF], "down": [FF, D], "pw": [512, 512],
                   "vec": [128, NV], "lamb": [128, 256], "bias": [128, 28 * 128]}[key]
            v = din("%s%d" % (key, self.l), shp)
            self[key] = v
            return v
    W = {l: _LazyW(l) for l in layers}
    tabs_d = din("tabs", [128, 4 * T])
    dmask_d = din("dmask", [128, 8 * 7 * 128])
    perm_d = din("perm", [128, 128])
    halo_d = din("halo", [128, 2])
    if first_seg:
        xT_d = din("xT", [D, T])
    else:
        stx_in = din("stx_in", [128, KC * T])
        stq_in = din("stq_in", [128, 12 * T], BF16)
    if last_seg:
        out_d = dout("outT", [D, T])
    KSEG = [("kA", 0, 512), ("kC", 512, 256), ("kD", 768, 512), ("uu", 1280, 512)]
    VSEG = [("vA", 0, 512), ("vC", 512, 256), ("vD", 768, 512)]
    NAMES = [k[0] for k in KSEG] + [v[0] for v in VSEG]

    def kloc(row):
        for name, r0, n in KSEG:
            if r0 <= row < r0 + n:
                return name, row - r0, n
        raise ValueError(row)

    def vloc(col):
        for name, c0, n in VSEG:
            if c0 <= col < c0 + n:
                return name, col - c0, n
        raise ValueError(col)

    def snd_shape(name):
        for nm, _, n in KSEG:
            if nm == name:
                return [n, T]
        for nm, _, n in VSEG:
            if nm == name:
                return [T, n]

    def gth_shape(name):
        sh = snd_shape(name)
        return [4 * sh[0], sh[1]]
    SND, GTH, SNDH, GTHH = {}, {}, {}, {}
    if not fused:
        if not last_seg:
            stx_out = dout("stx_out", [128, KC * T])
            stq_out = dout("stq_out", [128, 12 * T], BF16)
            SND[seg_lo] = {n_: dout("snd_" + n_, snd_shape(n_), BF16) for n_ in NAMES}
        if not first_seg:
            GTH[seg_lo - 1] = {n_: din("gth_" + n_, gth_shape(n_), BF16) for n_ in NAMES}
    else:
        for s_ in (0, 1):
            SNDH[s_] = {n_: nc.dram_tensor("snd%d_%s" % (s_, n_), snd_shape(n_), BF16) for n_ in NAMES}
            GTHH[s_] = {n_: nc.dram_tensor("gth%d_%s" % (s_, n_), gth_shape(n_), BF16) for n_ in NAMES}
            SND[s_] = {n_: SNDH[s_][n_].ap() for n_ in NAMES}
            GTH[s_] = {n_: GTHH[s_][n_].ap() for n_ in NAMES}
    B_snd = {s_: {n_: Buf("snd%d%s" % (s_, n_)) for n_ in NAMES} for s_ in (0, 1)}
    B_gth = {s_: {n_: Buf("gth%d%s" % (s_, n_)) for n_ in NAMES} for s_ in (0, 1)}
    B_out = Buf("out")
    B_st = Buf("stout")
    if debug:
        dbg_cat = dout("dbg_cat", [NTG, 128, KC * TG], BF16)
        dbg_xmid = dout("dbg_xmid", [128, KC * T])
        dbg_y = dout("dbg_y", [128, 4 * TG])
        dbg_zs = dout("dbg_zs", [128, 4 * TG], BF16)
    B_dbg = Buf("dbg")
    winK = {s_: nc.dram_tensor("winK%d" % s_, [1024, 1792], BF16) for s_ in (0, 1)}
    winV = {s_: nc.dram_tensor("winV%d" % s_, [1792, 512], BF16) for s_ in (0, 1)}
    B_winK = {s_: Buf("winK%d" % s_) for s_ in (0, 1)}
    B_winV = {s_: Buf("winV%d" % s_) for s_ in (0, 1)}

    def sb(name, shape, dt):
        return stack.enter_context(nc.sbuf_tensor("t_" + name, list(shape), dt))

    xT = sb("xT", [128, KC, T], F32)
    qT = sb("qT", [128, 12, T], BF16)
    ain = sb("ain", [128, KC, TG], BF16)
    B_x = [Buf("x%d" % i) for i in range(NTG)]
    B_q = [Buf("q%d" % i) for i in range(NTG)]
    B_ain = Buf("ain")
    wbufs = Ring([(sb("wb%d" % i, [128, 6144], BF16), Buf("wb%d" % i)) for i in range(2)])
    ptp = Ring([(sb("pt%d" % i, [128, TG], BF16), Buf("pt%d" % i)) for i in range(3)])
    sqp = Ring([(sb("sq%d" % i, [128, TG], BF16), Buf("sq%d" % i)) for i in range(3)])
    f32p = Ring([(sb("f%d" % i, [128, TG], F32), Buf("f%d" % i)) for i in range(4)])
    stgp = Ring([(sb("stg%d" % i, [128, TG], BF16), Buf("stg%d" % i)) for i in range(2)])
    vec = {l: sb("vec%d" % l, [128, NV], F32) for l in layers}
    lamb = {l: sb("lamb%d" % l, [128, 256], F32) for l in layers}
    lamv = {l: sb("lamv%d" % l, [128, 8], F32) for l in layers}
    B_vec = {l: Buf("vec%d" % l) for l in layers}
    B_lamv = {l: Buf("lamv%d" % l) for l in layers}
    perm = sb("perm", [128, 128], BF16)
    onesD = sb("onesD", [128, 128], BF16)
    ones128 = sb("ones128", [128, 128], BF16)
    ones512 = sb("ones512", [128, 128], BF16)
    ones1 = sb("ones1", [128, 128], BF16)
    halo = sb("halo", [128, 2], F32)
    epsc = sb("epsc", [128, 1], F32)
    B_const = Buf("const")
    B_perm = Buf("perm")
    B_halo = Buf("halo")
    ARENA_F32 = 13824
    arena = sb("arena", [128, ARENA_F32], F32)

    def a32(off, n):
        return arena[:, off:off + n]

    def a16(off, n):
        return arena[:, off:off + n].bitcast(BF16)

    ps = [stack.enter_context(nc.psum_tensor("ps%d" % i, [128, 512], F32)) for i in range(8)]
    B_ps = [Buf("ps%d" % i, psum=True) for i in range(8)]

    def mm(out, pairs, first=True, last=True):
        def fn(e):
            n = len(pairs)
            ins = None
            for i, (l_, r_) in enumerate(pairs):
                ins = e.matmul(out, l_, r_, start=(first and i == 0), stop=(last and i == n - 1))
            return ins
        return fn

    def dma(out, in_):
        return lambda e: e.dma_start(out=out, in_=in_)

    def act(out, in_, func, scale=None, bias=None):
        kw = {}
        if scale is not None:
            kw["scale"] = scale
        if bias is not None:
            kw["bias"] = bias
        return lambda e: e.activation(out=out, in_=in_, func=func, **kw)

    def tt(out, a, b, op):
        return lambda e: e.tensor_tensor(out=out, in0=a, in1=b, op=op)

    def ts(out, a, s1, s2, op0, op1):
        return lambda e: e.tensor_scalar(out=out, in0=a, scalar1=s1, scalar2=s2, op0=op0, op1=op1)

    def stt(out, a, s, b, op0, op1):
        return lambda e: e.scalar_tensor_tensor(out=out, in0=a, scalar=s, in1=b, op0=op0, op1=op1)

    def tgs(tg):
        return slice(tg * TG, (tg + 1) * TG)

    def recip(out, bout, in_, bin_):
        P.add("act", act(out, in_, AF.Ln), reads=[bin_], writes=[bout])
        P.add("act", act(out, out, AF.Exp, scale=-1.0), reads=[bout], writes=[bout])

    def rsqrt(out, bout, in_, bin_):
        P.add("act", act(out, in_, AF.Ln, bias=epsc[:, 0:1]), reads=[bin_, B_const], writes=[bout])
        P.add("act", act(out, out, AF.Exp, scale=-0.5), reads=[bout], writes=[bout])

    def setup_consts():
        for t_, v in ((onesD, 1.0 / D), (ones128, 1.0 / 128), (ones512, 1.0 / 512), (ones1, 1.0)):
            P.add("dve", (lambda e, t_=t_, v=v: e.memset(t_[:], v)), pw=[B_const])
        P.add("dve", (lambda e: e.memset(epsc[:], EPS)), pw=[B_const])
        P.add("pool", dma(perm[:], perm_d), writes=[B_perm], dma=True)
        P.add("sp", dma(halo[:], halo_d), writes=[B_halo], dma=True)
        for l in layers:
            P.add("sp", dma(vec[l][:], W[l]["vec"]), writes=[B_vec[l]], dma=True)
            bl = Buf("lamb")
            P.add("sp", dma(lamb[l][:], W[l]["lamb"]), writes=[bl], dma=True)
            lv = lamv[l]
            lam_init = 0.8 - 0.6 * float(np.exp(-0.3 * l))
            tmpf, btmp = f32p.next()
            P.add("dve", tt(tmpf[:, 0:64], lamb[l][:, 0:64], lamb[l][:, 64:128], ALU.mult), reads=[bl], writes=[btmp])
            P.add("dve", (lambda e, lv=lv, tmpf=tmpf: e.reduce_sum(out=lv[:, 0:1], in_=tmpf[:, 0:64], axis=mybir.AxisListType.X)),
                  reads=[btmp], pw=[B_lamv[l]])
            tmpf2, btmp2 = f32p.next()
            P.add("dve", tt(tmpf2[:, 0:64], lamb[l][:, 128:192], lamb[l][:, 192:256], ALU.mult), reads=[bl], writes=[btmp2])
            P.add("dve", (lambda e, lv=lv, tmpf2=tmpf2: e.reduce_sum(out=lv[:, 1:2], in_=tmpf2[:, 0:64], axis=mybir.AxisListType.X)),
                  reads=[btmp2], pw=[B_lamv[l]])
            P.add("act", act(lv[:, 2:4], lv[:, 0:2], AF.Exp), reads=[B_lamv[l]], pw=[B_lamv[l]])
            P.add("dve", stt(lv[:, 4:5], lv[:, 3:4], -lam_init, lv[:, 2:3], ALU.add, ALU.subtract), reads=[B_lamv[l]], pw=[B_lamv[l]])
            P.add("dve", (lambda e, lv=lv, l=l, li=lam_init: e.tensor_scalar_mul(out=lv[:, 5:6], in0=vec[l][:, 204:205], scalar1=1.0 - li)),
                  reads=[B_vec[l]], pw=[B_lamv[l]])

    def rmsnorm_to_ain(l, tg, gcol):
        for c in range(KC):
            sq, bsq = sqp.next()
            P.add("act", act(sq[:], xT[:, c, tgs(tg)], AF.Square), reads=[B_x[tg]], writes=[bsq])
            P.add("pe", mm(ps[6][:], [(onesD[:], sq[:])], first=(c == 0), last=(c == KC - 1)), reads=[bsq, B_const],
                  writes=[B_ps[6]] if c == 0 else [], pw=[] if c == 0 else [B_ps[6]])
        rstd, brstd = f32p.next()
        rsqrt(rstd[:], brstd, ps[6][:], B_ps[6])
        for c in range(KC):
            P.add("dve", stt(ain[:, c, :], xT[:, c, tgs(tg)], vec[l][:, gcol + c:gcol + c + 1], rstd[:], ALU.mult, ALU.mult),
                  reads=[B_x[tg], brstd, B_vec[l]], pw=[B_ain])

    def load_w(segs, kchunks, krow0=0):
        wt, bw = wbufs.next()
        tot = sum(s[2] for s in segs)
        assert kchunks * tot <= 6144
        view = wt[:, 0:kchunks * tot].rearrange("p (k c) -> p k c", c=tot)
        off = 0
        for i, (Wd, c0, n) in enumerate(segs):
            src = Wd[krow0:krow0 + kchunks * 128, c0:c0 + n].rearrange("(k p) c -> p k c", p=128)
            P.add("pool", dma(view[:, :, off:off + n], src), writes=[bw] if i == 0 else [], pw=[] if i == 0 else [bw], dma=True)
            off += n
        return view, bw

    gen_ps = Ring([0, 1, 2, 3])

    def phase1(l, seg, tg):
        P.new_phase()
        tabs = a32(0, 4 * TG).rearrange("p (f t) -> p f t", f=4)
        B_tabs = P.abuf("tabs")
        P.add("sp", dma(tabs, tabs_d.rearrange("p (f t) -> p f t", f=4)[:, :, tgs(tg)]), writes=[B_tabs], dma=True)
        if KCUT == 1:
            return
        rmsnorm_to_ain(l, tg, 0)
        if KCUT == 2:
            return
        w_in = W[l]["w_in"]

        def rope(psb, kind, dest, dest_bufs_pw, gcol=None):
            if kind == "plain":
                P.add("act", act(dest, ps[psb][:], AF.Copy), reads=[B_ps[psb]], pw=dest_bufs_pw)
                return
            ci, si = (0, 1) if kind == "A" else (2, 3)
            if kind == "C":
                sq, bsq = sqp.next()
                P.add("act", act(sq[:], ps[psb][:], AF.Square), reads=[B_ps[psb]], writes=[bsq])
                P.add("pe", mm(ps[7][:], [(ones128[:], sq[:])]), reads=[bsq, B_const], writes=[B_ps[7]])
                rstd2, brstd2 = f32p.next()
                rsqrt(rstd2[:], brstd2, ps[7][:], B_ps[7])
            xb, bxb = sqp.next()
            if kind == "C":
                P.add("act", act(xb[:], ps[psb][:], AF.Copy, scale=vec[l][:, gcol:gcol + 1]), reads=[B_ps[psb], B_vec[l]], writes=[bxb])
            else:
                P.add("act", act(xb[:], ps[psb][:], AF.Copy), reads=[B_ps[psb]], writes=[bxb])
            P.add("pe", mm(ps[5][:], [(perm[:], xb[:])]), reads=[bxb, B_perm], writes=[B_ps[5]])
            t1, bt1 = f32p.next()
            if kind == "C":
                P.add("dve", stt(t1[:], ps[psb][:], vec[l][:, gcol:gcol + 1], tabs[:, ci, :], ALU.mult, ALU.mult),
                      reads=[B_ps[psb], B_vec[l], B_tabs], writes=[bt1])
            else:
                P.add("dve", tt(t1[:], ps[psb][:], tabs[:, ci, :], ALU.mult), reads=[B_ps[psb], B_tabs], writes=[bt1])
            t2, bt2 = f32p.next()
            P.add("dve", tt(t2[:], ps[5][:], tabs[:, si, :], ALU.mult), reads=[B_ps[5], B_tabs], writes=[bt2])
            if kind == "C":
                P.add("dve", tt(t1[:], t1[:], t2[:], ALU.add), reads=[bt1, bt2], writes=[bt1])
                P.add("dve", tt(dest, t1[:], rstd2[:], ALU.mult), reads=[bt1, brstd2], pw=dest_bufs_pw)
            else:
                P.add("dve", tt(dest, t1[:], t2[:], ALU.add), reads=[bt1, bt2], pw=dest_bufs_pw)

        def to_sendK(kind, psb, row0, gcol=None):
            stg, bstg = stgp.next()
            rope(psb, kind, stg[:], [bstg], gcol)
            nm_, lr_, _n = kloc(row0)
            P.add("sp", dma(SND[seg][nm_][lr_:lr_ + 128, tgs(tg)], stg[:]), reads=[bstg], pw=[B_snd[seg][nm_]], dma=True, sigbuf=bstg)

        groups = [
            ((0, 1), ("qA", 0)), ((2, 3), ("qA", 2)), ((4, 5), ("kA", 0)), ((6, 7), ("kA", 256)),
            ((12, 16), ("glu", 0)), ((13, 17), ("glu", 1)), ((14, 18), ("glu", 2)), ((15, 19), ("glu", 3)),
            ((20, 21), ("qC", 4)), ((22, 23), ("qC", 6)), ((24, 25), ("kC", 512)),
            ((28, 29), ("qD", 8)), ((30, 31), ("qD", 10)), ((32, 33), ("kD", 768)), ((34, 35), ("kD", 1024)),
        ]
        if KCUT == 3:
            groups = groups[:1]
        if KCUT == 4:
            groups = groups[:5]
        for (c0, c1), (kind, arg) in groups:
            if c1 == c0 + 1:
                view, bw = load_w([(w_in, c0 * 128, 256)], KC)
            else:
                view, bw = load_w([(w_in, c0 * 128, 128), (w_in, c1 * 128, 128)], KC)
            banks = []
            for j in range(2):
                pb = gen_ps.next()
                banks.append(pb)
                P.add("pe", mm(ps[pb][:], [(view[:, kc, j * 128:(j + 1) * 128], ain[:, kc, :]) for kc in range(KC)]),
                      reads=[bw, B_ain], writes=[B_ps[pb]])
            if kind == "glu":
                sg, bsg = f32p.next()
                P.add("act", act(sg[:], ps[banks[1]][:], AF.Sigmoid), reads=[B_ps[banks[1]]], writes=[bsg])
                stg, bstg = stgp.next()
                P.add("dve", tt(stg[:], ps[banks[0]][:], sg[:], ALU.mult), reads=[B_ps[banks[0]], bsg], writes=[bstg])
                P.add("sp", dma(SND[seg]["uu"][arg * 128:(arg + 1) * 128, tgs(tg)], stg[:]), reads=[bstg], pw=[B_snd[seg]["uu"]], dma=True, sigbuf=bstg)
                continue
            for j in range(2):
                pb = banks[j]
                if kind == "qA":
                    rope(pb, "A", qT[:, arg + j, tgs(tg)], [B_q[tg]])
                elif kind == "qC":
                    rope(pb, "C", qT[:, arg + j, tgs(tg)], [B_q[tg]], gcol=205)
                elif kind == "qD":
                    rope(pb, "plain", qT[:, arg + j, tgs(tg)], [B_q[tg]])
                elif kind == "kA":
                    to_sendK("A", pb, arg + j * 128)
                elif kind == "kC":
                    to_sendK("C", pb, arg + j * 128, gcol=206)
                elif kind == "kD":
                    to_sendK("plain", pb, arg + j * 128)
        if KCUT in (3, 4, 5):
            return
        for wc0, vc0 in ((1024, 0), (1280, 256), (3328, 512), (4608, 768), (4864, 1024)):
            view, bw = load_w([(w_in, wc0, 256)], KC)
            for t4 in range(4):
                pb = gen_ps.next()
                P.add("pe", mm(ps[pb][:, 0:256], [(ain[:, kc, t4 * 128:(t4 + 1) * 128], view[:, kc, :]) for kc in range(KC)]),
                      reads=[bw, B_ain], writes=[B_ps[pb]])
                stg, bstg = stgp.next()
                P.add("act", act(stg[:, 0:256], ps[pb][:, 0:256], AF.Copy), reads=[B_ps[pb]], writes=[bstg])
                r0 = tg * TG + t4 * 128
                nm_, lc_, _n = vloc(vc0)
                P.add("sp", dma(SND[seg][nm_][r0:r0 + 128, lc_:lc_ + 256], stg[:, 0:256]), reads=[bstg], pw=[B_snd[seg][nm_]], dma=True, sigbuf=bstg)

    def exchange(seg):
        if not fused:
            return
        for n_ in NAMES:
            P.add("pool", (lambda e, n_=n_: e.collective_compute("AllGather", ALU.bypass, replica_groups=[[0, 1, 2, 3], [4, 5, 6, 7]],
                                                                 ins=[SNDH[seg][n_].ap().opt()], outs=[GTHH[seg][n_].ap().opt()])),
                  reads=[B_snd[seg][n_]], writes=[B_gth[seg][n_]], cc=True)

    def make_windows(seg):
        wK, wV = winK[seg], winV[seg]
        gkD = GTH[seg]["kD"].rearrange("(r k) t -> r k t", r=4)
        guu = GTH[seg]["uu"].rearrange("(r k) t -> r k t", r=4)
        gvD = GTH[seg]["vD"].rearrange("(r k) t -> r k t", r=4)

        def kcopies(e):
            return [
                e.dma_start(out=wK[0:512, 0:384], in_=gkD[bass.ds(DYN["lb"], 1), :, 640:1024].rearrange("o k t -> (o k) t")),
                e.dma_start(out=wK[0:512, 1408:1792], in_=gkD[bass.ds(DYN["rb"], 1), :, 0:384].rearrange("o k t -> (o k) t")),
                e.dma_start(out=wK[512:1024, 0:384], in_=guu[bass.ds(DYN["lb"], 1), :, 640:1024].rearrange("o k t -> (o k) t")),
                e.dma_start(out=wK[512:1024, 1408:1792], in_=guu[bass.ds(DYN["rb"], 1), :, 0:384].rearrange("o k t -> (o k) t")),
            ]
        P.add("pool", kcopies, reads=[B_gth[seg]["kD"], B_gth[seg]["uu"]], writes=[B_winK[seg]], dma=True, ninc=4)
        if fused:
            P.add("sp", dma(wK[0:512, 384:1408], SND[seg]["kD"]), reads=[B_snd[seg]["kD"]], pw=[B_winK[seg]], dma=True)
            P.add("sp", dma(wK[512:1024, 384:1408], SND[seg]["uu"]), reads=[B_snd[seg]["uu"]], pw=[B_winK[seg]], dma=True)
        else:
            def kown(e):
                return [
                    e.dma_start(out=wK[0:512, 384:1408], in_=gkD[bass.ds(DYN["rk"], 1), :, :].rearrange("o k t -> (o k) t")),
                    e.dma_start(out=wK[512:1024, 384:1408], in_=guu[bass.ds(DYN["rk"], 1), :, :].rearrange("o k t -> (o k) t")),
                ]
            P.add("pool", kown, reads=[B_gth[seg]["kD"], B_gth[seg]["uu"]], pw=[B_winK[seg]], dma=True, ninc=2)

        def vcopies(e):
            return [
                e.dma_start(out=wV[0:384, :], in_=gvD[bass.ds(DYN["lb"], 1), 640:1024, :].rearrange("o k t -> (o k) t")),
                e.dma_start(out=wV[1408:1792, :], in_=gvD[bass.ds(DYN["rb"], 1), 0:384, :].rearrange("o k t -> (o k) t")),
            ]
        P.add("pool", vcopies, reads=[B_gth[seg]["vD"]], writes=[B_winV[seg]], dma=True, ninc=2)
        if fused:
            P.add("sp", dma(wV[384:1408, :], SND[seg]["vD"]), reads=[B_snd[seg]["vD"]], pw=[B_winV[seg]], dma=True)
        else:
            P.add("pool", (lambda e: e.dma_start(out=wV[384:1408, :], in_=gvD[bass.ds(DYN["rk"], 1), :, :].rearrange("o k t -> (o k) t"))),
                  reads=[B_gth[seg]["vD"]], pw=[B_winV[seg]], dma=True)

    def phase2a(l, seg, tg):
        cat = ain
        P.new_phase()
        UW = 542
        uw = a16(0, 4 * UW // 2).rearrange("p (c t) -> p c t", c=4)
        y = a32(1088, 4 * TG).rearrange("p (c t) -> p c t", c=4)
        acc1 = a32(3136, TG)
        zs = a16(3648, 4 * TG // 2).rearrange("p (c t) -> p c t", c=4)
        mean = a32(4672, TG)
        B_uw, B_y, B_acc1, B_zs, B_mean = (P.abuf(n) for n in ("uw", "y", "acc1", "zs", "mean"))
        U0 = 1280

        c0w = 369 if tg == 0 else 881
        P.add("sp", dma(uw, winK[seg][512:1024, c0w:c0w + 542].rearrange("(c p) t -> p c t", p=128)), reads=[B_winK[seg]], writes=[B_uw], dma=True)
        hs = (slice(0, 15), 0) if tg == 0 else (slice(527, 542), 1)
        P.add("dve", (lambda e: e.tensor_scalar_mul(out=uw[:, :, hs[0]], in0=uw[:, :, hs[0]], scalar1=halo[:, hs[1]:hs[1] + 1])),
              reads=[B_uw, B_halo], writes=[B_uw])
        V = vec[l]
        for c in range(4):
            P.add("dve", (lambda e, c=c: e.tensor_scalar_mul(out=y[:, c, :], in0=uw[:, c, 0:TG], scalar1=V[:, 64 + c * 31:65 + c * 31])),
                  reads=[B_uw, B_vec[l]], pw=[B_y])
            P.add("dve", (lambda e, c=c: e.tensor_scalar_mul(out=acc1, in0=uw[:, c, 1:1 + TG], scalar1=V[:, 65 + c * 31:66 + c * 31])),
                  reads=[B_uw, B_vec[l]], writes=[B_acc1])
            for j in range(2, 31):
                dst, bd = (y[:, c, :], B_y) if j % 2 == 0 else (acc1, B_acc1)
                P.add("dve", stt(dst, uw[:, c, j:j + TG], V[:, 64 + c * 31 + j:65 + c * 31 + j], dst, ALU.mult, ALU.add),
                      reads=[B_uw, B_vec[l], bd], pw=[bd])
            P.add("dve", stt(y[:, c, :], acc1, V[:, 188 + c:189 + c], y[:, c, :], ALU.add, ALU.add), reads=[B_acc1, B_y, B_vec[l]], pw=[B_y])
            yb, byb = sqp.next()
            P.add("act", act(yb[:], y[:, c, :], AF.Copy), reads=[B_y], writes=[byb])
            P.add("pe", mm(ps[4][:], [(ones512[:], yb[:])], first=(c == 0), last=(c == 3)), reads=[byb, B_const],
                  writes=[B_ps[4]] if c == 0 else [], pw=[] if c == 0 else [B_ps[4]])
            ysq, bysq = sqp.next()
            P.add("act", act(ysq[:], y[:, c, :], AF.Square), reads=[B_y], writes=[bysq])
            P.add("pe", mm(ps[5][:], [(ones512[:], ysq[:])], first=(c == 0), last=(c == 3)), reads=[bysq, B_const],
                  writes=[B_ps[5]] if c == 0 else [], pw=[] if c == 0 else [B_ps[5]])
        if debug and seg == 0 and tg == 0:
            P.add("sp", dma(dbg_y.rearrange("p (c t) -> p c t", c=4), y), reads=[B_y], pw=[B_dbg], dma=True, sigbuf=B_y)
        P.add("act", act(mean, ps[4][:], AF.Copy), reads=[B_ps[4]], writes=[B_mean])
        var, bvar = a32(5184, TG), P.abuf("cvar")
        P.add("dve", stt(var[:], mean, -1.0, mean, ALU.mult, ALU.mult), reads=[B_mean], writes=[bvar])
        P.add("dve", tt(var[:], var[:], ps[5][:], ALU.add), reads=[bvar, B_ps[5]], writes=[bvar])
        rsqrt(var[:], bvar, var[:], bvar)
        for c in range(4):
            t1, bt1 = f32p.next()
            P.add("dve", tt(t1[:], y[:, c, :], mean, ALU.subtract), reads=[B_y, B_mean], writes=[bt1])
            P.add("dve", tt(t1[:], t1[:], var[:], ALU.mult), reads=[bt1, bvar], writes=[bt1])
            P.add("act", act(zs[:, c, :], t1[:], AF.Silu, scale=V[:, 192 + c:193 + c], bias=V[:, 196 + c:197 + c]),
                  reads=[bt1, B_vec[l]], pw=[B_zs])
        if debug and seg == 0 and tg == 0:
            P.add("sp", dma(dbg_zs.rearrange("p (c t) -> p c t", c=4), zs), reads=[B_zs], pw=[B_dbg], dma=True, sigbuf=B_zs)
        view, bw = load_w([(W[l]["pw"], 0, 512)], 4)
        for oc in range(4):
            pb = gen_ps.next()
            P.add("pe", mm(ps[pb][:], [(view[:, kc, oc * 128:(oc + 1) * 128], zs[:, kc, :]) for kc in range(4)]),
                  reads=[bw, B_zs], writes=[B_ps[pb]])
            P.add("act", act(cat[:, 4 + oc, :], ps[pb][:], AF.Identity, bias=V[:, 200 + oc:201 + oc]), reads=[B_ps[pb], B_vec[l]], pw=[B_ain])

        P.new_phase()
        kq = [a16(r * 512, 512) for r in range(4)]
        vq = [a16(2048 + r * 512, 512).rearrange("p (t d) -> p t d", d=128) for r in range(4)]
        bkq = [P.abuf("kq%d" % r) for r in range(4)]
        bvq = [P.abuf("vq%d" % r) for r in range(4)]
        biasT = a32(4096, 28 * 128).rearrange("p (n q) -> p n q", q=128)
        dm = a16(7680, 4 * 7 * 64).rearrange("p (i j q) -> p i j q", i=4, j=7)
        B_bias = P.abuf("biasT")
        B_dm = P.abuf("dm")
        P.add("sp", dma(biasT, W[l]["bias"].rearrange("p (n q) -> p n q", q=128)), writes=[B_bias], dma=True)
        P.add("pool", dma(dm, dmask_d.rearrange("p (i j q) -> p i j q", i=8, j=7)[:, tg * 4:(tg + 1) * 4]), writes=[B_dm], dma=True)
        st_ring = Ring([0, 1])

        def load_kv(krow0, vcol0):
            for r in range(4):
                kn_, lr_, kn = kloc(krow0)
                vn_, lc_, _vn = vloc(vcol0)
                P.add("sp", dma(kq[r], GTH[seg][kn_][r * kn + lr_:r * kn + lr_ + 128, :]), reads=[B_gth[seg][kn_]], writes=[bkq[r]], dma=True)
                P.add("sp", dma(vq[r], GTH[seg][vn_][r * T:(r + 1) * T, lc_:lc_ + 128].rearrange("(t p) c -> p t c", p=128)),
                      reads=[B_gth[seg][vn_]], writes=[bvq[r]], dma=True)

        def attn(q_ap, kpart, scale, ob, sb_):
            for kt in range(32):
                r, kl = kt // 8, kt % 8
                sbk = st_ring.next()
                P.add("pe", mm(ps[sbk][:], [(kq[r][kpart, kl * 128:(kl + 1) * 128], q_ap)]), reads=[bkq[r], B_q[tg]], writes=[B_ps[sbk]])
                pT, bpt = ptp.next()
                P.add("act", act(pT[:], ps[sbk][:], AF.Exp, scale=scale), reads=[B_ps[sbk]], writes=[bpt])
                w_, p_ = ([B_ps[ob]], []) if kt == 0 else ([], [B_ps[ob]])
                P.add("pe", mm(ps[ob][:], [(vq[r][:, kl, :], pT[:])], first=(kt == 0), last=(kt == 31)), reads=[bvq[r], bpt], writes=w_, pw=p_)
                w_, p_ = ([B_ps[sb_]], []) if kt == 0 else ([], [B_ps[sb_]])
                P.add("pe", mm(ps[sb_][:], [(ones1[:], pT[:])], first=(kt == 0), last=(kt == 31)), reads=[bpt, B_const], writes=w_, pw=p_)

        lv = lamv[l]
        for h in range(4):
            load_kv(h * 128, h * 128)
            for m in range(2):
                kp = slice(m * 64, (m + 1) * 64)
                attn(qT[kp, h, tgs(tg)], kp, 0.125, 2 + m, 4 + m)
            r0, br0 = f32p.next()
            recip(r0[:], br0, ps[4][:], B_ps[4])
            P.add("dve", tt(r0[:], ps[2][:], r0[:], ALU.mult), reads=[B_ps[2], br0], writes=[br0])
            r1, br1 = f32p.next()
            recip(r1[:], br1, ps[5][:], B_ps[5])
            P.add("dve", tt(r1[:], ps[3][:], r1[:], ALU.mult), reads=[B_ps[3], br1], writes=[br1])
            P.add("dve", stt(r1[:], r1[:], lv[:, 4:5], r0[:], ALU.mult, ALU.add), reads=[br0, br1, B_lamv[l]], writes=[br1])
            sq, bsq = sqp.next()
            P.add("act", act(sq[:], r1[:], AF.Square), reads=[br1], writes=[bsq])
            P.add("pe", mm(ps[6][:], [(ones128[:], sq[:])]), reads=[bsq, B_const], writes=[B_ps[6]])
            rs_, brs = f32p.next()
            rsqrt(rs_[:], brs, ps[6][:], B_ps[6])
            P.add("dve", stt(cat[:, h, :], r1[:], lv[:, 5:6], rs_[:], ALU.mult, ALU.mult), reads=[br1, brs, B_lamv[l]], pw=[B_ain])
        sC = 128 ** -0.5
        for g in range(2):
            load_kv(512 + g * 128, 512 + g * 128)
            for rr in range(2):
                hq = 2 * g + rr
                attn(qT[:, 4 + hq, tgs(tg)], slice(0, 128), sC, 2, 4)
                rc, brc = f32p.next()
                recip(rc[:], brc, ps[4][:], B_ps[4])
                P.add("dve", tt(cat[:, 8 + hq, :], ps[2][:], rc[:], ALU.mult), reads=[B_ps[2], brc], pw=[B_ain])
        for h in range(4):
            wK, wV = winK[seg], winV[seg]
            for part, (c_lo, c_hi) in enumerate(((0, 384), (384, 1408), (1408, 1792))):
                n_ = c_hi - c_lo
                P.add("sp", dma(kq[part][:, 0:n_], wK[h * 128:(h + 1) * 128, c_lo:c_hi]), reads=[B_winK[seg]], writes=[bkq[part]], dma=True)
                P.add("sp", dma(vq[part][:, 0:n_ // 128, :], wV[c_lo:c_hi, h * 128:(h + 1) * 128].rearrange("(t p) c -> p t c", p=128)),
                      reads=[B_winV[seg]], writes=[bvq[part]], dma=True)
            def kwin(wt):
                if wt < 3:
                    return kq[0][:, wt * 128:(wt + 1) * 128]
                if wt < 11:
                    return kq[1][:, (wt - 3) * 128:(wt - 2) * 128]
                return kq[2][:, (wt - 11) * 128:(wt - 10) * 128]

            def vwin(wt):
                if wt < 3:
                    return vq[0][:, wt, :]
                if wt < 11:
                    return vq[1][:, wt - 3, :]
                return vq[2][:, wt - 11, :]
            for i in range(4):
                lt = tg * 4 + i
                for j in range(7):
                    wt = lt + j
                    sbk = st_ring.next()
                    P.add("pe", mm(ps[sbk][:, 0:128], [(kwin(wt), qT[:, 8 + h, tg * TG + i * 128:tg * TG + (i + 1) * 128])]),
                          reads=[bkq[0], bkq[1], bkq[2], B_q[tg]], writes=[B_ps[sbk]])
                    t_, bt_ = f32p.next()
                    P.add("dve", stt(t_[:, 0:128], ps[sbk][:, 0:128], sC, biasT[:, h * 7 + j, :], ALU.mult, ALU.add),
                          reads=[B_ps[sbk], B_bias], writes=[bt_])
                    e_, be_ = sqp.next()
                    P.add("act", act(e_[:, 0:128], t_[:, 0:128], AF.Exp), reads=[bt_], writes=[be_])
                    pT, bpt = ptp.next()
                    P.add("dve", tt(pT[:, 0:128], e_[:, 0:128], dm[:, i, j, :], ALU.mult), reads=[be_, B_dm], writes=[bpt])
                    w_, p_ = ([B_ps[3]], []) if j == 0 else ([], [B_ps[3]])
                    P.add("pe", mm(ps[3][:, 0:128], [(vwin(wt), pT[:, 0:128])], first=(j == 0), last=(j == 6)), reads=[bvq[0], bvq[1], bvq[2], bpt], writes=w_, pw=p_)
                    w_, p_ = ([B_ps[5]], []) if j == 0 else ([], [B_ps[5]])
                    P.add("pe", mm(ps[5][:, 0:128], [(ones1[:], pT[:, 0:128])], first=(j == 0), last=(j == 6)), reads=[bpt, B_const], writes=w_, pw=p_)
                rc, brc = f32p.next()
                recip(rc[:, 0:128], brc, ps[5][:, 0:128], B_ps[5])
                P.add("dve", tt(cat[:, 12 + h, i * 128:(i + 1) * 128], ps[3][:, 0:128], rc[:, 0:128], ALU.mult), reads=[B_ps[3], brc], pw=[B_ain])

    def phase2b(l, tg):
        P.new_phase()
        mixed = a32(0, KC * TG).rearrange("p (c t) -> p c t", c=KC)
        B_mixed = P.abuf("mixed")
        cat = ain
        for g in range(8):
            view, bw = load_w([(W[l]["w_out"], g * 256, 256)], KC)
            for j in range(2):
                oc = g * 2 + j
                pb = gen_ps.next()
                P.add("pe", mm(ps[pb][:], [(view[:, kc, j * 128:(j + 1) * 128], cat[:, kc, :]) for kc in range(KC)]),
                      reads=[bw, B_ain], writes=[B_ps[pb]])
                P.add("dve", (lambda e, oc=oc, pb=pb: e.tensor_copy(out=mixed[:, oc, :], in_=ps[pb][:])), reads=[B_ps[pb]], pw=[B_mixed])
                sq, bsq = sqp.next()
                P.add("act", act(sq[:], ps[pb][:], AF.Square), reads=[B_ps[pb]], writes=[bsq])
                P.add("pe", mm(ps[6][:], [(onesD[:], sq[:])], first=(oc == 0), last=(oc == KC - 1)), reads=[bsq, B_const],
                      writes=[B_ps[6]] if oc == 0 else [], pw=[] if oc == 0 else [B_ps[6]])
        rstd, brstd = f32p.next()
        rsqrt(rstd[:], brstd, ps[6][:], B_ps[6])
        for oc in range(KC):
            P.add("dve", stt(mixed[:, oc, :], mixed[:, oc, :], vec[l][:, 16 + oc:17 + oc], rstd[:], ALU.mult, ALU.mult),
                  reads=[B_mixed, brstd, B_vec[l]], pw=[B_mixed])
            P.add("dve", tt(xT[:, oc, tgs(tg)], xT[:, oc, tgs(tg)], mixed[:, oc, :], ALU.add), reads=[B_mixed, B_x[tg]], pw=[B_x[tg]])

    def phase2c(l, tg):
        P.new_phase()
        fT = a32(0, KC * TG).rearrange("p (c t) -> p c t", c=KC)
        actT = a16(8192, 22 * TG // 2).rearrange("p (c t) -> p c t", c=22)
        B_fT = P.abuf("fT")
        B_act = P.abuf("act")
        rmsnorm_to_ain(l, tg, 32)
        for hf in range(2):
            for pr in range(11):
                fl0 = 2 * pr
                fc0 = hf * 22 + fl0
                viewg, bwg = load_w([(W[l]["gate"], fc0 * 128, 256)], KC)
                pg = [gen_ps.next(), gen_ps.next()]
                for j in range(2):
                    P.add("pe", mm(ps[pg[j]][:], [(viewg[:, kc, j * 128:(j + 1) * 128], ain[:, kc, :]) for kc in range(KC)]),
                          reads=[bwg, B_ain], writes=[B_ps[pg[j]]])
                sgs = []
                for j in range(2):
                    sg, bsg = f32p.next()
                    P.add("act", act(sg[:], ps[pg[j]][:], AF.Silu), reads=[B_ps[pg[j]]], writes=[bsg])
                    sgs.append((sg, bsg))
                viewu, bwu = load_w([(W[l]["up"], fc0 * 128, 256)], KC)
                pu = [gen_ps.next(), gen_ps.next()]
                for j in range(2):
                    P.add("pe", mm(ps[pu[j]][:], [(viewu[:, kc, j * 128:(j + 1) * 128], ain[:, kc, :]) for kc in range(KC)]),
                          reads=[bwu, B_ain], writes=[B_ps[pu[j]]])
                for j in range(2):
                    P.add("dve", tt(actT[:, fl0 + j, :], ps[pu[j]][:], sgs[j][0][:], ALU.mult), reads=[B_ps[pu[j]], sgs[j][1]], pw=[B_act])
            for oc2 in range(KC // 2):
                view, bw = load_w([(W[l]["down"], oc2 * 256, 256)], 22, krow0=hf * 22 * 128)
                for j in range(2):
                    oc = oc2 * 2 + j
                    pb = gen_ps.next()
                    P.add("pe", mm(ps[pb][:], [(view[:, fl, j * 128:(j + 1) * 128], actT[:, fl, :]) for fl in range(22)]), reads=[bw, B_act], writes=[B_ps[pb]])
                    if hf == 0:
                        P.add("act", act(fT[:, oc, :], ps[pb][:], AF.Copy), reads=[B_ps[pb]], pw=[B_fT])
                    else:
                        P.add("dve", tt(fT[:, oc, :], fT[:, oc, :], ps[pb][:], ALU.add), reads=[B_ps[pb], B_fT], pw=[B_fT])
                        sq, bsq = sqp.next()
                        P.add("act", act(sq[:], fT[:, oc, :], AF.Square), reads=[B_fT], writes=[bsq])
                        P.add("pe", mm(ps[6][:], [(onesD[:], sq[:])], first=(oc == 0), last=(oc == KC - 1)), reads=[bsq, B_const],
                              writes=[B_ps[6]] if oc == 0 else [], pw=[] if oc == 0 else [B_ps[6]])
        rstd, brstd = f32p.next()
        rsqrt(rstd[:], brstd, ps[6][:], B_ps[6])
        for oc in range(KC):
            P.add("dve", stt(fT[:, oc, :], fT[:, oc, :], vec[l][:, 48 + oc:49 + oc], rstd[:], ALU.mult, ALU.mult),
                  reads=[B_fT, brstd, B_vec[l]], pw=[B_fT])
            P.add("dve", tt(xT[:, oc, tgs(tg)], xT[:, oc, tgs(tg)], fT[:, oc, :], ALU.add), reads=[B_fT, B_x[tg]], pw=[B_x[tg]])

    def dyn_init(e):
        pid = e.partition_id()
        rk = e.snap(pid % 4)
        DYN["rk"] = rk
        DYN["lb"] = e.snap((rk + 3) % 4)
        DYN["rb"] = e.snap((rk + 1) % 4)
        return None
    P.ops["pool"].append(_mk_raw(dyn_init))
    setup_consts()
    if first_seg:
        for tg in range(NTG):
            P.add("sp", dma(xT[:, :, tgs(tg)], xT_d.rearrange("(c p) t -> p c t", p=128)[:, :, tgs(tg)]), writes=[B_x[tg]], dma=True)
    else:
        for tg in range(NTG):
            P.add("sp", dma(xT[:, :, tgs(tg)], stx_in.rearrange("p (c t) -> p c t", c=KC)[:, :, tgs(tg)]), writes=[B_x[tg]], dma=True)
            P.add("sp", dma(qT[:, :, tgs(tg)], stq_in.rearrange("p (c t) -> p c t", c=12)[:, :, tgs(tg)]), writes=[B_q[tg]], dma=True)
    for seg in range(seg_lo, seg_hi + 1):
        if seg >= 1:
            lprev = seg - 1
            make_windows(seg - 1)
            for tg in range(NTG):
                phase2a(lprev, seg - 1, tg)
                if debug and seg == 1:
                    P.add("sp", dma(dbg_cat[tg], ain[:].rearrange("p c t -> p (c t)")), reads=[B_ain], pw=[B_dbg], dma=True, sigbuf=B_ain)
                phase2b(lprev, tg)
                if debug and seg == 1:
                    P.add("sp", dma(dbg_xmid.rearrange("p (c t) -> p c t", c=KC)[:, :, tgs(tg)], xT[:, :, tgs(tg)]), reads=[B_x[tg]], pw=[B_dbg], dma=True, sigbuf=B_x[tg])
                phase2c(lprev, tg)
        if seg <= 1:
            for tg in range(NTG):
                phase1(seg, seg, tg)
            exchange(seg)
    finals = []
    if last_seg:
        for tg in range(NTG):
            P.add("sp", dma(out_d.rearrange("(c p) t -> p c t", p=128)[:, :, tgs(tg)], xT[:, :, tgs(tg)]), reads=[B_x[tg]], pw=[B_out], dma=True, sigbuf=B_x[tg])
        finals = [B_out]
    elif not fused:
        for tg in range(NTG):
            P.add("sp", dma(stx_out.rearrange("p (c t) -> p c t", c=KC)[:, :, tgs(tg)], xT[:, :, tgs(tg)]), reads=[B_x[tg]], pw=[B_st], dma=True, sigbuf=B_x[tg])
            P.add("sp", dma(stq_out.rearrange("p (c t) -> p c t", c=12)[:, :, tgs(tg)], qT[:, :, tgs(tg)]), reads=[B_q[tg]], pw=[B_st], dma=True, sigbuf=B_q[tg])
        finals = [B_st, B_dbg] + list(B_snd[seg_lo].values())
    P.add("sp", None, reads=finals)
    P.finalize(nc, stack)
    stack.close()
    nc._in_names = in_names
    return nc


def _host_consts():
    theta = 10000.0
    inv = np.power(theta, -np.arange(0, 64, 2, dtype=np.float32) / 64).astype(np.float32)
    p = np.arange(128)
    j = p % 32
    sign = np.where((p % 64) < 32, -1.0, 1.0).astype(np.float32)
    tabs, dmasks, halos = [], [], []
    for rank in range(4):
        s = (rank * T + np.arange(T))
        posA = s.astype(np.float32)
        angA = posA[None, :] * inv[j][:, None]
        row = (s // 64).astype(np.float32)
        col = (s % 64).astype(np.float32)
        posC = np.where((p < 64)[:, None], row[None, :], col[None, :]).astype(np.float32)
        angC = posC * inv[j][:, None]
        tab = np.stack([np.cos(angA), np.sin(angA) * sign[:, None], np.cos(angC), np.sin(angC) * sign[:, None]], axis=1)
        tabs.append(np.ascontiguousarray(tab.reshape(128, 4 * T).astype(np.float32)))
        dmk = np.zeros((128, 8, 7, 128), np.float32)
        kk = np.arange(128)
        kr_par, kc = kk // 64, kk % 64
        qq = np.arange(128)
        qr_l, qc = qq // 64, qq % 64
        for lt in range(8):
            b = rank * 8 + lt
            qr = 2 * b + qr_l
            win_r = np.clip(qr - 4, 0, 56)
            win_c = np.clip(qc - 8, 0, 48)
            for jj in range(7):
                kr = 2 * b + 2 * (jj - 3) + kr_par
                ok = ((kr[:, None] >= 0) & (kr[:, None] < 64) & (kr[:, None] >= win_r[None, :]) & (kr[:, None] < win_r[None, :] + 8)
                      & (kc[:, None] >= win_c[None, :]) & (kc[:, None] < win_c[None, :] + 16))
                dmk[:, lt, jj, :] = ok
        dmasks.append(np.ascontiguousarray(dmk.reshape(128, 8 * 7 * 128)))
        hl = np.zeros((128, 2), np.float32)
        hl[:, 0] = 0.0 if rank == 0 else 1.0
        hl[:, 1] = 0.0 if rank == 3 else 1.0
        halos.append(hl)
    m = np.arange(128)
    perm = np.zeros((128, 128), np.float32)
    perm[m ^ 32, m] = 1.0
    return tabs, dmasks, halos, perm


def _layer_inputs(inp, l):
    def pc(v, n):
        return np.ascontiguousarray(np.asarray(v, np.float32).reshape(n, 128).T)
    vec = np.zeros((128, NV), np.float32)
    vec[:, 0:16] = pc(inp["norm_mix_pre"][l], 16)
    vec[:, 16:32] = pc(inp["norm_mix_post"][l], 16)
    vec[:, 32:48] = pc(inp["norm_ffn_pre"][l], 16)
    vec[:, 48:64] = pc(inp["norm_ffn_post"][l], 16)
    dw = np.asarray(inp["conv_dw"][l], np.float32)
    vec[:, 64:188] = dw.reshape(31, 4, 128).transpose(2, 1, 0).reshape(128, 124)
    vec[:, 188:192] = pc(inp["conv_dw_b"][l], 4)
    vec[:, 192:196] = pc(inp["conv_ln_g"][l], 4)
    vec[:, 196:200] = pc(inp["conv_ln_b"][l], 4)
    vec[:, 200:204] = pc(inp["conv_pw_b"][l], 4)
    vec[:, 204] = np.asarray(inp["diff_subln"][l], np.float32)
    vec[:, 205] = np.asarray(inp["gqa_q_norm"][l], np.float32)
    vec[:, 206] = np.asarray(inp["gqa_k_norm"][l], np.float32)
    lamb = np.ascontiguousarray(np.broadcast_to(np.asarray(inp["diff_lambda"][l], np.float32).reshape(1, 256), (128, 256)))
    rpb = np.asarray(inp["na_rpb"][l], np.float32)
    kk = np.arange(128)
    kr_par, kc = kk // 64, kk % 64
    qq = np.arange(128)
    qr_l, qc = qq // 64, qq % 64
    bias = np.zeros((128, 4, 7, 128), np.float32)
    for jj in range(7):
        dr = 2 * (jj - 3) + kr_par[:, None] - qr_l[None, :]
        ir = np.clip(dr + 7, 0, 14)
        ic = np.clip(kc[:, None] - qc[None, :] + 15, 0, 30)
        for h in range(4):
            bias[:, h, jj, :] = rpb[h][ir, ic]
    d = {
        "w_in%d" % l: np.ascontiguousarray(inp["w_in"][l], np.float32), "w_out%d" % l: np.ascontiguousarray(inp["w_out"][l], np.float32),
        "gate%d" % l: np.ascontiguousarray(inp["ffn_gate"][l], np.float32), "up%d" % l: np.ascontiguousarray(inp["ffn_up"][l], np.float32),
        "down%d" % l: np.ascontiguousarray(inp["ffn_down"][l], np.float32), "pw%d" % l: np.ascontiguousarray(inp["conv_pw"][l], np.float32),
        "vec%d" % l: vec, "lamb%d" % l: lamb, "bias%d" % l: np.ascontiguousarray(bias.reshape(128, 28 * 128)),
    }
    return d


_NC_CACHE = {}


def _get_nc(lo, hi, fused):
    key = (lo, hi, fused)
    if key not in _NC_CACHE:
        _NC_CACHE[key] = build_program(lo, hi, fused)
    return _NC_CACHE[key]


def kernel(**inp):
    inp = {k: np.asarray(v) for k, v in inp.items()}
    x = inp["x"].astype(np.float32, copy=False)
    tabs, dmasks, halos, perm = _host_consts()
    lay = {l: _layer_inputs(inp, l) for l in range(L)}
    cores = list(range(8))

    def common(c):
        r = c % 4
        return {"tabs": tabs[r], "dmask": dmasks[r], "halo": halos[r], "perm": perm}

    def xT_of(c):
        b, r = c // 4, c % 4
        return np.ascontiguousarray(x[b, r * T:(r + 1) * T, :].T)

    if FUSED:
        nc = _get_nc(0, 2, True)
        maps = []
        for c in cores:
            m = common(c)
            m.update(lay[0])
            m.update(lay[1])
            m["xT"] = xT_of(c)
            maps.append(m)
        maps = [{k: m[k] for k in nc._in_names} for m in maps]
        res = run_bass_kernel_spmd(nc, maps, core_ids=cores)
        outs = [np.asarray(res.results[c]["outT"]) for c in cores]
    else:
        state = None
        outs = None
        for seg in range(3):
            nc = _get_nc(seg, seg, False)
            maps = []
            for c in cores:
                m = common(c)
                for l in sorted({0 if seg <= 1 else 1, 1 if seg >= 1 else 0}):
                    m.update(lay[l])
                if seg == 0:
                    m["xT"] = xT_of(c)
                else:
                    g0 = (c // 4) * 4
                    m["stx_in"] = state[c]["stx_out"]
                    m["stq_in"] = state[c]["stq_out"]
                    for n_ in ("kA", "kC", "kD", "uu", "vA", "vC", "vD"):
                        m["gth_" + n_] = np.concatenate([state[g0 + r]["snd_" + n_] for r in range(4)], axis=0)
                maps.append(m)
            maps = [{k: m[k] for k in nc._in_names} for m in maps]
            res = run_bass_kernel_spmd(nc, maps, core_ids=cores)
            if seg < 2:
                state = [{k: np.asarray(v) for k, v in res.results[c].items()} for c in cores]
            else:
                outs = [np.asarray(res.results[c]["outT"]) for c in cores]
    out = np.zeros((2, 4096, D), np.float32)
    for c in cores:
        b, r = c // 4, c % 4
        out[b, r * T:(r + 1) * T, :] = outs[c].T
    return out
```

```python
import contextlib
import numpy as np
import ml_dtypes
import concourse.bass as bass
import concourse.mybir as mybir
from concourse.bass_utils import run_bass_kernel_spmd

F32 = mybir.dt.float32
BF16 = mybir.dt.bfloat16
AF = mybir.ActivationFunctionType
ALU = mybir.AluOpType

L = 2
D = 2048
KC = 16
T = 1024
TG = 512
NTG = 2
FF = 5632
FC = 44
INC = 5120
EPS = 1e-6
NV = 208
KROWS = 1792
VCOLS = 1280
ENGS = ("pe", "act", "dve", "pool", "sp")
FUSED = True
KCUT = 0


class Buf:
    __slots__ = ("name", "ws", "rs", "sem", "cnt", "arena", "psum")

    def __init__(self, name, arena=False, psum=False):
        self.psum = psum
        self.name = name
        self.ws = {}
        self.rs = {}
        self.sem = None
        self.cnt = 0
        self.arena = arena


class Op:
    __slots__ = ("eng", "fn", "deps", "dma", "cc", "sig", "val", "dbuf", "ninc")


class Prog:
    def __init__(self):
        self.ops = {e: [] for e in ENGS}
        self.last_compute = {}
        self.arena_dmas = []
        self.fence_ops = []
        self.dma_bufs = []
        self.cc_cnt = 0

    def add(self, eng, fn, reads=(), writes=(), pw=(), dma=False, cc=False, ninc=1, sigbuf=None):
        op = Op()
        op.eng = eng
        op.fn = fn
        op.dma = dma
        op.cc = cc
        op.sig = False
        op.val = None
        op.dbuf = None
        op.ninc = ninc
        deps = {}
        for b in reads:
            for w in b.ws.values():
                deps[w] = True
            if b.psum:
                for r in b.rs.values():
                    if r.eng != eng and r not in deps:
                        deps[r] = False
        for b in list(writes) + list(pw):
            for r in b.rs.values():
                if r not in deps:
                    deps[r] = False
        for b in writes:
            for w in b.ws.values():
                if w not in deps:
                    deps[w] = False
        op.deps = deps
        if dma:
            db = sigbuf if sigbuf is not None else (list(writes) + list(pw))[0]
            if db.sem is None:
                self.dma_bufs.append(db)
                db.sem = True
            db.cnt += 16 * ninc
            op.val = db.cnt
            op.dbuf = db
            if any(b.arena for b in list(reads) + list(writes) + list(pw)):
                self.arena_dmas.append(op)
        elif cc:
            self.cc_cnt += 1
            op.val = self.cc_cnt
        else:
            self.last_compute[eng] = op
        wkey = ("dma", id(op.dbuf)) if dma else ("cc" if cc else eng)
        for b in reads:
            if dma or cc:
                b.rs[("dma", id(op.dbuf) if dma else "cc")] = op
            else:
                b.rs[eng] = op
        for b in writes:
            b.ws = {wkey: op}
            b.rs = {}
        for b in pw:
            b.ws[wkey] = op
        self.ops[eng].append(op)
        return op

    def new_phase(self):
        self.fence_ops = list(self.last_compute.values()) + list(self.arena_dmas)
        self.arena_dmas = []

    def abuf(self, name):
        if not hasattr(self, "_ab"):
            self._ab = {}
        if name not in self._ab:
            self._ab[name] = Buf(name, arena=True)
        return self.fence(self._ab[name])

    def fence(self, buf):
        buf.ws = {}
        buf.rs = {("f", i): o for i, o in enumerate(self.fence_ops)}
        return buf

    def finalize(self, nc, stack):
        engsem = {e: stack.enter_context(nc.semaphore("s_" + e)) for e in ENGS}
        ccsem = stack.enter_context(nc.semaphore("s_cc"))
        for i, b in enumerate(self.dma_bufs):
            b.sem = stack.enter_context(nc.semaphore("d%d" % i))
        for e in ENGS:
            for op in self.ops[e]:
                for d, raw in op.deps.items():
                    if d.dma or d.cc:
                        continue
                    if d.eng == op.eng and not op.dma and not op.cc:
                        if e == "pe":
                            continue
                    d.sig = True
        for e in ENGS:
            c = 0
            for op in self.ops[e]:
                if not op.dma and not op.cc and op.sig:
                    c += 1
                    op.val = c
        block = stack.enter_context(nc.Block())

        def emit(e, eng):
            waited = {}
            for op in self.ops[e]:
                for d, raw in op.deps.items():
                    if d.dma:
                        sem, key, v = d.dbuf.sem, ("d", id(d.dbuf)), d.val
                    elif d.cc:
                        sem, key, v = ccsem, "cc", d.val
                    else:
                        if d.eng == e and not op.dma and not op.cc and e == "pe":
                            continue
                        sem, key, v = engsem[d.eng], d.eng, d.val
                    if waited.get(key, 0) >= v:
                        continue
                    waited[key] = v
                    eng.wait_ge(sem, v)
                if op.fn is None:
                    continue
                res = op.fn(eng)
                if op.dma:
                    if not isinstance(res, (list, tuple)):
                        res = [res]
                    assert len(res) == op.ninc
                    for r in res:
                        r.then_inc(op.dbuf.sem, 16)
                elif op.cc:
                    res.then_inc(ccsem, 1)
                elif op.sig:
                    res.then_inc(engsem[e], 1)

        @block.tensor
        def _(eng):
            emit("pe", eng)

        @block.scalar
        def _(eng):
            emit("act", eng)

        @block.vector
        def _(eng):
            emit("dve", eng)

        @block.gpsimd
        def _(eng):
            emit("pool", eng)

        @block.sync
        def _(eng):
            emit("sp", eng)


def _mk_raw(fn):
    op = Op()
    op.eng = "pool"
    op.fn = fn
    op.deps = {}
    op.dma = False
    op.cc = False
    op.sig = False
    op.val = None
    op.dbuf = None
    op.ninc = 0
    return op


class Ring:
    def __init__(self, items):
        self.items = items
        self.i = 0

    def next(self):
        it = self.items[self.i % len(self.items)]
        self.i += 1
        return it


def build_program(seg_lo, seg_hi, fused, debug=False):
    nc = bass.Bass("TRN2", target_bir_lowering=False)
    P = Prog()
    DYN = {}
    stack = contextlib.ExitStack()
    first_seg, last_seg = seg_lo == 0, seg_hi == 2
    layers = sorted({0 if s <= 1 else 1 for s in range(seg_lo, seg_hi + 1)} | {1 if s >= 1 else 0 for s in range(seg_lo, seg_hi + 1)})

    in_names = []

    def din(name, shape, dt=F32):
        in_names.append(name)
        return nc.dram_tensor(name, list(shape), dt, kind="ExternalInput").ap()

    def dout(name, shape, dt=F32):
        return nc.dram_tensor(name, list(shape), dt, kind="ExternalOutput").ap()

    class _LazyW(dict):
        def __init__(self, l):
            super().__init__()
            self.l = l

        def __missing__(self, key):
            shp = {"w_in": [D, INC], "w_out": [D, D], "gate": [D, FF], "up": [D, FF], "down": [FF, D], "pw": [512, 512],
                   "vec": [128, NV], "lamb": [128, 256], "bias": [128, 28 * 128]}[key]
            v = din("%s%d" % (key, self.l), shp)
            self[key] = v
            return v
    W = {l: _LazyW(l) for l in layers}
    tabs_d = din("tabs", [128, 4 * T])
    dmask_d = din("dmask", [128, 8 * 7 * 128])
    perm_d = din("perm", [128, 128])
    halo_d = din("halo", [128, 2])
    if first_seg:
        xT_d = din("xT", [D, T])
    else:
        stx_in = din("stx_in", [128, KC * T])
        stq_in = din("stq_in", [128, 12 * T], BF16)
    if last_seg:
        out_d = dout("outT", [D, T])
    KSEG = [("kA", 0, 512), ("kC", 512, 256), ("kD", 768, 512), ("uu", 1280, 512)]
    VSEG = [("vA", 0, 512), ("vC", 512, 256), ("vD", 768, 512)]
    NAMES = [k[0] for k in KSEG] + [v[0] for v in VSEG]

    def kloc(row):
        for name, r0, n in KSEG:
            if r0 <= row < r0 + n:
                return name, row - r0, n
        raise ValueError(row)

    def vloc(col):
        for name, c0, n in VSEG:
            if c0 <= col < c0 + n:
                return name, col - c0, n
        raise ValueError(col)

    def snd_shape(name):
        for nm, _, n in KSEG:
            if nm == name:
                return [n, T]
        for nm, _, n in VSEG:
            if nm == name:
                return [T, n]

    def gth_shape(name):
        sh = snd_shape(name)
        return [4 * sh[0], sh[1]]
    SND, GTH, SNDH, GTHH = {}, {}, {}, {}
    if not fused:
        if not last_seg:
            stx_out = dout("stx_out", [128, KC * T])
            stq_out = dout("stq_out", [128, 12 * T], BF16)
            SND[seg_lo] = {n_: dout("snd_" + n_, snd_shape(n_), BF16) for n_ in NAMES}
        if not first_seg:
            GTH[seg_lo - 1] = {n_: din("gth_" + n_, gth_shape(n_), BF16) for n_ in NAMES}
    else:
        for s_ in (0, 1):
            SNDH[s_] = {n_: nc.dram_tensor("snd%d_%s" % (s_, n_), snd_shape(n_), BF16) for n_ in NAMES}
            GTHH[s_] = {n_: nc.dram_tensor("gth%d_%s" % (s_, n_), gth_shape(n_), BF16) for n_ in NAMES}
            SND[s_] = {n_: SNDH[s_][n_].ap() for n_ in NAMES}
            GTH[s_] = {n_: GTHH[s_][n_].ap() for n_ in NAMES}
    B_snd = {s_: {n_: Buf("snd%d%s" % (s_, n_)) for n_ in NAMES} for s_ in (0, 1)}
    B_gth = {s_: {n_: Buf("gth%d%s" % (s_, n_)) for n_ in NAMES} for s_ in (0, 1)}
    B_out = Buf("out")
    B_st = Buf("stout")
    if debug:
        dbg_cat = dout("dbg_cat", [NTG, 128, KC * TG], BF16)
        dbg_xmid = dout("dbg_xmid", [128, KC * T])
        dbg_y = dout("dbg_y", [128, 4 * TG])
        dbg_zs = dout("dbg_zs", [128, 4 * TG], BF16)
    B_dbg = Buf("dbg")
    winK = {s_: nc.dram_tensor("winK%d" % s_, [1024, 1792], BF16) for s_ in (0, 1)}
    winV = {s_: nc.dram_tensor("winV%d" % s_, [1792, 512], BF16) for s_ in (0, 1)}
    B_winK = {s_: Buf("winK%d" % s_) for s_ in (0, 1)}
    B_winV = {s_: Buf("winV%d" % s_) for s_ in (0, 1)}

    def sb(name, shape, dt):
        return stack.enter_context(nc.sbuf_tensor("t_" + name, list(shape), dt))

    xT = sb("xT", [128, KC, T], F32)
    qT = sb("qT", [128, 12, T], BF16)
    ain = sb("ain", [128, KC, TG], BF16)
    B_x = [Buf("x%d" % i) for i in range(NTG)]
    B_q = [Buf("q%d" % i) for i in range(NTG)]
    B_ain = Buf("ain")
    wbufs = Ring([(sb("wb%d" % i, [128, 4096], BF16), Buf("wb%d" % i)) for i in range(3)])
    ptp = Ring([(sb("pt%d" % i, [128, TG], BF16), Buf("pt%d" % i)) for i in range(3)])
    sqp = Ring([(sb("sq%d" % i, [128, TG], BF16), Buf("sq%d" % i)) for i in range(3)])
    f32p = Ring([(sb("f%d" % i, [128, TG], F32), Buf("f%d" % i)) for i in range(4)])
    stgp = Ring([(sb("stg%d" % i, [128, TG], BF16), Buf("stg%d" % i)) for i in range(2)])
    vec = {l: sb("vec%d" % l, [128, NV], F32) for l in layers}
    lamb = {l: sb("lamb%d" % l, [128, 256], F32) for l in layers}
    lamv = {l: sb("lamv%d" % l, [128, 8], F32) for l in layers}
    B_vec = {l: Buf("vec%d" % l) for l in layers}
    B_lamv = {l: Buf("lamv%d" % l) for l in layers}
    perm = sb("perm", [128, 128], BF16)
    onesD = sb("onesD", [128, 128], BF16)
    ones128 = sb("ones128", [128, 128], BF16)
    ones512 = sb("ones512", [128, 128], BF16)
    ones1 = sb("ones1", [128, 128], BF16)
    halo = sb("halo", [128, 2], F32)
    epsc = sb("epsc", [128, 1], F32)
    B_const = Buf("const")
    B_perm = Buf("perm")
    B_halo = Buf("halo")
    ARENA_F32 = 13824
    arena = sb("arena", [128, ARENA_F32], F32)

    def a32(off, n):
        return arena[:, off:off + n]

    def a16(off, n):
        return arena[:, off:off + n].bitcast(BF16)

    ps = [stack.enter_context(nc.psum_tensor("ps%d" % i, [128, 512], F32)) for i in range(8)]
    B_ps = [Buf("ps%d" % i, psum=True) for i in range(8)]

    def mm(out, pairs, first=True, last=True):
        def fn(e):
            n = len(pairs)
            ins = None
            for i, (l_, r_) in enumerate(pairs):
                ins = e.matmul(out, l_, r_, start=(first and i == 0), stop=(last and i == n - 1))
            return ins
        return fn

    def dma(out, in_):
        return lambda e: e.dma_start(out=out, in_=in_)

    def act(out, in_, func, scale=None, bias=None):
        kw = {}
        if scale is not None:
            kw["scale"] = scale
        if bias is not None:
            kw["bias"] = bias
        return lambda e: e.activation(out=out, in_=in_, func=func, **kw)

    def tt(out, a, b, op):
        return lambda e: e.tensor_tensor(out=out, in0=a, in1=b, op=op)

    def ts(out, a, s1, s2, op0, op1):
        return lambda e: e.tensor_scalar(out=out, in0=a, scalar1=s1, scalar2=s2, op0=op0, op1=op1)

    def stt(out, a, s, b, op0, op1):
        return lambda e: e.scalar_tensor_tensor(out=out, in0=a, scalar=s, in1=b, op0=op0, op1=op1)

    def tgs(tg):
        return slice(tg * TG, (tg + 1) * TG)

    def recip(out, bout, in_, bin_):
        P.add("act", act(out, in_, AF.Ln), reads=[bin_], writes=[bout])
        P.add("act", act(out, out, AF.Exp, scale=-1.0), reads=[bout], writes=[bout])

    def rsqrt(out, bout, in_, bin_):
        P.add("act", act(out, in_, AF.Ln, bias=epsc[:, 0:1]), reads=[bin_, B_const], writes=[bout])
        P.add("act", act(out, out, AF.Exp, scale=-0.5), reads=[bout], writes=[bout])

    def setup_consts():
        for t_, v in ((onesD, 1.0 / D), (ones128, 1.0 / 128), (ones512, 1.0 / 512), (ones1, 1.0)):
            P.add("dve", (lambda e, t_=t_, v=v: e.memset(t_[:], v)), pw=[B_const])
        P.add("dve", (lambda e: e.memset(epsc[:], EPS)), pw=[B_const])
        P.add("pool", dma(perm[:], perm_d), writes=[B_perm], dma=True)
        P.add("sp", dma(halo[:], halo_d), writes=[B_halo], dma=True)
        for l in layers:
            P.add("sp", dma(vec[l][:], W[l]["vec"]), writes=[B_vec[l]], dma=True)
            bl = Buf("lamb")
            P.add("sp", dma(lamb[l][:], W[l]["lamb"]), writes=[bl], dma=True)
            lv = lamv[l]
            lam_init = 0.8 - 0.6 * float(np.exp(-0.3 * l))
            tmpf, btmp = f32p.next()
            P.add("dve", tt(tmpf[:, 0:64], lamb[l][:, 0:64], lamb[l][:, 64:128], ALU.mult), reads=[bl], writes=[btmp])
            P.add("dve", (lambda e, lv=lv, tmpf=tmpf: e.reduce_sum(out=lv[:, 0:1], in_=tmpf[:, 0:64], axis=mybir.AxisListType.X)),
                  reads=[btmp], pw=[B_lamv[l]])
            tmpf2, btmp2 = f32p.next()
            P.add("dve", tt(tmpf2[:, 0:64], lamb[l][:, 128:192], lamb[l][:, 192:256], ALU.mult), reads=[bl], writes=[btmp2])
            P.add("dve", (lambda e, lv=lv, tmpf2=tmpf2: e.reduce_sum(out=lv[:, 1:2], in_=tmpf2[:, 0:64], axis=mybir.AxisListType.X)),
                  reads=[btmp2], pw=[B_lamv[l]])
            P.add("act", act(lv[:, 2:4], lv[:, 0:2], AF.Exp), reads=[B_lamv[l]], pw=[B_lamv[l]])
            P.add("dve", stt(lv[:, 4:5], lv[:, 3:4], -lam_init, lv[:, 2:3], ALU.add, ALU.subtract), reads=[B_lamv[l]], pw=[B_lamv[l]])
            P.add("dve", (lambda e, lv=lv, l=l, li=lam_init: e.tensor_scalar_mul(out=lv[:, 5:6], in0=vec[l][:, 204:205], scalar1=1.0 - li)),
                  reads=[B_vec[l]], pw=[B_lamv[l]])

    def rmsnorm_to_ain(l, tg, gcol):
        for c in range(KC):
            sq, bsq = sqp.next()
            P.add("act", act(sq[:], xT[:, c, tgs(tg)], AF.Square), reads=[B_x[tg]], writes=[bsq])
            P.add("pe", mm(ps[6][:], [(onesD[:], sq[:])], first=(c == 0), last=(c == KC - 1)), reads=[bsq, B_const],
                  writes=[B_ps[6]] if c == 0 else [], pw=[] if c == 0 else [B_ps[6]])
        rstd, brstd = f32p.next()
        rsqrt(rstd[:], brstd, ps[6][:], B_ps[6])
        for c in range(KC):
            P.add("dve", stt(ain[:, c, :], xT[:, c, tgs(tg)], vec[l][:, gcol + c:gcol + c + 1], rstd[:], ALU.mult, ALU.mult),
                  reads=[B_x[tg], brstd, B_vec[l]], pw=[B_ain])

    def load_w(segs, kchunks, krow0=0):
        wt, bw = wbufs.next()
        tot = sum(s[2] for s in segs)
        assert kchunks * tot <= 4096
        view = wt[:, 0:kchunks * tot].rearrange("p (k c) -> p k c", c=tot)
        off = 0
        for i, (Wd, c0, n) in enumerate(segs):
            src = Wd[krow0:krow0 + kchunks * 128, c0:c0 + n].rearrange("(k p) c -> p k c", p=128)
            P.add("pool", dma(view[:, :, off:off + n], src), writes=[bw] if i == 0 else [], pw=[] if i == 0 else [bw], dma=True)
            off += n
        return view, bw

    gen_ps = Ring([0, 1, 2, 3])

    def phase1(l, seg, tg):
        P.new_phase()
        tabs = a32(0, 4 * TG).rearrange("p (f t) -> p f t", f=4)
        B_tabs = P.abuf("tabs")
        P.add("sp", dma(tabs, tabs_d.rearrange("p (f t) -> p f t", f=4)[:, :, tgs(tg)]), writes=[B_tabs], dma=True)
        if KCUT == 1:
            return
        rmsnorm_to_ain(l, tg, 0)
        if KCUT == 2:
            return
        w_in = W[l]["w_in"]

        def rope(psb, kind, dest, dest_bufs_pw, gcol=None):
            if kind == "plain":
                P.add("act", act(dest, ps[psb][:], AF.Copy), reads=[B_ps[psb]], pw=dest_bufs_pw)
                return
            ci, si = (0, 1) if kind == "A" else (2, 3)
            if kind == "C":
                sq, bsq = sqp.next()
                P.add("act", act(sq[:], ps[psb][:], AF.Square), reads=[B_ps[psb]], writes=[bsq])
                P.add("pe", mm(ps[7][:], [(ones128[:], sq[:])]), reads=[bsq, B_const], writes=[B_ps[7]])
                rstd2, brstd2 = f32p.next()
                rsqrt(rstd2[:], brstd2, ps[7][:], B_ps[7])
            xb, bxb = sqp.next()
            if kind == "C":
                P.add("act", act(xb[:], ps[psb][:], AF.Copy, scale=vec[l][:, gcol:gcol + 1]), reads=[B_ps[psb], B_vec[l]], writes=[bxb])
            else:
                P.add("act", act(xb[:], ps[psb][:], AF.Copy), reads=[B_ps[psb]], writes=[bxb])
            P.add("pe", mm(ps[5][:], [(perm[:], xb[:])]), reads=[bxb, B_perm], writes=[B_ps[5]])
            t1, bt1 = f32p.next()
            if kind == "C":
                P.add("dve", stt(t1[:], ps[psb][:], vec[l][:, gcol:gcol + 1], tabs[:, ci, :], ALU.mult, ALU.mult),
                      reads=[B_ps[psb], B_vec[l], B_tabs], writes=[bt1])
            else:
                P.add("dve", tt(t1[:], ps[psb][:], tabs[:, ci, :], ALU.mult), reads=[B_ps[psb], B_tabs], writes=[bt1])
            t2, bt2 = f32p.next()
            P.add("dve", tt(t2[:], ps[5][:], tabs[:, si, :], ALU.mult), reads=[B_ps[5], B_tabs], writes=[bt2])
            if kind == "C":
                P.add("dve", tt(t1[:], t1[:], t2[:], ALU.add), reads=[bt1, bt2], writes=[bt1])
                P.add("dve", tt(dest, t1[:], rstd2[:], ALU.mult), reads=[bt1, brstd2], pw=dest_bufs_pw)
            else:
                P.add("dve", tt(dest, t1[:], t2[:], ALU.add), reads=[bt1, bt2], pw=dest_bufs_pw)

        def to_sendK(kind, psb, row0, gcol=None):
            stg, bstg = stgp.next()
            rope(psb, kind, stg[:], [bstg], gcol)
            nm_, lr_, _n = kloc(row0)
            P.add("sp", dma(SND[seg][nm_][lr_:lr_ + 128, tgs(tg)], stg[:]), reads=[bstg], pw=[B_snd[seg][nm_]], dma=True, sigbuf=bstg)

        groups = [
            ((0, 1), ("qA", 0)), ((2, 3), ("qA", 2)), ((4, 5), ("kA", 0)), ((6, 7), ("kA", 256)),
            ((12, 16), ("glu", 0)), ((13, 17), ("glu", 1)), ((14, 18), ("glu", 2)), ((15, 19), ("glu", 3)),
            ((20, 21), ("qC", 4)), ((22, 23), ("qC", 6)), ((24, 25), ("kC", 512)),
            ((28, 29), ("qD", 8)), ((30, 31), ("qD", 10)), ((32, 33), ("kD", 768)), ((34, 35), ("kD", 1024)),
        ]
        if KCUT == 3:
            groups = groups[:1]
        if KCUT == 4:
            groups = groups[:5]
        for (c0, c1), (kind, arg) in groups:
            if c1 == c0 + 1:
                view, bw = load_w([(w_in, c0 * 128, 256)], KC)
            else:
                view, bw = load_w([(w_in, c0 * 128, 128), (w_in, c1 * 128, 128)], KC)
            banks = []
            for j in range(2):
                pb = gen_ps.next()
                banks.append(pb)
                P.add("pe", mm(ps[pb][:], [(view[:, kc, j * 128:(j + 1) * 128], ain[:, kc, :]) for kc in range(KC)]),
                      reads=[bw, B_ain], writes=[B_ps[pb]])
            if kind == "glu":
                sg, bsg = f32p.next()
                P.add("act", act(sg[:], ps[banks[1]][:], AF.Sigmoid), reads=[B_ps[banks[1]]], writes=[bsg])
                stg, bstg = stgp.next()
                P.add("dve", tt(stg[:], ps[banks[0]][:], sg[:], ALU.mult), reads=[B_ps[banks[0]], bsg], writes=[bstg])
                P.add("sp", dma(SND[seg]["uu"][arg * 128:(arg + 1) * 128, tgs(tg)], stg[:]), reads=[bstg], pw=[B_snd[seg]["uu"]], dma=True, sigbuf=bstg)
                continue
            for j in range(2):
                pb = banks[j]
                if kind == "qA":
                    rope(pb, "A", qT[:, arg + j, tgs(tg)], [B_q[tg]])
                elif kind == "qC":
                    rope(pb, "C", qT[:, arg + j, tgs(tg)], [B_q[tg]], gcol=205)
                elif kind == "qD":
                    rope(pb, "plain", qT[:, arg + j, tgs(tg)], [B_q[tg]])
                elif kind == "kA":
                    to_sendK("A", pb, arg + j * 128)
                elif kind == "kC":
                    to_sendK("C", pb, arg + j * 128, gcol=206)
                elif kind == "kD":
                    to_sendK("plain", pb, arg + j * 128)
        if KCUT in (3, 4, 5):
            return
        for wc0, vc0 in ((1024, 0), (1280, 256), (3328, 512), (4608, 768), (4864, 1024)):
            view, bw = load_w([(w_in, wc0, 256)], KC)
            for t4 in range(4):
                pb = gen_ps.next()
                P.add("pe", mm(ps[pb][:, 0:256], [(ain[:, kc, t4 * 128:(t4 + 1) * 128], view[:, kc, :]) for kc in range(KC)]),
                      reads=[bw, B_ain], writes=[B_ps[pb]])
                stg, bstg = stgp.next()
                P.add("act", act(stg[:, 0:256], ps[pb][:, 0:256], AF.Copy), reads=[B_ps[pb]], writes=[bstg])
                r0 = tg * TG + t4 * 128
                nm_, lc_, _n = vloc(vc0)
                P.add("sp", dma(SND[seg][nm_][r0:r0 + 128, lc_:lc_ + 256], stg[:, 0:256]), reads=[bstg], pw=[B_snd[seg][nm_]], dma=True, sigbuf=bstg)

    def exchange(seg):
        if not fused:
            return
        for n_ in NAMES:
            P.add("pool", (lambda e, n_=n_: e.collective_compute("AllGather", ALU.bypass, replica_groups=[[0, 1, 2, 3], [4, 5, 6, 7]],
                                                                 ins=[SNDH[seg][n_].ap().opt()], outs=[GTHH[seg][n_].ap().opt()])),
                  reads=[B_snd[seg][n_]], writes=[B_gth[seg][n_]], cc=True)

    def make_windows(seg):
        wK, wV = winK[seg], winV[seg]
        gkD = GTH[seg]["kD"].rearrange("(r k) t -> r k t", r=4)
        guu = GTH[seg]["uu"].rearrange("(r k) t -> r k t", r=4)
        gvD = GTH[seg]["vD"].rearrange("(r k) t -> r k t", r=4)

        def kcopies(e):
            return [
                e.dma_start(out=wK[0:512, 0:384], in_=gkD[bass.ds(DYN["lb"], 1), :, 640:1024].rearrange("o k t -> (o k) t")),
                e.dma_start(out=wK[0:512, 1408:1792], in_=gkD[bass.ds(DYN["rb"], 1), :, 0:384].rearrange("o k t -> (o k) t")),
                e.dma_start(out=wK[512:1024, 0:384], in_=guu[bass.ds(DYN["lb"], 1), :, 640:1024].rearrange("o k t -> (o k) t")),
                e.dma_start(out=wK[512:1024, 1408:1792], in_=guu[bass.ds(DYN["rb"], 1), :, 0:384].rearrange("o k t -> (o k) t")),
            ]
        P.add("pool", kcopies, reads=[B_gth[seg]["kD"], B_gth[seg]["uu"]], writes=[B_winK[seg]], dma=True, ninc=4)
        if fused:
            P.add("sp", dma(wK[0:512, 384:1408], SND[seg]["kD"]), reads=[B_snd[seg]["kD"]], pw=[B_winK[seg]], dma=True)
            P.add("sp", dma(wK[512:1024, 384:1408], SND[seg]["uu"]), reads=[B_snd[seg]["uu"]], pw=[B_winK[seg]], dma=True)
        else:
            def kown(e):
                return [
                    e.dma_start(out=wK[0:512, 384:1408], in_=gkD[bass.ds(DYN["rk"], 1), :, :].rearrange("o k t -> (o k) t")),
                    e.dma_start(out=wK[512:1024, 384:1408], in_=guu[bass.ds(DYN["rk"], 1), :, :].rearrange("o k t -> (o k) t")),
                ]
            P.add("pool", kown, reads=[B_gth[seg]["kD"], B_gth[seg]["uu"]], pw=[B_winK[seg]], dma=True, ninc=2)

        def vcopies(e):
            return [
                e.dma_start(out=wV[0:384, :], in_=gvD[bass.ds(DYN["lb"], 1), 640:1024, :].rearrange("o k t -> (o k) t")),
                e.dma_start(out=wV[1408:1792, :], in_=gvD[bass.ds(DYN["rb"], 1), 0:384, :].rearrange("o k t -> (o k) t")),
            ]
        P.add("pool", vcopies, reads=[B_gth[seg]["vD"]], writes=[B_winV[seg]], dma=True, ninc=2)
        if fused:
            P.add("sp", dma(wV[384:1408, :], SND[seg]["vD"]), reads=[B_snd[seg]["vD"]], pw=[B_winV[seg]], dma=True)
        else:
            P.add("pool", (lambda e: e.dma_start(out=wV[384:1408, :], in_=gvD[bass.ds(DYN["rk"], 1), :, :].rearrange("o k t -> (o k) t"))),
                  reads=[B_gth[seg]["vD"]], pw=[B_winV[seg]], dma=True)

    def phase2a(l, seg, tg):
        cat = ain
        P.new_phase()
        UW = 542
        uw = a16(0, 4 * UW // 2).rearrange("p (c t) -> p c t", c=4)
        y = a32(1088, 4 * TG).rearrange("p (c t) -> p c t", c=4)
        acc1 = a32(3136, TG)
        zs = a16(3648, 4 * TG // 2).rearrange("p (c t) -> p c t", c=4)
        mean = a32(4672, TG)
        B_uw, B_y, B_acc1, B_zs, B_mean = (P.abuf(n) for n in ("uw", "y", "acc1", "zs", "mean"))
        U0 = 1280

        c0w = 369 if tg == 0 else 881
        P.add("sp", dma(uw, winK[seg][512:1024, c0w:c0w + 542].rearrange("(c p) t -> p c t", p=128)), reads=[B_winK[seg]], writes=[B_uw], dma=True)
        hs = (slice(0, 15), 0) if tg == 0 else (slice(527, 542), 1)
        P.add("dve", (lambda e: e.tensor_scalar_mul(out=uw[:, :, hs[0]], in0=uw[:, :, hs[0]], scalar1=halo[:, hs[1]:hs[1] + 1])),
              reads=[B_uw, B_halo], writes=[B_uw])
        V = vec[l]
        for c in range(4):
            P.add("dve", (lambda e, c=c: e.tensor_scalar_mul(out=y[:, c, :], in0=uw[:, c, 0:TG], scalar1=V[:, 64 + c * 31:65 + c * 31])),
                  reads=[B_uw, B_vec[l]], pw=[B_y])
            P.add("dve", (lambda e, c=c: e.tensor_scalar_mul(out=acc1, in0=uw[:, c, 1:1 + TG], scalar1=V[:, 65 + c * 31:66 + c * 31])),
                  reads=[B_uw, B_vec[l]], writes=[B_acc1])
            for j in range(2, 31):
                dst, bd = (y[:, c, :], B_y) if j % 2 == 0 else (acc1, B_acc1)
                P.add("dve", stt(dst, uw[:, c, j:j + TG], V[:, 64 + c * 31 + j:65 + c * 31 + j], dst, ALU.mult, ALU.add),
                      reads=[B_uw, B_vec[l], bd], pw=[bd])
            P.add("dve", stt(y[:, c, :], acc1, V[:, 188 + c:189 + c], y[:, c, :], ALU.add, ALU.add), reads=[B_acc1, B_y, B_vec[l]], pw=[B_y])
            yb, byb = sqp.next()
            P.add("act", act(yb[:], y[:, c, :], AF.Copy), reads=[B_y], writes=[byb])
            P.add("pe", mm(ps[4][:], [(ones512[:], yb[:])], first=(c == 0), last=(c == 3)), reads=[byb, B_const],
                  writes=[B_ps[4]] if c == 0 else [], pw=[] if c == 0 else [B_ps[4]])
            ysq, bysq = sqp.next()
            P.add("act", act(ysq[:], y[:, c, :], AF.Square), reads=[B_y], writes=[bysq])
            P.add("pe", mm(ps[5][:], [(ones512[:], ysq[:])], first=(c == 0), last=(c == 3)), reads=[bysq, B_const],
                  writes=[B_ps[5]] if c == 0 else [], pw=[] if c == 0 else [B_ps[5]])
        if debug and seg == 0 and tg == 0:
            P.add("sp", dma(dbg_y.rearrange("p (c t) -> p c t", c=4), y), reads=[B_y], pw=[B_dbg], dma=True, sigbuf=B_y)
        P.add("act", act(mean, ps[4][:], AF.Copy), reads=[B_ps[4]], writes=[B_mean])
        var, bvar = a32(5184, TG), P.abuf("cvar")
        P.add("dve", stt(var[:], mean, -1.0, mean, ALU.mult, ALU.mult), reads=[B_mean], writes=[bvar])
        P.add("dve", tt(var[:], var[:], ps[5][:], ALU.add), reads=[bvar, B_ps[5]], writes=[bvar])
        rsqrt(var[:], bvar, var[:], bvar)
        for c in range(4):
            t1, bt1 = f32p.next()
            P.add("dve", tt(t1[:], y[:, c, :], mean, ALU.subtract), reads=[B_y, B_mean], writes=[bt1])
            P.add("dve", tt(t1[:], t1[:], var[:], ALU.mult), reads=[bt1, bvar], writes=[bt1])
            P.add("act", act(zs[:, c, :], t1[:], AF.Silu, scale=V[:, 192 + c:193 + c], bias=V[:, 196 + c:197 + c]),
                  reads=[bt1, B_vec[l]], pw=[B_zs])
        if debug and seg == 0 and tg == 0:
            P.add("sp", dma(dbg_zs.rearrange("p (c t) -> p c t", c=4), zs), reads=[B_zs], pw=[B_dbg], dma=True, sigbuf=B_zs)
        view, bw = load_w([(W[l]["pw"], 0, 512)], 4)
        for oc in range(4):
            pb = gen_ps.next()
            P.add("pe", mm(ps[pb][:], [(view[:, kc, oc * 128:(oc + 1) * 128], zs[:, kc, :]) for kc in range(4)]),
                  reads=[bw, B_zs], writes=[B_ps[pb]])
            P.add("act", act(cat[:, 4 + oc, :], ps[pb][:], AF.Identity, bias=V[:, 200 + oc:201 + oc]), reads=[B_ps[pb], B_vec[l]], pw=[B_ain])

        P.new_phase()
        kq = [a16(r * 512, 512) for r in range(4)]
        vq = [a16(2048 + r * 512, 512).rearrange("p (t d) -> p t d", d=128) for r in range(4)]
        bkq = [P.abuf("kq%d" % r) for r in range(4)]
        bvq = [P.abuf("vq%d" % r) for r in range(4)]
        biasT = a32(4096, 28 * 128).rearrange("p (n q) -> p n q", q=128)
        dm = a16(7680, 4 * 7 * 64).rearrange("p (i j q) -> p i j q", i=4, j=7)
        B_bias = P.abuf("biasT")
        B_dm = P.abuf("dm")
        P.add("sp", dma(biasT, W[l]["bias"].rearrange("p (n q) -> p n q", q=128)), writes=[B_bias], dma=True)
        P.add("pool", dma(dm, dmask_d.rearrange("p (i j q) -> p i j q", i=8, j=7)[:, tg * 4:(tg + 1) * 4]), writes=[B_dm], dma=True)
        st_ring = Ring([0, 1])

        def load_kv(krow0, vcol0):
            for r in range(4):
                kn_, lr_, kn = kloc(krow0)
                vn_, lc_, _vn = vloc(vcol0)
                P.add("sp", dma(kq[r], GTH[seg][kn_][r * kn + lr_:r * kn + lr_ + 128, :]), reads=[B_gth[seg][kn_]], writes=[bkq[r]], dma=True)
                P.add("sp", dma(vq[r], GTH[seg][vn_][r * T:(r + 1) * T, lc_:lc_ + 128].rearrange("(t p) c -> p t c", p=128)),
                      reads=[B_gth[seg][vn_]], writes=[bvq[r]], dma=True)

        def attn(q_ap, kpart, scale, ob, sb_):
            for kt in range(32):
                r, kl = kt // 8, kt % 8
                sbk = st_ring.next()
                P.add("pe", mm(ps[sbk][:], [(kq[r][kpart, kl * 128:(kl + 1) * 128], q_ap)]), reads=[bkq[r], B_q[tg]], writes=[B_ps[sbk]])
                pT, bpt = ptp.next()
                P.add("act", act(pT[:], ps[sbk][:], AF.Exp, scale=scale), reads=[B_ps[sbk]], writes=[bpt])
                w_, p_ = ([B_ps[ob]], []) if kt == 0 else ([], [B_ps[ob]])
                P.add("pe", mm(ps[ob][:], [(vq[r][:, kl, :], pT[:])], first=(kt == 0), last=(kt == 31)), reads=[bvq[r], bpt], writes=w_, pw=p_)
                w_, p_ = ([B_ps[sb_]], []) if kt == 0 else ([], [B_ps[sb_]])
                P.add("pe", mm(ps[sb_][:], [(ones1[:], pT[:])], first=(kt == 0), last=(kt == 31)), reads=[bpt, B_const], writes=w_, pw=p_)

        lv = lamv[l]
        for h in range(4):
            load_kv(h * 128, h * 128)
            for m in range(2):
                kp = slice(m * 64, (m + 1) * 64)
                attn(qT[kp, h, tgs(tg)], kp, 0.125, 2 + m, 4 + m)
            r0, br0 = f32p.next()
            recip(r0[:], br0, ps[4][:], B_ps[4])
            P.add("dve", tt(r0[:], ps[2][:], r0[:], ALU.mult), reads=[B_ps[2], br0], writes=[br0])
            r1, br1 = f32p.next()
            recip(r1[:], br1, ps[5][:], B_ps[5])
            P.add("dve", tt(r1[:], ps[3][:], r1[:], ALU.mult), reads=[B_ps[3], br1], writes=[br1])
            P.add("dve", stt(r1[:], r1[:], lv[:, 4:5], r0[:], ALU.mult, ALU.add), reads=[br0, br1, B_lamv[l]], writes=[br1])
            sq, bsq = sqp.next()
            P.add("act", act(sq[:], r1[:], AF.Square), reads=[br1], writes=[bsq])
            P.add("pe", mm(ps[6][:], [(ones128[:], sq[:])]), reads=[bsq, B_const], writes=[B_ps[6]])
            rs_, brs = f32p.next()
            rsqrt(rs_[:], brs, ps[6][:], B_ps[6])
            P.add("dve", stt(cat[:, h, :], r1[:], lv[:, 5:6], rs_[:], ALU.mult, ALU.mult), reads=[br1, brs, B_lamv[l]], pw=[B_ain])
        sC = 128 ** -0.5
        for g in range(2):
            load_kv(512 + g * 128, 512 + g * 128)
            for rr in range(2):
                hq = 2 * g + rr
                attn(qT[:, 4 + hq, tgs(tg)], slice(0, 128), sC, 2, 4)
                rc, brc = f32p.next()
                recip(rc[:], brc, ps[4][:], B_ps[4])
                P.add("dve", tt(cat[:, 8 + hq, :], ps[2][:], rc[:], ALU.mult), reads=[B_ps[2], brc], pw=[B_ain])
        for h in range(4):
            wK, wV = winK[seg], winV[seg]
            for part, (c_lo, c_hi) in enumerate(((0, 384), (384, 1408), (1408, 1792))):
                n_ = c_hi - c_lo
                P.add("sp", dma(kq[part][:, 0:n_], wK[h * 128:(h + 1) * 128, c_lo:c_hi]), reads=[B_winK[seg]], writes=[bkq[part]], dma=True)
                P.add("sp", dma(vq[part][:, 0:n_ // 128, :], wV[c_lo:c_hi, h * 128:(h + 1) * 128].rearrange("(t p) c -> p t c", p=128)),
                      reads=[B_winV[seg]], writes=[bvq[part]], dma=True)
            def kwin(wt):
                if wt < 3:
                    return kq[0][:, wt * 128:(wt + 1) * 128]
                if wt < 11:
                    return kq[1][:, (wt - 3) * 128:(wt - 2) * 128]
                return kq[2][:, (wt - 11) * 128:(wt - 10) * 128]

            def vwin(wt):
                if wt < 3:
                    return vq[0][:, wt, :]
                if wt < 11:
                    return vq[1][:, wt - 3, :]
                return vq[2][:, wt - 11, :]
            for i in range(4):
                lt = tg * 4 + i
                for j in range(7):
                    wt = lt + j
                    sbk = st_ring.next()
                    P.add("pe", mm(ps[sbk][:, 0:128], [(kwin(wt), qT[:, 8 + h, tg * TG + i * 128:tg * TG + (i + 1) * 128])]),
                          reads=[bkq[0], bkq[1], bkq[2], B_q[tg]], writes=[B_ps[sbk]])
                    t_, bt_ = f32p.next()
                    P.add("dve", stt(t_[:, 0:128], ps[sbk][:, 0:128], sC, biasT[:, h * 7 + j, :], ALU.mult, ALU.add),
                          reads=[B_ps[sbk], B_bias], writes=[bt_])
                    e_, be_ = sqp.next()
                    P.add("act", act(e_[:, 0:128], t_[:, 0:128], AF.Exp), reads=[bt_], writes=[be_])
                    pT, bpt = ptp.next()
                    P.add("dve", tt(pT[:, 0:128], e_[:, 0:128], dm[:, i, j, :], ALU.mult), reads=[be_, B_dm], writes=[bpt])
                    w_, p_ = ([B_ps[3]], []) if j == 0 else ([], [B_ps[3]])
                    P.add("pe", mm(ps[3][:, 0:128], [(vwin(wt), pT[:, 0:128])], first=(j == 0), last=(j == 6)), reads=[bvq[0], bvq[1], bvq[2], bpt], writes=w_, pw=p_)
                    w_, p_ = ([B_ps[5]], []) if j == 0 else ([], [B_ps[5]])
                    P.add("pe", mm(ps[5][:, 0:128], [(ones1[:], pT[:, 0:128])], first=(j == 0), last=(j == 6)), reads=[bpt, B_const], writes=w_, pw=p_)
                rc, brc = f32p.next()
                recip(rc[:, 0:128], brc, ps[5][:, 0:128], B_ps[5])
                P.add("dve", tt(cat[:, 12 + h, i * 128:(i + 1) * 128], ps[3][:, 0:128], rc[:, 0:128], ALU.mult), reads=[B_ps[3], brc], pw=[B_ain])

    def phase2b(l, tg):
        P.new_phase()
        mixed = a32(0, KC * TG).rearrange("p (c t) -> p c t", c=KC)
        B_mixed = P.abuf("mixed")
        cat = ain
        for g in range(8):
            view, bw = load_w([(W[l]["w_out"], g * 256, 256)], KC)
            for j in range(2):
                oc = g * 2 + j
                pb = gen_ps.next()
                P.add("pe", mm(ps[pb][:], [(view[:, kc, j * 128:(j + 1) * 128], cat[:, kc, :]) for kc in range(KC)]),
                      reads=[bw, B_ain], writes=[B_ps[pb]])
                P.add("dve", (lambda e, oc=oc, pb=pb: e.tensor_copy(out=mixed[:, oc, :], in_=ps[pb][:])), reads=[B_ps[pb]], pw=[B_mixed])
                sq, bsq = sqp.next()
                P.add("act", act(sq[:], ps[pb][:], AF.Square), reads=[B_ps[pb]], writes=[bsq])
                P.add("pe", mm(ps[6][:], [(onesD[:], sq[:])], first=(oc == 0), last=(oc == KC - 1)), reads=[bsq, B_const],
                      writes=[B_ps[6]] if oc == 0 else [], pw=[] if oc == 0 else [B_ps[6]])
        rstd, brstd = f32p.next()
        rsqrt(rstd[:], brstd, ps[6][:], B_ps[6])
        for oc in range(KC):
            P.add("dve", stt(mixed[:, oc, :], mixed[:, oc, :], vec[l][:, 16 + oc:17 + oc], rstd[:], ALU.mult, ALU.mult),
                  reads=[B_mixed, brstd, B_vec[l]], pw=[B_mixed])
            P.add("dve", tt(xT[:, oc, tgs(tg)], xT[:, oc, tgs(tg)], mixed[:, oc, :], ALU.add), reads=[B_mixed, B_x[tg]], pw=[B_x[tg]])

    def phase2c(l, tg):
        P.new_phase()
        fT = a32(0, KC * TG).rearrange("p (c t) -> p c t", c=KC)
        actT = a16(8192, 22 * TG // 2).rearrange("p (c t) -> p c t", c=22)
        B_fT = P.abuf("fT")
        B_act = P.abuf("act")
        rmsnorm_to_ain(l, tg, 32)
        for hf in range(2):
            for pr in range(11):
                fl0 = 2 * pr
                fc0 = hf * 22 + fl0
                viewg, bwg = load_w([(W[l]["gate"], fc0 * 128, 256)], KC)
                pg = [gen_ps.next(), gen_ps.next()]
                for j in range(2):
                    P.add("pe", mm(ps[pg[j]][:], [(viewg[:, kc, j * 128:(j + 1) * 128], ain[:, kc, :]) for kc in range(KC)]),
                          reads=[bwg, B_ain], writes=[B_ps[pg[j]]])
                sgs = []
                for j in range(2):
                    sg, bsg = f32p.next()
                    P.add("act", act(sg[:], ps[pg[j]][:], AF.Silu), reads=[B_ps[pg[j]]], writes=[bsg])
                    sgs.append((sg, bsg))
                viewu, bwu = load_w([(W[l]["up"], fc0 * 128, 256)], KC)
                pu = [gen_ps.next(), gen_ps.next()]
                for j in range(2):
                    P.add("pe", mm(ps[pu[j]][:], [(viewu[:, kc, j * 128:(j + 1) * 128], ain[:, kc, :]) for kc in range(KC)]),
                          reads=[bwu, B_ain], writes=[B_ps[pu[j]]])
                for j in range(2):
                    P.add("dve", tt(actT[:, fl0 + j, :], ps[pu[j]][:], sgs[j][0][:], ALU.mult), reads=[B_ps[pu[j]], sgs[j][1]], pw=[B_act])
            for oc in range(KC):
                view, bw = load_w([(W[l]["down"], oc * 128, 128)], 22, krow0=hf * 22 * 128)
                for j in range(1):
                    pb = gen_ps.next()
                    P.add("pe", mm(ps[pb][:], [(view[:, fl, :], actT[:, fl, :]) for fl in range(22)]), reads=[bw, B_act], writes=[B_ps[pb]])
                    if hf == 0:
                        P.add("act", act(fT[:, oc, :], ps[pb][:], AF.Copy), reads=[B_ps[pb]], pw=[B_fT])
                    else:
                        P.add("dve", tt(fT[:, oc, :], fT[:, oc, :], ps[pb][:], ALU.add), reads=[B_ps[pb], B_fT], pw=[B_fT])
                        sq, bsq = sqp.next()
                        P.add("act", act(sq[:], fT[:, oc, :], AF.Square), reads=[B_fT], writes=[bsq])
                        P.add("pe", mm(ps[6][:], [(onesD[:], sq[:])], first=(oc == 0), last=(oc == KC - 1)), reads=[bsq, B_const],
                              writes=[B_ps[6]] if oc == 0 else [], pw=[] if oc == 0 else [B_ps[6]])
        rstd, brstd = f32p.next()
        rsqrt(rstd[:], brstd, ps[6][:], B_ps[6])
        for oc in range(KC):
            P.add("dve", stt(fT[:, oc, :], fT[:, oc, :], vec[l][:, 48 + oc:49 + oc], rstd[:], ALU.mult, ALU.mult),
                  reads=[B_fT, brstd, B_vec[l]], pw=[B_fT])
            P.add("dve", tt(xT[:, oc, tgs(tg)], xT[:, oc, tgs(tg)], fT[:, oc, :], ALU.add), reads=[B_fT, B_x[tg]], pw=[B_x[tg]])

    def dyn_init(e):
        pid = e.partition_id()
        rk = e.snap(pid % 4)
        DYN["rk"] = rk
        DYN["lb"] = e.snap((rk + 3) % 4)
        DYN["rb"] = e.snap((rk + 1) % 4)
        return None
    P.ops["pool"].append(_mk_raw(dyn_init))
    setup_consts()
    if first_seg:
        for tg in range(NTG):
            P.add("sp", dma(xT[:, :, tgs(tg)], xT_d.rearrange("(c p) t -> p c t", p=128)[:, :, tgs(tg)]), writes=[B_x[tg]], dma=True)
    else:
        for tg in range(NTG):
            P.add("sp", dma(xT[:, :, tgs(tg)], stx_in.rearrange("p (c t) -> p c t", c=KC)[:, :, tgs(tg)]), writes=[B_x[tg]], dma=True)
            P.add("sp", dma(qT[:, :, tgs(tg)], stq_in.rearrange("p (c t) -> p c t", c=12)[:, :, tgs(tg)]), writes=[B_q[tg]], dma=True)
    for seg in range(seg_lo, seg_hi + 1):
        if seg >= 1:
            lprev = seg - 1
            make_windows(seg - 1)
            for tg in range(NTG):
                phase2a(lprev, seg - 1, tg)
                if debug and seg == 1:
                    P.add("sp", dma(dbg_cat[tg], ain[:].rearrange("p c t -> p (c t)")), reads=[B_ain], pw=[B_dbg], dma=True, sigbuf=B_ain)
                phase2b(lprev, tg)
                if debug and seg == 1:
                    P.add("sp", dma(dbg_xmid.rearrange("p (c t) -> p c t", c=KC)[:, :, tgs(tg)], xT[:, :, tgs(tg)]), reads=[B_x[tg]], pw=[B_dbg], dma=True, sigbuf=B_x[tg])
                phase2c(lprev, tg)
        if seg <= 1:
            for tg in range(NTG):
                phase1(seg, seg, tg)
            exchange(seg)
    finals = []
    if last_seg:
        for tg in range(NTG):
            P.add("sp", dma(out_d.rearrange("(c p) t -> p c t", p=128)[:, :, tgs(tg)], xT[:, :, tgs(tg)]), reads=[B_x[tg]], pw=[B_out], dma=True, sigbuf=B_x[tg])
        finals = [B_out]
    elif not fused:
        for tg in range(NTG):
            P.add("sp", dma(stx_out.rearrange("p (c t) -> p c t", c=KC)[:, :, tgs(tg)], xT[:, :, tgs(tg)]), reads=[B_x[tg]], pw=[B_st], dma=True, sigbuf=B_x[tg])
            P.add("sp", dma(stq_out.rearrange("p (c t) -> p c t", c=12)[:, :, tgs(tg)], qT[:, :, tgs(tg)]), reads=[B_q[tg]], pw=[B_st], dma=True, sigbuf=B_q[tg])
        finals = [B_st, B_dbg] + list(B_snd[seg_lo].values())
    P.add("sp", None, reads=finals)
    P.finalize(nc, stack)
    stack.close()
    nc._in_names = in_names
    return nc


def _host_consts():
    theta = 10000.0
    inv = np.power(theta, -np.arange(0, 64, 2, dtype=np.float32) / 64).astype(np.float32)
    p = np.arange(128)
    j = p % 32
    sign = np.where((p % 64) < 32, -1.0, 1.0).astype(np.float32)
    tabs, dmasks, halos = [], [], []
    for rank in range(4):
        s = (rank * T + np.arange(T))
        posA = s.astype(np.float32)
        angA = posA[None, :] * inv[j][:, None]
        row = (s // 64).astype(np.float32)
        col = (s % 64).astype(np.float32)
        posC = np.where((p < 64)[:, None], row[None, :], col[None, :]).astype(np.float32)
        angC = posC * inv[j][:, None]
        tab = np.stack([np.cos(angA), np.sin(angA) * sign[:, None], np.cos(angC), np.sin(angC) * sign[:, None]], axis=1)
        tabs.append(np.ascontiguousarray(tab.reshape(128, 4 * T).astype(np.float32)))
        dmk = np.zeros((128, 8, 7, 128), np.float32)
        kk = np.arange(128)
        kr_par, kc = kk // 64, kk % 64
        qq = np.arange(128)
        qr_l, qc = qq // 64, qq % 64
        for lt in range(8):
            b = rank * 8 + lt
            qr = 2 * b + qr_l
            win_r = np.clip(qr - 4, 0, 56)
            win_c = np.clip(qc - 8, 0, 48)
            for jj in range(7):
                kr = 2 * b + 2 * (jj - 3) + kr_par
                ok = ((kr[:, None] >= 0) & (kr[:, None] < 64) & (kr[:, None] >= win_r[None, :]) & (kr[:, None] < win_r[None, :] + 8)
                      & (kc[:, None] >= win_c[None, :]) & (kc[:, None] < win_c[None, :] + 16))
                dmk[:, lt, jj, :] = ok
        dmasks.append(np.ascontiguousarray(dmk.reshape(128, 8 * 7 * 128)))
        hl = np.zeros((128, 2), np.float32)
        hl[:, 0] = 0.0 if rank == 0 else 1.0
        hl[:, 1] = 0.0 if rank == 3 else 1.0
        halos.append(hl)
    m = np.arange(128)
    perm = np.zeros((128, 128), np.float32)
    perm[m ^ 32, m] = 1.0
    return tabs, dmasks, halos, perm


def _layer_inputs(inp, l):
    def pc(v, n):
        return np.ascontiguousarray(np.asarray(v, np.float32).reshape(n, 128).T)
    vec = np.zeros((128, NV), np.float32)
    vec[:, 0:16] = pc(inp["norm_mix_pre"][l], 16)
    vec[:, 16:32] = pc(inp["norm_mix_post"][l], 16)
    vec[:, 32:48] = pc(inp["norm_ffn_pre"][l], 16)
    vec[:, 48:64] = pc(inp["norm_ffn_post"][l], 16)
    dw = np.asarray(inp["conv_dw"][l], np.float32)
    vec[:, 64:188] = dw.reshape(31, 4, 128).transpose(2, 1, 0).reshape(128, 124)
    vec[:, 188:192] = pc(inp["conv_dw_b"][l], 4)
    vec[:, 192:196] = pc(inp["conv_ln_g"][l], 4)
    vec[:, 196:200] = pc(inp["conv_ln_b"][l], 4)
    vec[:, 200:204] = pc(inp["conv_pw_b"][l], 4)
    vec[:, 204] = np.asarray(inp["diff_subln"][l], np.float32)
    vec[:, 205] = np.asarray(inp["gqa_q_norm"][l], np.float32)
    vec[:, 206] = np.asarray(inp["gqa_k_norm"][l], np.float32)
    lamb = np.ascontiguousarray(np.broadcast_to(np.asarray(inp["diff_lambda"][l], np.float32).reshape(1, 256), (128, 256)))
    rpb = np.asarray(inp["na_rpb"][l], np.float32)
    kk = np.arange(128)
    kr_par, kc = kk // 64, kk % 64
    qq = np.arange(128)
    qr_l, qc = qq // 64, qq % 64
    bias = np.zeros((128, 4, 7, 128), np.float32)
    for jj in range(7):
        dr = 2 * (jj - 3) + kr_par[:, None] - qr_l[None, :]
        ir = np.clip(dr + 7, 0, 14)
        ic = np.clip(kc[:, None] - qc[None, :] + 15, 0, 30)
        for h in range(4):
            bias[:, h, jj, :] = rpb[h][ir, ic]
    d = {
        "w_in%d" % l: np.ascontiguousarray(inp["w_in"][l], np.float32), "w_out%d" % l: np.ascontiguousarray(inp["w_out"][l], np.float32),
        "gate%d" % l: np.ascontiguousarray(inp["ffn_gate"][l], np.float32), "up%d" % l: np.ascontiguousarray(inp["ffn_up"][l], np.float32),
        "down%d" % l: np.ascontiguousarray(inp["ffn_down"][l], np.float32), "pw%d" % l: np.ascontiguousarray(inp["conv_pw"][l], np.float32),
        "vec%d" % l: vec, "lamb%d" % l: lamb, "bias%d" % l: np.ascontiguousarray(bias.reshape(128, 28 * 128)),
    }
    return d


_NC_CACHE = {}


def _get_nc(lo, hi, fused):
    key = (lo, hi, fused)
    if key not in _NC_CACHE:
        _NC_CACHE[key] = build_program(lo, hi, fused)
    return _NC_CACHE[key]


def kernel(**inp):
    inp = {k: np.asarray(v) for k, v in inp.items()}
    x = inp["x"].astype(np.float32, copy=False)
    tabs, dmasks, halos, perm = _host_consts()
    lay = {l: _layer_inputs(inp, l) for l in range(L)}
    cores = list(range(8))

    def common(c):
        r = c % 4
        return {"tabs": tabs[r], "dmask": dmasks[r], "halo": halos[r], "perm": perm}

    def xT_of(c):
        b, r = c // 4, c % 4
        return np.ascontiguousarray(x[b, r * T:(r + 1) * T, :].T)

    if FUSED:
        nc = _get_nc(0, 2, True)
        maps = []
        for c in cores:
            m = common(c)
            m.update(lay[0])
            m.update(lay[1])
            m["xT"] = xT_of(c)
            maps.append(m)
        maps = [{k: m[k] for k in nc._in_names} for m in maps]
        res = run_bass_kernel_spmd(nc, maps, core_ids=cores)
        outs = [np.asarray(res.results[c]["outT"]) for c in cores]
    else:
        state = None
        outs = None
        for seg in range(3):
            nc = _get_nc(seg, seg, False)
            maps = []
            for c in cores:
                m = common(c)
                for l in sorted({0 if seg <= 1 else 1, 1 if seg >= 1 else 0}):
                    m.update(lay[l])
                if seg == 0:
                    m["xT"] = xT_of(c)
                else:
                    g0 = (c // 4) * 4
                    m["stx_in"] = state[c]["stx_out"]
                    m["stq_in"] = state[c]["stq_out"]
                    for n_ in ("kA", "kC", "kD", "uu", "vA", "vC", "vD"):
                        m["gth_" + n_] = np.concatenate([state[g0 + r]["snd_" + n_] for r in range(4)], axis=0)
                maps.append(m)
            maps = [{k: m[k] for k in nc._in_names} for m in maps]
            res = run_bass_kernel_spmd(nc, maps, core_ids=cores)
            if seg < 2:
                state = [{k: np.asarray(v) for k, v in res.results[c].items()} for c in cores]
            else:
                outs = [np.asarray(res.results[c]["outT"]) for c in cores]
    out = np.zeros((2, 4096, D), np.float32)
    for c in cores:
        b, r = c // 4, c % 4
        out[b, r * T:(r + 1) * T, :] = outs[c].T
    return out
```

```python
import contextlib
import numpy as np
import ml_dtypes
import concourse.bass as bass
import concourse.mybir as mybir
from concourse.bass_utils import run_bass_kernel_spmd

F32 = mybir.dt.float32
BF16 = mybir.dt.bfloat16
AF = mybir.ActivationFunctionType
ALU = mybir.AluOpType

L = 2
D = 2048
KC = 16
T = 1024
TG = 512
NTG = 2
FF = 5632
FC = 44
INC = 5120
EPS = 1e-6
NV = 208
KROWS = 1792
VCOLS = 1280
ENGS = ("pe", "act", "dve", "pool", "sp")
FUSED = True
KCUT = 0


class Buf:
    __slots__ = ("name", "ws", "rs", "sem", "cnt", "arena", "psum")

    def __init__(self, name, arena=False, psum=False):
        self.psum = psum
        self.name = name
        self.ws = {}
        self.rs = {}
        self.sem = None
        self.cnt = 0
        self.arena = arena


class Op:
    __slots__ = ("eng", "fn", "deps", "dma", "cc", "sig", "val", "dbuf", "ninc")


class Prog:
    def __init__(self):
        self.ops = {e: [] for e in ENGS}
        self.last_compute = {}
        self.arena_dmas = []
        self.fence_ops = []
        self.dma_bufs = []
        self.cc_cnt = 0

    def add(self, eng, fn, reads=(), writes=(), pw=(), dma=False, cc=False, ninc=1, sigbuf=None):
        op = Op()
        op.eng = eng
        op.fn = fn
        op.dma = dma
        op.cc = cc
        op.sig = False
        op.val = None
        op.dbuf = None
        op.ninc = ninc
        deps = {}
        for b in reads:
            for w in b.ws.values():
                deps[w] = True
            if b.psum:
                for r in b.rs.values():
                    if r.eng != eng and r not in deps:
                        deps[r] = False
        for b in list(writes) + list(pw):
            for r in b.rs.values():
                if r not in deps:
                    deps[r] = False
        for b in writes:
            for w in b.ws.values():
                if w not in deps:
                    deps[w] = False
        op.deps = deps
        if dma:
            db = sigbuf if sigbuf is not None else (list(writes) + list(pw))[0]
            if db.sem is None:
                self.dma_bufs.append(db)
                db.sem = True
            db.cnt += 16 * ninc
            op.val = db.cnt
            op.dbuf = db
            if any(b.arena for b in list(reads) + list(writes) + list(pw)):
                self.arena_dmas.append(op)
        elif cc:
            self.cc_cnt += 1
            op.val = self.cc_cnt
        else:
            self.last_compute[eng] = op
        wkey = ("dma", id(op.dbuf)) if dma else ("cc" if cc else eng)
        for b in reads:
            if dma or cc:
                b.rs[("dma", id(op.dbuf) if dma else "cc")] = op
            else:
                b.rs[eng] = op
        for b in writes:
            b.ws = {wkey: op}
            b.rs = {}
        for b in pw:
            b.ws[wkey] = op
        self.ops[eng].append(op)
        return op

    def new_phase(self):
        self.fence_ops = list(self.last_compute.values()) + list(self.arena_dmas)
        self.arena_dmas = []

    def abuf(self, name):
        if not hasattr(self, "_ab"):
            self._ab = {}
        if name not in self._ab:
            self._ab[name] = Buf(name, arena=True)
        return self.fence(self._ab[name])

    def fence(self, buf):
        buf.ws = {}
        buf.rs = {("f", i): o for i, o in enumerate(self.fence_ops)}
        return buf

    def finalize(self, nc, stack):
        engsem = {e: stack.enter_context(nc.semaphore("s_" + e)) for e in ENGS}
        ccsem = stack.enter_context(nc.semaphore("s_cc"))
        for i, b in enumerate(self.dma_bufs):
            b.sem = stack.enter_context(nc.semaphore("d%d" % i))
        for e in ENGS:
            for op in self.ops[e]:
                for d, raw in op.deps.items():
                    if d.dma or d.cc:
                        continue
                    if d.eng == op.eng and not op.dma and not op.cc:
                        if e == "pe":
                            continue
                    d.sig = True
        for e in ENGS:
            c = 0
            for op in self.ops[e]:
                if not op.dma and not op.cc and op.sig:
                    c += 1
                    op.val = c
        block = stack.enter_context(nc.Block())

        def emit(e, eng):
            waited = {}
            for op in self.ops[e]:
                for d, raw in op.deps.items():
                    if d.dma:
                        sem, key, v = d.dbuf.sem, ("d", id(d.dbuf)), d.val
                    elif d.cc:
                        sem, key, v = ccsem, "cc", d.val
                    else:
                        if d.eng == e and not op.dma and not op.cc and e == "pe":
                            continue
                        sem, key, v = engsem[d.eng], d.eng, d.val
                    if waited.get(key, 0) >= v:
                        continue
                    waited[key] = v
                    eng.wait_ge(sem, v)
                if op.fn is None:
                    continue
                res = op.fn(eng)
                if op.dma:
                    if not isinstance(res, (list, tuple)):
                        res = [res]
                    assert len(res) == op.ninc
                    for r in res:
                        r.then_inc(op.dbuf.sem, 16)
                elif op.cc:
                    res.then_inc(ccsem, 1)
                elif op.sig:
                    res.then_inc(engsem[e], 1)

        @block.tensor
        def _(eng):
            emit("pe", eng)

        @block.scalar
        def _(eng):
            emit("act", eng)

        @block.vector
        def _(eng):
            emit("dve", eng)

        @block.gpsimd
        def _(eng):
            emit("pool", eng)

        @block.sync
        def _(eng):
            emit("sp", eng)


def _mk_raw(fn):
    op = Op()
    op.eng = "pool"
    op.fn = fn
    op.deps = {}
    op.dma = False
    op.cc = False
    op.sig = False
    op.val = None
    op.dbuf = None
    op.ninc = 0
    return op


class Ring:
    def __init__(self, items):
        self.items = items
        self.i = 0

    def next(self):
        it = self.items[self.i % len(self.items)]
        self.i += 1
        return it


def build_program(seg_lo, seg_hi, fused, debug=False):
    nc = bass.Bass("TRN2", target_bir_lowering=False)
    P = Prog()
    DYN = {}
    stack = contextlib.ExitStack()
    first_seg, last_seg = seg_lo == 0, seg_hi == 2
    layers = sorted({0 if s <= 1 else 1 for s in range(seg_lo, seg_hi + 1)} | {1 if s >= 1 else 0 for s in range(seg_lo, seg_hi + 1)})

    in_names = []

    def din(name, shape, dt=F32):
        in_names.append(name)
        return nc.dram_tensor(name, list(shape), dt, kind="ExternalInput").ap()

    def dout(name, shape, dt=F32):
        return nc.dram_tensor(name, list(shape), dt, kind="ExternalOutput").ap()

    class _LazyW(dict):
        def __init__(self, l):
            super().__init__()
            self.l = l

        def __missing__(self, key):
            shp = {"w_in": [D, INC], "w_out": [D, D], "gate": [D, FF], "up": [D, FF], "down": [FF, D], "pw": [512, 512],
                   "vec": [128, NV], "lamb": [128, 256], "bias": [128, 28 * 128]}[key]
            v = din("%s%d" % (key, self.l), shp)
            self[key] = v
            return v
    W = {l: _LazyW(l) for l in layers}
    tabs_d = din("tabs", [128, 4 * T])
    dmask_d = din("dmask", [128, 8 * 7 * 128])
    perm_d = din("perm", [128, 128])
    halo_d = din("halo", [128, 2])
    if first_seg:
        xT_d = din("xT", [D, T])
    else:
        stx_in = din("stx_in", [128, KC * T])
        stq_in = din("stq_in", [128, 12 * T], BF16)
    if last_seg:
        out_d = dout("outT", [D, T])
    KSEG = [("kA", 0, 512), ("kC", 512, 256), ("kD", 768, 512), ("uu", 1280, 512)]
    VSEG = [("vA", 0, 512), ("vC", 512, 256), ("vD", 768, 512)]
    NAMES = [k[0] for k in KSEG] + [v[0] for v in VSEG]

    def kloc(row):
        for name, r0, n in KSEG:
            if r0 <= row < r0 + n:
                return name, row - r0, n
        raise ValueError(row)

    def vloc(col):
        for name, c0, n in VSEG:
            if c0 <= col < c0 + n:
                return name, col - c0, n
        raise ValueError(col)

    def snd_shape(name):
        for nm, _, n in KSEG:
            if nm == name:
                return [n, T]
        for nm, _, n in VSEG:
            if nm == name:
                return [T, n]

    def gth_shape(name):
        sh = snd_shape(name)
        return [4 * sh[0], sh[1]]
    SND, GTH, SNDH, GTHH = {}, {}, {}, {}
    if not fused:
        if not last_seg:
            stx_out = dout("stx_out", [128, KC * T])
            stq_out = dout("stq_out", [128, 12 * T], BF16)
            SND[seg_lo] = {n_: dout("snd_" + n_, snd_shape(n_), BF16) for n_ in NAMES}
        if not first_seg:
            GTH[seg_lo - 1] = {n_: din("gth_" + n_, gth_shape(n_), BF16) for n_ in NAMES}
    else:
        for s_ in (0, 1):
            SNDH[s_] = {n_: nc.dram_tensor("snd%d_%s" % (s_, n_), snd_shape(n_), BF16) for n_ in NAMES}
            GTHH[s_] = {n_: nc.dram_tensor("gth%d_%s" % (s_, n_), gth_shape(n_), BF16) for n_ in NAMES}
            SND[s_] = {n_: SNDH[s_][n_].ap() for n_ in NAMES}
            GTH[s_] = {n_: GTHH[s_][n_].ap() for n_ in NAMES}
    B_snd = {s_: {n_: Buf("snd%d%s" % (s_, n_)) for n_ in NAMES} for s_ in (0, 1)}
    B_gth = {s_: {n_: Buf("gth%d%s" % (s_, n_)) for n_ in NAMES} for s_ in (0, 1)}
    B_out = Buf("out")
    B_st = Buf("stout")
    if debug:
        dbg_cat = dout("dbg_cat", [NTG, 128, KC * TG], BF16)
        dbg_xmid = dout("dbg_xmid", [128, KC * T])
        dbg_y = dout("dbg_y", [128, 4 * TG])
        dbg_zs = dout("dbg_zs", [128, 4 * TG], BF16)
    B_dbg = Buf("dbg")
    winK = {s_: nc.dram_tensor("winK%d" % s_, [1024, 1792], BF16) for s_ in (0, 1)}
    winV = {s_: nc.dram_tensor("winV%d" % s_, [1792, 512], BF16) for s_ in (0, 1)}
    B_winK = {s_: Buf("winK%d" % s_) for s_ in (0, 1)}
    B_winV = {s_: Buf("winV%d" % s_) for s_ in (0, 1)}

    def sb(name, shape, dt):
        return stack.enter_context(nc.sbuf_tensor("t_" + name, list(shape), dt))

    xT = sb("xT", [128, KC, T], F32)
    qT = sb("qT", [128, 12, T], BF16)
    ain = sb("ain", [128, KC, TG], BF16)
    B_x = [Buf("x%d" % i) for i in range(NTG)]
    B_q = [Buf("q%d" % i) for i in range(NTG)]
    B_ain = Buf("ain")
    wbufs = Ring([(sb("wb%d" % i, [128, 4096], BF16), Buf("wb%d" % i)) for i in range(3)])
    ptp = Ring([(sb("pt%d" % i, [128, TG], BF16), Buf("pt%d" % i)) for i in range(3)])
    sqp = Ring([(sb("sq%d" % i, [128, TG], BF16), Buf("sq%d" % i)) for i in range(3)])
    f32p = Ring([(sb("f%d" % i, [128, TG], F32), Buf("f%d" % i)) for i in range(4)])
    stgp = Ring([(sb("stg%d" % i, [128, TG], BF16), Buf("stg%d" % i)) for i in range(2)])
    vec = {l: sb("vec%d" % l, [128, NV], F32) for l in layers}
    lamb = {l: sb("lamb%d" % l, [128, 256], F32) for l in layers}
    lamv = {l: sb("lamv%d" % l, [128, 8], F32) for l in layers}
    B_vec = {l: Buf("vec%d" % l) for l in layers}
    B_lamv = {l: Buf("lamv%d" % l) for l in layers}
    perm = sb("perm", [128, 128], BF16)
    onesD = sb("onesD", [128, 128], BF16)
    ones128 = sb("ones128", [128, 128], BF16)
    ones512 = sb("ones512", [128, 128], BF16)
    ones1 = sb("ones1", [128, 128], BF16)
    halo = sb("halo", [128, 2], F32)
    epsc = sb("epsc", [128, 1], F32)
    B_const = Buf("const")
    B_perm = Buf("perm")
    B_halo = Buf("halo")
    ARENA_F32 = 13824
    arena = sb("arena", [128, ARENA_F32], F32)

    def a32(off, n):
        return arena[:, off:off + n]

    def a16(off, n):
        return arena[:, off:off + n].bitcast(BF16)

    ps = [stack.enter_context(nc.psum_tensor("ps%d" % i, [128, 512], F32)) for i in range(8)]
    B_ps = [Buf("ps%d" % i, psum=True) for i in range(8)]

    def mm(out, pairs, first=True, last=True):
        def fn(e):
            n = len(pairs)
            ins = None
            for i, (l_, r_) in enumerate(pairs):
                ins = e.matmul(out, l_, r_, start=(first and i == 0), stop=(last and i == n - 1))
            return ins
        return fn

    def dma(out, in_):
        return lambda e: e.dma_start(out=out, in_=in_)

    def act(out, in_, func, scale=None, bias=None):
        kw = {}
        if scale is not None:
            kw["scale"] = scale
        if bias is not None:
            kw["bias"] = bias
        return lambda e: e.activation(out=out, in_=in_, func=func, **kw)

    def tt(out, a, b, op):
        return lambda e: e.tensor_tensor(out=out, in0=a, in1=b, op=op)

    def ts(out, a, s1, s2, op0, op1):
        return lambda e: e.tensor_scalar(out=out, in0=a, scalar1=s1, scalar2=s2, op0=op0, op1=op1)

    def stt(out, a, s, b, op0, op1):
        return lambda e: e.scalar_tensor_tensor(out=out, in0=a, scalar=s, in1=b, op0=op0, op1=op1)

    def tgs(tg):
        return slice(tg * TG, (tg + 1) * TG)

    def recip(out, bout, in_, bin_):
        P.add("act", act(out, in_, AF.Ln), reads=[bin_], writes=[bout])
        P.add("act", act(out, out, AF.Exp, scale=-1.0), reads=[bout], writes=[bout])

    def rsqrt(out, bout, in_, bin_):
        P.add("act", act(out, in_, AF.Ln, bias=epsc[:, 0:1]), reads=[bin_, B_const], writes=[bout])
        P.add("act", act(out, out, AF.Exp, scale=-0.5), reads=[bout], writes=[bout])

    def setup_consts():
        for t_, v in ((onesD, 1.0 / D), (ones128, 1.0 / 128), (ones512, 1.0 / 512), (ones1, 1.0)):
            P.add("dve", (lambda e, t_=t_, v=v: e.memset(t_[:], v)), pw=[B_const])
        P.add("dve", (lambda e: e.memset(epsc[:], EPS)), pw=[B_const])
        P.add("pool", dma(perm[:], perm_d), writes=[B_perm], dma=True)
        P.add("sp", dma(halo[:], halo_d), writes=[B_halo], dma=True)
        for l in layers:
            P.add("sp", dma(vec[l][:], W[l]["vec"]), writes=[B_vec[l]], dma=True)
            bl = Buf("lamb")
            P.add("sp", dma(lamb[l][:], W[l]["lamb"]), writes=[bl], dma=True)
            lv = lamv[l]
            lam_init = 0.8 - 0.6 * float(np.exp(-0.3 * l))
            tmpf, btmp = f32p.next()
            P.add("dve", tt(tmpf[:, 0:64], lamb[l][:, 0:64], lamb[l][:, 64:128], ALU.mult), reads=[bl], writes=[btmp])
            P.add("dve", (lambda e, lv=lv, tmpf=tmpf: e.reduce_sum(out=lv[:, 0:1], in_=tmpf[:, 0:64], axis=mybir.AxisListType.X)),
                  reads=[btmp], pw=[B_lamv[l]])
            tmpf2, btmp2 = f32p.next()
            P.add("dve", tt(tmpf2[:, 0:64], lamb[l][:, 128:192], lamb[l][:, 192:256], ALU.mult), reads=[bl], writes=[btmp2])
            P.add("dve", (lambda e, lv=lv, tmpf2=tmpf2: e.reduce_sum(out=lv[:, 1:2], in_=tmpf2[:, 0:64], axis=mybir.AxisListType.X)),
                  reads=[btmp2], pw=[B_lamv[l]])
            P.add("act", act(lv[:, 2:4], lv[:, 0:2], AF.Exp), reads=[B_lamv[l]], pw=[B_lamv[l]])
            P.add("dve", stt(lv[:, 4:5], lv[:, 3:4], -lam_init, lv[:, 2:3], ALU.add, ALU.subtract), reads=[B_lamv[l]], pw=[B_lamv[l]])
            P.add("dve", (lambda e, lv=lv, l=l, li=lam_init: e.tensor_scalar_mul(out=lv[:, 5:6], in0=vec[l][:, 204:205], scalar1=1.0 - li)),
                  reads=[B_vec[l]], pw=[B_lamv[l]])

    def rmsnorm_to_ain(l, tg, gcol):
        for c in range(KC):
            sq, bsq = sqp.next()
            P.add("act", act(sq[:], xT[:, c, tgs(tg)], AF.Square), reads=[B_x[tg]], writes=[bsq])
            P.add("pe", mm(ps[6][:], [(onesD[:], sq[:])], first=(c == 0), last=(c == KC - 1)), reads=[bsq, B_const],
                  writes=[B_ps[6]] if c == 0 else [], pw=[] if c == 0 else [B_ps[6]])
        rstd, brstd = f32p.next()
        rsqrt(rstd[:], brstd, ps[6][:], B_ps[6])
        for c in range(KC):
            P.add("dve", stt(ain[:, c, :], xT[:, c, tgs(tg)], vec[l][:, gcol + c:gcol + c + 1], rstd[:], ALU.mult, ALU.mult),
                  reads=[B_x[tg], brstd, B_vec[l]], pw=[B_ain])

    def load_w(segs, kchunks, krow0=0):
        wt, bw = wbufs.next()
        tot = sum(s[2] for s in segs)
        assert kchunks * tot <= 4096
        view = wt[:, 0:kchunks * tot].rearrange("p (k c) -> p k c", c=tot)
        off = 0
        for i, (Wd, c0, n) in enumerate(segs):
            src = Wd[krow0:krow0 + kchunks * 128, c0:c0 + n].rearrange("(k p) c -> p k c", p=128)
            P.add("pool", dma(view[:, :, off:off + n], src), writes=[bw] if i == 0 else [], pw=[] if i == 0 else [bw], dma=True)
            off += n
        return view, bw

    gen_ps = Ring([0, 1, 2, 3])

    def phase1(l, seg, tg):
        P.new_phase()
        tabs = a32(0, 4 * TG).rearrange("p (f t) -> p f t", f=4)
        B_tabs = P.abuf("tabs")
        P.add("sp", dma(tabs, tabs_d.rearrange("p (f t) -> p f t", f=4)[:, :, tgs(tg)]), writes=[B_tabs], dma=True)
        if KCUT == 1:
            return
        rmsnorm_to_ain(l, tg, 0)
        if KCUT == 2:
            return
        w_in = W[l]["w_in"]

        def rope(psb, kind, dest, dest_bufs_pw, gcol=None):
            if kind == "plain":
                P.add("act", act(dest, ps[psb][:], AF.Copy), reads=[B_ps[psb]], pw=dest_bufs_pw)
                return
            ci, si = (0, 1) if kind == "A" else (2, 3)
            if kind == "C":
                sq, bsq = sqp.next()
                P.add("act", act(sq[:], ps[psb][:], AF.Square), reads=[B_ps[psb]], writes=[bsq])
                P.add("pe", mm(ps[7][:], [(ones128[:], sq[:])]), reads=[bsq, B_const], writes=[B_ps[7]])
                rstd2, brstd2 = f32p.next()
                rsqrt(rstd2[:], brstd2, ps[7][:], B_ps[7])
            xb, bxb = sqp.next()
            if kind == "C":
                P.add("act", act(xb[:], ps[psb][:], AF.Copy, scale=vec[l][:, gcol:gcol + 1]), reads=[B_ps[psb], B_vec[l]], writes=[bxb])
            else:
                P.add("act", act(xb[:], ps[psb][:], AF.Copy), reads=[B_ps[psb]], writes=[bxb])
            P.add("pe", mm(ps[5][:], [(perm[:], xb[:])]), reads=[bxb, B_perm], writes=[B_ps[5]])
            t1, bt1 = f32p.next()
            if kind == "C":
                P.add("dve", stt(t1[:], ps[psb][:], vec[l][:, gcol:gcol + 1], tabs[:, ci, :], ALU.mult, ALU.mult),
                      reads=[B_ps[psb], B_vec[l], B_tabs], writes=[bt1])
            else:
                P.add("dve", tt(t1[:], ps[psb][:], tabs[:, ci, :], ALU.mult), reads=[B_ps[psb], B_tabs], writes=[bt1])
            t2, bt2 = f32p.next()
            P.add("dve", tt(t2[:], ps[5][:], tabs[:, si, :], ALU.mult), reads=[B_ps[5], B_tabs], writes=[bt2])
            if kind == "C":
                P.add("dve", tt(t1[:], t1[:], t2[:], ALU.add), reads=[bt1, bt2], writes=[bt1])
                P.add("dve", tt(dest, t1[:], rstd2[:], ALU.mult), reads=[bt1, brstd2], pw=dest_bufs_pw)
            else:
                P.add("dve", tt(dest, t1[:], t2[:], ALU.add), reads=[bt1, bt2], pw=dest_bufs_pw)

        def to_sendK(kind, psb, row0, gcol=None):
            stg, bstg = stgp.next()
            rope(psb, kind, stg[:], [bstg], gcol)
            nm_, lr_, _n = kloc(row0)
            P.add("sp", dma(SND[seg][nm_][lr_:lr_ + 128, tgs(tg)], stg[:]), reads=[bstg], pw=[B_snd[seg][nm_]], dma=True, sigbuf=bstg)

        groups = [
            ((0, 1), ("qA", 0)), ((2, 3), ("qA", 2)), ((4, 5), ("kA", 0)), ((6, 7), ("kA", 256)),
            ((12, 16), ("glu", 0)), ((13, 17), ("glu", 1)), ((14, 18), ("glu", 2)), ((15, 19), ("glu", 3)),
            ((20, 21), ("qC", 4)), ((22, 23), ("qC", 6)), ((24, 25), ("kC", 512)),
            ((28, 29), ("qD", 8)), ((30, 31), ("qD", 10)), ((32, 33), ("kD", 768)), ((34, 35), ("kD", 1024)),
        ]
        if KCUT == 3:
            groups = groups[:1]
        if KCUT == 4:
            groups = groups[:5]
        for (c0, c1), (kind, arg) in groups:
            if c1 == c0 + 1:
                view, bw = load_w([(w_in, c0 * 128, 256)], KC)
            else:
                view, bw = load_w([(w_in, c0 * 128, 128), (w_in, c1 * 128, 128)], KC)
            banks = []
            for j in range(2):
                pb = gen_ps.next()
                banks.append(pb)
                P.add("pe", mm(ps[pb][:], [(view[:, kc, j * 128:(j + 1) * 128], ain[:, kc, :]) for kc in range(KC)]),
                      reads=[bw, B_ain], writes=[B_ps[pb]])
            if kind == "glu":
                sg, bsg = f32p.next()
                P.add("act", act(sg[:], ps[banks[1]][:], AF.Sigmoid), reads=[B_ps[banks[1]]], writes=[bsg])
                stg, bstg = stgp.next()
                P.add("dve", tt(stg[:], ps[banks[0]][:], sg[:], ALU.mult), reads=[B_ps[banks[0]], bsg], writes=[bstg])
                P.add("sp", dma(SND[seg]["uu"][arg * 128:(arg + 1) * 128, tgs(tg)], stg[:]), reads=[bstg], pw=[B_snd[seg]["uu"]], dma=True, sigbuf=bstg)
                continue
            for j in range(2):
                pb = banks[j]
                if kind == "qA":
                    rope(pb, "A", qT[:, arg + j, tgs(tg)], [B_q[tg]])
                elif kind == "qC":
                    rope(pb, "C", qT[:, arg + j, tgs(tg)], [B_q[tg]], gcol=205)
                elif kind == "qD":
                    rope(pb, "plain", qT[:, arg + j, tgs(tg)], [B_q[tg]])
                elif kind == "kA":
                    to_sendK("A", pb, arg + j * 128)
                elif kind == "kC":
                    to_sendK("C", pb, arg + j * 128, gcol=206)
                elif kind == "kD":
                    to_sendK("plain", pb, arg + j * 128)
        if KCUT in (3, 4, 5):
            return
        for wc0, vc0 in ((1024, 0), (1280, 256), (3328, 512), (4608, 768), (4864, 1024)):
            view, bw = load_w([(w_in, wc0, 256)], KC)
            for t4 in range(4):
                pb = gen_ps.next()
                P.add("pe", mm(ps[pb][:, 0:256], [(ain[:, kc, t4 * 128:(t4 + 1) * 128], view[:, kc, :]) for kc in range(KC)]),
                      reads=[bw, B_ain], writes=[B_ps[pb]])
                stg, bstg = stgp.next()
                P.add("act", act(stg[:, 0:256], ps[pb][:, 0:256], AF.Copy), reads=[B_ps[pb]], writes=[bstg])
                r0 = tg * TG + t4 * 128
                nm_, lc_, _n = vloc(vc0)
                P.add("sp", dma(SND[seg][nm_][r0:r0 + 128, lc_:lc_ + 256], stg[:, 0:256]), reads=[bstg], pw=[B_snd[seg][nm_]], dma=True, sigbuf=bstg)

    def exchange(seg):
        if not fused:
            return
        for n_ in NAMES:
            P.add("pool", (lambda e, n_=n_: e.collective_compute("AllGather", ALU.bypass, replica_groups=[[0, 1, 2, 3], [4, 5, 6, 7]],
                                                                 ins=[SNDH[seg][n_].ap().opt()], outs=[GTHH[seg][n_].ap().opt()])),
                  reads=[B_snd[seg][n_]], writes=[B_gth[seg][n_]], cc=True)

    def make_windows(seg):
        wK, wV = winK[seg], winV[seg]
        gkD = GTH[seg]["kD"].rearrange("(r k) t -> r k t", r=4)
        guu = GTH[seg]["uu"].rearrange("(r k) t -> r k t", r=4)
        gvD = GTH[seg]["vD"].rearrange("(r k) t -> r k t", r=4)

        def kcopies(e):
            return [
                e.dma_start(out=wK[0:512, 0:384], in_=gkD[bass.ds(DYN["lb"], 1), :, 640:1024].rearrange("o k t -> (o k) t")),
                e.dma_start(out=wK[0:512, 1408:1792], in_=gkD[bass.ds(DYN["rb"], 1), :, 0:384].rearrange("o k t -> (o k) t")),
                e.dma_start(out=wK[512:1024, 0:384], in_=guu[bass.ds(DYN["lb"], 1), :, 640:1024].rearrange("o k t -> (o k) t")),
                e.dma_start(out=wK[512:1024, 1408:1792], in_=guu[bass.ds(DYN["rb"], 1), :, 0:384].rearrange("o k t -> (o k) t")),
            ]
        P.add("pool", kcopies, reads=[B_gth[seg]["kD"], B_gth[seg]["uu"]], writes=[B_winK[seg]], dma=True, ninc=4)
        if fused:
            P.add("sp", dma(wK[0:512, 384:1408], SND[seg]["kD"]), reads=[B_snd[seg]["kD"]], pw=[B_winK[seg]], dma=True)
            P.add("sp", dma(wK[512:1024, 384:1408], SND[seg]["uu"]), reads=[B_snd[seg]["uu"]], pw=[B_winK[seg]], dma=True)
        else:
            def kown(e):
                return [
                    e.dma_start(out=wK[0:512, 384:1408], in_=gkD[bass.ds(DYN["rk"], 1), :, :].rearrange("o k t -> (o k) t")),
                    e.dma_start(out=wK[512:1024, 384:1408], in_=guu[bass.ds(DYN["rk"], 1), :, :].rearrange("o k t -> (o k) t")),
                ]
            P.add("pool", kown, reads=[B_gth[seg]["kD"], B_gth[seg]["uu"]], pw=[B_winK[seg]], dma=True, ninc=2)

        def vcopies(e):
            return [
                e.dma_start(out=wV[0:384, :], in_=gvD[bass.ds(DYN["lb"], 1), 640:1024, :].rearrange("o k t -> (o k) t")),
                e.dma_start(out=wV[1408:1792, :], in_=gvD[bass.ds(DYN["rb"], 1), 0:384, :].rearrange("o k t -> (o k) t")),
            ]
        P.add("pool", vcopies, reads=[B_gth[seg]["vD"]], writes=[B_winV[seg]], dma=True, ninc=2)
        if fused:
            P.add("sp", dma(wV[384:1408, :], SND[seg]["vD"]), reads=[B_snd[seg]["vD"]], pw=[B_winV[seg]], dma=True)
        else:
            P.add("pool", (lambda e: e.dma_start(out=wV[384:1408, :], in_=gvD[bass.ds(DYN["rk"], 1), :, :].rearrange("o k t -> (o k) t"))),
                  reads=[B_gth[seg]["vD"]], pw=[B_winV[seg]], dma=True)

    def phase2a(l, seg, tg):
        cat = ain
        P.new_phase()
        UW = 542
        uw = a16(0, 4 * UW // 2).rearrange("p (c t) -> p c t", c=4)
        y = a32(1088, 4 * TG).rearrange("p (c t) -> p c t", c=4)
        acc1 = a32(3136, TG)
        zs = a16(3648, 4 * TG // 2).rearrange("p (c t) -> p c t", c=4)
        mean = a32(4672, TG)
        B_uw, B_y, B_acc1, B_zs, B_mean = (P.abuf(n) for n in ("uw", "y", "acc1", "zs", "mean"))
        U0 = 1280

        c0w = 369 if tg == 0 else 881
        P.add("sp", dma(uw, winK[seg][512:1024, c0w:c0w + 542].rearrange("(c p) t -> p c t", p=128)), reads=[B_winK[seg]], writes=[B_uw], dma=True)
        hs = (slice(0, 15), 0) if tg == 0 else (slice(527, 542), 1)
        P.add("dve", (lambda e: e.tensor_scalar_mul(out=uw[:, :, hs[0]], in0=uw[:, :, hs[0]], scalar1=halo[:, hs[1]:hs[1] + 1])),
              reads=[B_uw, B_halo], writes=[B_uw])
        V = vec[l]
        for c in range(4):
            P.add("dve", (lambda e, c=c: e.tensor_scalar_mul(out=y[:, c, :], in0=uw[:, c, 0:TG], scalar1=V[:, 64 + c * 31:65 + c * 31])),
                  reads=[B_uw, B_vec[l]], pw=[B_y])
            P.add("dve", (lambda e, c=c: e.tensor_scalar_mul(out=acc1, in0=uw[:, c, 1:1 + TG], scalar1=V[:, 65 + c * 31:66 + c * 31])),
                  reads=[B_uw, B_vec[l]], writes=[B_acc1])
            for j in range(2, 31):
                dst, bd = (y[:, c, :], B_y) if j % 2 == 0 else (acc1, B_acc1)
                P.add("dve", stt(dst, uw[:, c, j:j + TG], V[:, 64 + c * 31 + j:65 + c * 31 + j], dst, ALU.mult, ALU.add),
                      reads=[B_uw, B_vec[l], bd], pw=[bd])
            P.add("dve", stt(y[:, c, :], acc1, V[:, 188 + c:189 + c], y[:, c, :], ALU.add, ALU.add), reads=[B_acc1, B_y, B_vec[l]], pw=[B_y])
            yb, byb = sqp.next()
            P.add("act", act(yb[:], y[:, c, :], AF.Copy), reads=[B_y], writes=[byb])
            P.add("pe", mm(ps[4][:], [(ones512[:], yb[:])], first=(c == 0), last=(c == 3)), reads=[byb, B_const],
                  writes=[B_ps[4]] if c == 0 else [], pw=[] if c == 0 else [B_ps[4]])
            ysq, bysq = sqp.next()
            P.add("act", act(ysq[:], y[:, c, :], AF.Square), reads=[B_y], writes=[bysq])
            P.add("pe", mm(ps[5][:], [(ones512[:], ysq[:])], first=(c == 0), last=(c == 3)), reads=[bysq, B_const],
                  writes=[B_ps[5]] if c == 0 else [], pw=[] if c == 0 else [B_ps[5]])
        if debug and seg == 0 and tg == 0:
            P.add("sp", dma(dbg_y.rearrange("p (c t) -> p c t", c=4), y), reads=[B_y], pw=[B_dbg], dma=True, sigbuf=B_y)
        P.add("act", act(mean, ps[4][:], AF.Copy), reads=[B_ps[4]], writes=[B_mean])
        var, bvar = a32(5184, TG), P.abuf("cvar")
        P.add("dve", stt(var[:], mean, -1.0, mean, ALU.mult, ALU.mult), reads=[B_mean], writes=[bvar])
        P.add("dve", tt(var[:], var[:], ps[5][:], ALU.add), reads=[bvar, B_ps[5]], writes=[bvar])
        rsqrt(var[:], bvar, var[:], bvar)
        for c in range(4):
            t1, bt1 = f32p.next()
            P.add("dve", tt(t1[:], y[:, c, :], mean, ALU.subtract), reads=[B_y, B_mean], writes=[bt1])
            P.add("dve", tt(t1[:], t1[:], var[:], ALU.mult), reads=[bt1, bvar], writes=[bt1])
            P.add("act", act(zs[:, c, :], t1[:], AF.Silu, scale=V[:, 192 + c:193 + c], bias=V[:, 196 + c:197 + c]),
                  reads=[bt1, B_vec[l]], pw=[B_zs])
        if debug and seg == 0 and tg == 0:
            P.add("sp", dma(dbg_zs.rearrange("p (c t) -> p c t", c=4), zs), reads=[B_zs], pw=[B_dbg], dma=True, sigbuf=B_zs)
        view, bw = load_w([(W[l]["pw"], 0, 512)], 4)
        for oc in range(4):
            pb = gen_ps.next()
            P.add("pe", mm(ps[pb][:], [(view[:, kc, oc * 128:(oc + 1) * 128], zs[:, kc, :]) for kc in range(4)]),
                  reads=[bw, B_zs], writes=[B_ps[pb]])
            P.add("act", act(cat[:, 4 + oc, :], ps[pb][:], AF.Identity, bias=V[:, 200 + oc:201 + oc]), reads=[B_ps[pb], B_vec[l]], pw=[B_ain])

        P.new_phase()
        kq = [a16(r * 512, 512) for r in range(4)]
        vq = [a16(2048 + r * 512, 512).rearrange("p (t d) -> p t d", d=128) for r in range(4)]
        bkq = [P.abuf("kq%d" % r) for r in range(4)]
        bvq = [P.abuf("vq%d" % r) for r in range(4)]
        biasT = a32(4096, 28 * 128).rearrange("p (n q) -> p n q", q=128)
        dm = a16(7680, 4 * 7 * 64).rearrange("p (i j q) -> p i j q", i=4, j=7)
        B_bias = P.abuf("biasT")
        B_dm = P.abuf("dm")
        P.add("sp", dma(biasT, W[l]["bias"].rearrange("p (n q) -> p n q", q=128)), writes=[B_bias], dma=True)
        P.add("pool", dma(dm, dmask_d.rearrange("p (i j q) -> p i j q", i=8, j=7)[:, tg * 4:(tg + 1) * 4]), writes=[B_dm], dma=True)
        st_ring = Ring([0, 1])

        def load_kv(krow0, vcol0):
            for r in range(4):
                kn_, lr_, kn = kloc(krow0)
                vn_, lc_, _vn = vloc(vcol0)
                P.add("sp", dma(kq[r], GTH[seg][kn_][r * kn + lr_:r * kn + lr_ + 128, :]), reads=[B_gth[seg][kn_]], writes=[bkq[r]], dma=True)
                P.add("sp", dma(vq[r], GTH[seg][vn_][r * T:(r + 1) * T, lc_:lc_ + 128].rearrange("(t p) c -> p t c", p=128)),
                      reads=[B_gth[seg][vn_]], writes=[bvq[r]], dma=True)

        def attn(q_ap, kpart, scale, ob, sb_):
            pend = None
            for kt in range(33):
                cur = None
                if kt < 32:
                    r, kl = kt // 8, kt % 8
                    sbk = st_ring.next()
                    P.add("pe", mm(ps[sbk][:], [(kq[r][kpart, kl * 128:(kl + 1) * 128], q_ap)]), reads=[bkq[r], B_q[tg]], writes=[B_ps[sbk]])
                    pT, bpt = ptp.next()
                    P.add("act", act(pT[:], ps[sbk][:], AF.Exp, scale=scale), reads=[B_ps[sbk]], writes=[bpt])
                    cur = (kt, r, kl, pT, bpt)
                if pend is not None:
                    k0, r0_, kl0, pT0, bpt0 = pend
                    w_, p_ = ([B_ps[ob]], []) if k0 == 0 else ([], [B_ps[ob]])
                    P.add("pe", mm(ps[ob][:], [(vq[r0_][:, kl0, :], pT0[:])], first=(k0 == 0), last=(k0 == 31)), reads=[bvq[r0_], bpt0], writes=w_, pw=p_)
                    w_, p_ = ([B_ps[sb_]], []) if k0 == 0 else ([], [B_ps[sb_]])
                    P.add("pe", mm(ps[sb_][:], [(ones1[:], pT0[:])], first=(k0 == 0), last=(k0 == 31)), reads=[bpt0, B_const], writes=w_, pw=p_)
                pend = cur

        lv = lamv[l]
        for h in range(4):
            load_kv(h * 128, h * 128)
            for m in range(2):
                kp = slice(m * 64, (m + 1) * 64)
                attn(qT[kp, h, tgs(tg)], kp, 0.125, 2 + m, 4 + m)
            r0, br0 = f32p.next()
            recip(r0[:], br0, ps[4][:], B_ps[4])
            P.add("dve", tt(r0[:], ps[2][:], r0[:], ALU.mult), reads=[B_ps[2], br0], writes=[br0])
            r1, br1 = f32p.next()
            recip(r1[:], br1, ps[5][:], B_ps[5])
            P.add("dve", tt(r1[:], ps[3][:], r1[:], ALU.mult), reads=[B_ps[3], br1], writes=[br1])
            P.add("dve", stt(r1[:], r1[:], lv[:, 4:5], r0[:], ALU.mult, ALU.add), reads=[br0, br1, B_lamv[l]], writes=[br1])
            sq, bsq = sqp.next()
            P.add("act", act(sq[:], r1[:], AF.Square), reads=[br1], writes=[bsq])
            P.add("pe", mm(ps[6][:], [(ones128[:], sq[:])]), reads=[bsq, B_const], writes=[B_ps[6]])
            rs_, brs = f32p.next()
            rsqrt(rs_[:], brs, ps[6][:], B_ps[6])
            P.add("dve", stt(cat[:, h, :], r1[:], lv[:, 5:6], rs_[:], ALU.mult, ALU.mult), reads=[br1, brs, B_lamv[l]], pw=[B_ain])
        sC = 128 ** -0.5
        for g in range(2):
            load_kv(512 + g * 128, 512 + g * 128)
            for rr in range(2):
                hq = 2 * g + rr
                attn(qT[:, 4 + hq, tgs(tg)], slice(0, 128), sC, 2, 4)
                rc, brc = f32p.next()
                recip(rc[:], brc, ps[4][:], B_ps[4])
                P.add("dve", tt(cat[:, 8 + hq, :], ps[2][:], rc[:], ALU.mult), reads=[B_ps[2], brc], pw=[B_ain])
        for h in range(4):
            wK, wV = winK[seg], winV[seg]
            for part, (c_lo, c_hi) in enumerate(((0, 384), (384, 1408), (1408, 1792))):
                n_ = c_hi - c_lo
                P.add("sp", dma(kq[part][:, 0:n_], wK[h * 128:(h + 1) * 128, c_lo:c_hi]), reads=[B_winK[seg]], writes=[bkq[part]], dma=True)
                P.add("sp", dma(vq[part][:, 0:n_ // 128, :], wV[c_lo:c_hi, h * 128:(h + 1) * 128].rearrange("(t p) c -> p t c", p=128)),
                      reads=[B_winV[seg]], writes=[bvq[part]], dma=True)
            def kwin(wt):
                if wt < 3:
                    return kq[0][:, wt * 128:(wt + 1) * 128]
                if wt < 11:
                    return kq[1][:, (wt - 3) * 128:(wt - 2) * 128]
                return kq[2][:, (wt - 11) * 128:(wt - 10) * 128]

            def vwin(wt):
                if wt < 3:
                    return vq[0][:, wt, :]
                if wt < 11:
                    return vq[1][:, wt - 3, :]
                return vq[2][:, wt - 11, :]
            for i in range(4):
                lt = tg * 4 + i
                for j in range(7):
                    wt = lt + j
                    sbk = st_ring.next()
                    P.add("pe", mm(ps[sbk][:, 0:128], [(kwin(wt), qT[:, 8 + h, tg * TG + i * 128:tg * TG + (i + 1) * 128])]),
                          reads=[bkq[0], bkq[1], bkq[2], B_q[tg]], writes=[B_ps[sbk]])
                    t_, bt_ = f32p.next()
                    P.add("dve", stt(t_[:, 0:128], ps[sbk][:, 0:128], sC, biasT[:, h * 7 + j, :], ALU.mult, ALU.add),
                          reads=[B_ps[sbk], B_bias], writes=[bt_])
                    e_, be_ = sqp.next()
                    P.add("act", act(e_[:, 0:128], t_[:, 0:128], AF.Exp), reads=[bt_], writes=[be_])
                    pT, bpt = ptp.next()
                    P.add("dve", tt(pT[:, 0:128], e_[:, 0:128], dm[:, i, j, :], ALU.mult), reads=[be_, B_dm], writes=[bpt])
                    w_, p_ = ([B_ps[3]], []) if j == 0 else ([], [B_ps[3]])
                    P.add("pe", mm(ps[3][:, 0:128], [(vwin(wt), pT[:, 0:128])], first=(j == 0), last=(j == 6)), reads=[bvq[0], bvq[1], bvq[2], bpt], writes=w_, pw=p_)
                    w_, p_ = ([B_ps[5]], []) if j == 0 else ([], [B_ps[5]])
                    P.add("pe", mm(ps[5][:, 0:128], [(ones1[:], pT[:, 0:128])], first=(j == 0), last=(j == 6)), reads=[bpt, B_const], writes=w_, pw=p_)
                rc, brc = f32p.next()
                recip(rc[:, 0:128], brc, ps[5][:, 0:128], B_ps[5])
                P.add("dve", tt(cat[:, 12 + h, i * 128:(i + 1) * 128], ps[3][:, 0:128], rc[:, 0:128], ALU.mult), reads=[B_ps[3], brc], pw=[B_ain])

    def phase2b(l, tg):
        P.new_phase()
        mixed = a32(0, KC * TG).rearrange("p (c t) -> p c t", c=KC)
        B_mixed = P.abuf("mixed")
        cat = ain
        for g in range(8):
            view, bw = load_w([(W[l]["w_out"], g * 256, 256)], KC)
            for j in range(2):
                oc = g * 2 + j
                pb = gen_ps.next()
                P.add("pe", mm(ps[pb][:], [(view[:, kc, j * 128:(j + 1) * 128], cat[:, kc, :]) for kc in range(KC)]),
                      reads=[bw, B_ain], writes=[B_ps[pb]])
                P.add("dve", (lambda e, oc=oc, pb=pb: e.tensor_copy(out=mixed[:, oc, :], in_=ps[pb][:])), reads=[B_ps[pb]], pw=[B_mixed])
                sq, bsq = sqp.next()
                P.add("act", act(sq[:], ps[pb][:], AF.Square), reads=[B_ps[pb]], writes=[bsq])
                P.add("pe", mm(ps[6][:], [(onesD[:], sq[:])], first=(oc == 0), last=(oc == KC - 1)), reads=[bsq, B_const],
                      writes=[B_ps[6]] if oc == 0 else [], pw=[] if oc == 0 else [B_ps[6]])
        rstd, brstd = f32p.next()
        rsqrt(rstd[:], brstd, ps[6][:], B_ps[6])
        for oc in range(KC):
            P.add("dve", stt(mixed[:, oc, :], mixed[:, oc, :], vec[l][:, 16 + oc:17 + oc], rstd[:], ALU.mult, ALU.mult),
                  reads=[B_mixed, brstd, B_vec[l]], pw=[B_mixed])
            P.add("dve", tt(xT[:, oc, tgs(tg)], xT[:, oc, tgs(tg)], mixed[:, oc, :], ALU.add), reads=[B_mixed, B_x[tg]], pw=[B_x[tg]])

    def phase2c(l, tg):
        P.new_phase()
        fT = a32(0, KC * TG).rearrange("p (c t) -> p c t", c=KC)
        actT = a16(8192, 22 * TG // 2).rearrange("p (c t) -> p c t", c=22)
        B_fT = P.abuf("fT")
        B_act = P.abuf("act")
        rmsnorm_to_ain(l, tg, 32)
        for hf in range(2):
            for pr in range(11):
                fl0 = 2 * pr
                fc0 = hf * 22 + fl0
                viewg, bwg = load_w([(W[l]["gate"], fc0 * 128, 256)], KC)
                pg = [gen_ps.next(), gen_ps.next()]
                for j in range(2):
                    P.add("pe", mm(ps[pg[j]][:], [(viewg[:, kc, j * 128:(j + 1) * 128], ain[:, kc, :]) for kc in range(KC)]),
                          reads=[bwg, B_ain], writes=[B_ps[pg[j]]])
                sgs = []
                for j in range(2):
                    sg, bsg = f32p.next()
                    P.add("act", act(sg[:], ps[pg[j]][:], AF.Silu), reads=[B_ps[pg[j]]], writes=[bsg])
                    sgs.append((sg, bsg))
                viewu, bwu = load_w([(W[l]["up"], fc0 * 128, 256)], KC)
                pu = [gen_ps.next(), gen_ps.next()]
                for j in range(2):
                    P.add("pe", mm(ps[pu[j]][:], [(viewu[:, kc, j * 128:(j + 1) * 128], ain[:, kc, :]) for kc in range(KC)]),
                          reads=[bwu, B_ain], writes=[B_ps[pu[j]]])
                for j in range(2):
                    P.add("dve", tt(actT[:, fl0 + j, :], ps[pu[j]][:], sgs[j][0][:], ALU.mult), reads=[B_ps[pu[j]], sgs[j][1]], pw=[B_act])
            for oc in range(KC):
                view, bw = load_w([(W[l]["down"], oc * 128, 128)], 22, krow0=hf * 22 * 128)
                for j in range(1):
                    pb = gen_ps.next()
                    P.add("pe", mm(ps[pb][:], [(view[:, fl, :], actT[:, fl, :]) for fl in range(22)]), reads=[bw, B_act], writes=[B_ps[pb]])
                    if hf == 0:
                        P.add("act", act(fT[:, oc, :], ps[pb][:], AF.Copy), reads=[B_ps[pb]], pw=[B_fT])
                    else:
                        P.add("dve", tt(fT[:, oc, :], fT[:, oc, :], ps[pb][:], ALU.add), reads=[B_ps[pb], B_fT], pw=[B_fT])
                        sq, bsq = sqp.next()
                        P.add("act", act(sq[:], fT[:, oc, :], AF.Square), reads=[B_fT], writes=[bsq])
                        P.add("pe", mm(ps[6][:], [(onesD[:], sq[:])], first=(oc == 0), last=(oc == KC - 1)), reads=[bsq, B_const],
                              writes=[B_ps[6]] if oc == 0 else [], pw=[] if oc == 0 else [B_ps[6]])
        rstd, brstd = f32p.next()
        rsqrt(rstd[:], brstd, ps[6][:], B_ps[6])
        for oc in range(KC):
            P.add("dve", stt(fT[:, oc, :], fT[:, oc, :], vec[l][:, 48 + oc:49 + oc], rstd[:], ALU.mult, ALU.mult),
                  reads=[B_fT, brstd, B_vec[l]], pw=[B_fT])
            P.add("dve", tt(xT[:, oc, tgs(tg)], xT[:, oc, tgs(tg)], fT[:, oc, :], ALU.add), reads=[B_fT, B_x[tg]], pw=[B_x[tg]])

    def dyn_init(e):
        pid = e.partition_id()
        rk = e.snap(pid % 4)
        DYN["rk"] = rk
        DYN["lb"] = e.snap((rk + 3) % 4)
        DYN["rb"] = e.snap((rk + 1) % 4)
        return None
    P.ops["pool"].append(_mk_raw(dyn_init))
    setup_consts()
    if first_seg:
        for tg in range(NTG):
            P.add("sp", dma(xT[:, :, tgs(tg)], xT_d.rearrange("(c p) t -> p c t", p=128)[:, :, tgs(tg)]), writes=[B_x[tg]], dma=True)
    else:
        for tg in range(NTG):
            P.add("sp", dma(xT[:, :, tgs(tg)], stx_in.rearrange("p (c t) -> p c t", c=KC)[:, :, tgs(tg)]), writes=[B_x[tg]], dma=True)
            P.add("sp", dma(qT[:, :, tgs(tg)], stq_in.rearrange("p (c t) -> p c t", c=12)[:, :, tgs(tg)]), writes=[B_q[tg]], dma=True)
    for seg in range(seg_lo, seg_hi + 1):
        if seg >= 1:
            lprev = seg - 1
            make_windows(seg - 1)
            for tg in range(NTG):
                phase2a(lprev, seg - 1, tg)
                if debug and seg == 1:
                    P.add("sp", dma(dbg_cat[tg], ain[:].rearrange("p c t -> p (c t)")), reads=[B_ain], pw=[B_dbg], dma=True, sigbuf=B_ain)
                phase2b(lprev, tg)
                if debug and seg == 1:
                    P.add("sp", dma(dbg_xmid.rearrange("p (c t) -> p c t", c=KC)[:, :, tgs(tg)], xT[:, :, tgs(tg)]), reads=[B_x[tg]], pw=[B_dbg], dma=True, sigbuf=B_x[tg])
                phase2c(lprev, tg)
        if seg <= 1:
            for tg in range(NTG):
                phase1(seg, seg, tg)
            exchange(seg)
    finals = []
    if last_seg:
        for tg in range(NTG):
            P.add("sp", dma(out_d.rearrange("(c p) t -> p c t", p=128)[:, :, tgs(tg)], xT[:, :, tgs(tg)]), reads=[B_x[tg]], pw=[B_out], dma=True, sigbuf=B_x[tg])
        finals = [B_out]
    elif not fused:
        for tg in range(NTG):
            P.add("sp", dma(stx_out.rearrange("p (c t) -> p c t", c=KC)[:, :, tgs(tg)], xT[:, :, tgs(tg)]), reads=[B_x[tg]], pw=[B_st], dma=True, sigbuf=B_x[tg])
            P.add("sp", dma(stq_out.rearrange("p (c t) -> p c t", c=12)[:, :, tgs(tg)], qT[:, :, tgs(tg)]), reads=[B_q[tg]], pw=[B_st], dma=True, sigbuf=B_q[tg])
        finals = [B_st, B_dbg] + list(B_snd[seg_lo].values())
    P.add("sp", None, reads=finals)
    P.finalize(nc, stack)
    stack.close()
    nc._in_names = in_names
    return nc


def _host_consts():
    theta = 10000.0
    inv = np.power(theta, -np.arange(0, 64, 2, dtype=np.float32) / 64).astype(np.float32)
    p = np.arange(128)
    j = p % 32
    sign = np.where((p % 64) < 32, -1.0, 1.0).astype(np.float32)
    tabs, dmasks, halos = [], [], []
    for rank in range(4):
        s = (rank * T + np.arange(T))
        posA = s.astype(np.float32)
        angA = posA[None, :] * inv[j][:, None]
        row = (s // 64).astype(np.float32)
        col = (s % 64).astype(np.float32)
        posC = np.where((p < 64)[:, None], row[None, :], col[None, :]).astype(np.float32)
        angC = posC * inv[j][:, None]
        tab = np.stack([np.cos(angA), np.sin(angA) * sign[:, None], np.cos(angC), np.sin(angC) * sign[:, None]], axis=1)
        tabs.append(np.ascontiguousarray(tab.reshape(128, 4 * T).astype(np.float32)))
        dmk = np.zeros((128, 8, 7, 128), np.float32)
        kk = np.arange(128)
        kr_par, kc = kk // 64, kk % 64
        qq = np.arange(128)
        qr_l, qc = qq // 64, qq % 64
        for lt in range(8):
            b = rank * 8 + lt
            qr = 2 * b + qr_l
            win_r = np.clip(qr - 4, 0, 56)
            win_c = np.clip(qc - 8, 0, 48)
            for jj in range(7):
                kr = 2 * b + 2 * (jj - 3) + kr_par
                ok = ((kr[:, None] >= 0) & (kr[:, None] < 64) & (kr[:, None] >= win_r[None, :]) & (kr[:, None] < win_r[None, :] + 8)
                      & (kc[:, None] >= win_c[None, :]) & (kc[:, None] < win_c[None, :] + 16))
                dmk[:, lt, jj, :] = ok
        dmasks.append(np.ascontiguousarray(dmk.reshape(128, 8 * 7 * 128)))
        hl = np.zeros((128, 2), np.float32)
        hl[:, 0] = 0.0 if rank == 0 else 1.0
        hl[:, 1] = 0.0 if rank == 3 else 1.0
        halos.append(hl)
    m = np.arange(128)
    perm = np.zeros((128, 128), np.float32)
    perm[m ^ 32, m] = 1.0
    return tabs, dmasks, halos, perm


def _layer_inputs(inp, l):
    def pc(v, n):
        return np.ascontiguousarray(np.asarray(v, np.float32).reshape(n, 128).T)
    vec = np.zeros((128, NV), np.float32)
    vec[:, 0:16] = pc(inp["norm_mix_pre"][l], 16)
    vec[:, 16:32] = pc(inp["norm_mix_post"][l], 16)
    vec[:, 32:48] = pc(inp["norm_ffn_pre"][l], 16)
    vec[:, 48:64] = pc(inp["norm_ffn_post"][l], 16)
    dw = np.asarray(inp["conv_dw"][l], np.float32)
    vec[:, 64:188] = dw.reshape(31, 4, 128).transpose(2, 1, 0).reshape(128, 124)
    vec[:, 188:192] = pc(inp["conv_dw_b"][l], 4)
    vec[:, 192:196] = pc(inp["conv_ln_g"][l], 4)
    vec[:, 196:200] = pc(inp["conv_ln_b"][l], 4)
    vec[:, 200:204] = pc(inp["conv_pw_b"][l], 4)
    vec[:, 204] = np.asarray(inp["diff_subln"][l], np.float32)
    vec[:, 205] = np.asarray(inp["gqa_q_norm"][l], np.float32)
    vec[:, 206] = np.asarray(inp["gqa_k_norm"][l], np.float32)
    lamb = np.ascontiguousarray(np.broadcast_to(np.asarray(inp["diff_lambda"][l], np.float32).reshape(1, 256), (128, 256)))
    rpb = np.asarray(inp["na_rpb"][l], np.float32)
    kk = np.arange(128)
    kr_par, kc = kk // 64, kk % 64
    qq = np.arange(128)
    qr_l, qc = qq // 64, qq % 64
    bias = np.zeros((128, 4, 7, 128), np.float32)
    for jj in range(7):
        dr = 2 * (jj - 3) + kr_par[:, None] - qr_l[None, :]
        ir = np.clip(dr + 7, 0, 14)
        ic = np.clip(kc[:, None] - qc[None, :] + 15, 0, 30)
        for h in range(4):
            bias[:, h, jj, :] = rpb[h][ir, ic]
    d = {
        "w_in%d" % l: np.ascontiguousarray(inp["w_in"][l], np.float32), "w_out%d" % l: np.ascontiguousarray(inp["w_out"][l], np.float32),
        "gate%d" % l: np.ascontiguousarray(inp["ffn_gate"][l], np.float32), "up%d" % l: np.ascontiguousarray(inp["ffn_up"][l], np.float32),
        "down%d" % l: np.ascontiguousarray(inp["ffn_down"][l], np.float32), "pw%d" % l: np.ascontiguousarray(inp["conv_pw"][l], np.float32),
        "vec%d" % l: vec, "lamb%d" % l: lamb, "bias%d" % l: np.ascontiguousarray(bias.reshape(128, 28 * 128)),
    }
    return d


_NC_CACHE = {}


def _get_nc(lo, hi, fused):
    key = (lo, hi, fused)
    if key not in _NC_CACHE:
        _NC_CACHE[key] = build_program(lo, hi, fused)
    return _NC_CACHE[key]


def kernel(**inp):
    inp = {k: np.asarray(v) for k, v in inp.items()}
    x = inp["x"].astype(np.float32, copy=False)
    tabs, dmasks, halos, perm = _host_consts()
    lay = {l: _layer_inputs(inp, l) for l in range(L)}
    cores = list(range(8))

    def common(c):
        r = c % 4
        return {"tabs": tabs[r], "dmask": dmasks[r], "halo": halos[r], "perm": perm}

    def xT_of(c):
        b, r = c // 4, c % 4
        return np.ascontiguousarray(x[b, r * T:(r + 1) * T, :].T)

    if FUSED:
        nc = _get_nc(0, 2, True)
        maps = []
        for c in cores:
            m = common(c)
            m.update(lay[0])
            m.update(lay[1])
            m["xT"] = xT_of(c)
            maps.append(m)
        maps = [{k: m[k] for k in nc._in_names} for m in maps]
        res = run_bass_kernel_spmd(nc, maps, core_ids=cores)
        outs = [np.asarray(res.results[c]["outT"]) for c in cores]
    else:
        state = None
        outs = None
        for seg in range(3):
            nc = _get_nc(seg, seg, False)
            maps = []
            for c in cores:
                m = common(c)
                for l in sorted({0 if seg <= 1 else 1, 1 if seg >= 1 else 0}):
                    m.update(lay[l])
                if seg == 0:
                    m["xT"] = xT_of(c)
                else:
                    g0 = (c // 4) * 4
                    m["stx_in"] = state[c]["stx_out"]
                    m["stq_in"] = state[c]["stq_out"]
                    for n_ in ("kA", "kC", "kD", "uu", "vA", "vC", "vD"):
                        m["gth_" + n_] = np.concatenate([state[g0 + r]["snd_" + n_] for r in range(4)], axis=0)
                maps.append(m)
            maps = [{k: m[k] for k in nc._in_names} for m in maps]
            res = run_bass_kernel_spmd(nc, maps, core_ids=cores)
            if seg < 2:
                state = [{k: np.asarray(v) for k, v in res.results[c].items()} for c in cores]
            else:
                outs = [np.asarray(res.results[c]["outT"]) for c in cores]
    out = np.zeros((2, 4096, D), np.float32)
    for c in cores:
        b, r = c // 4, c % 4
        out[b, r * T:(r + 1) * T, :] = outs[c].T
    return out
```

```python
import contextlib
import numpy as np
import ml_dtypes
import concourse.bass as bass
import concourse.mybir as mybir
from concourse.bass_utils import run_bass_kernel_spmd

F32 = mybir.dt.float32
BF16 = mybir.dt.bfloat16
AF = mybir.ActivationFunctionType
ALU = mybir.AluOpType

L = 2
D = 2048
KC = 16
T = 1024
TG = 512
NTG = 2
FF = 5632
FC = 44
INC = 5120
EPS = 1e-6
NV = 208
KROWS = 1792
VCOLS = 1280
ENGS = ("pe", "act", "dve", "pool", "sp")
FUSED = True
KCUT = 0


class Buf:
    __slots__ = ("name", "ws", "rs", "sem", "cnt", "arena", "psum")

    def __init__(self, name, arena=False, psum=False):
        self.psum = psum
        self.name = name
        self.ws = {}
        self.rs = {}
        self.sem = None
        self.cnt = 0
        self.arena = arena


class Op:
    __slots__ = ("eng", "fn", "deps", "dma", "cc", "sig", "val", "dbuf", "ninc")


class Prog:
    def __init__(self):
        self.ops = {e: [] for e in ENGS}
        self.last_compute = {}
        self.arena_dmas = []
        self.fence_ops = []
        self.dma_bufs = []
        self.cc_cnt = 0

    def add(self, eng, fn, reads=(), writes=(), pw=(), dma=False, cc=False, ninc=1, sigbuf=None):
        op = Op()
        op.eng = eng
        op.fn = fn
        op.dma = dma
        op.cc = cc
        op.sig = False
        op.val = None
        op.dbuf = None
        op.ninc = ninc
        deps = {}
        for b in reads:
            for w in b.ws.values():
                deps[w] = True
            if b.psum:
                for r in b.rs.values():
                    if r.eng != eng and r not in deps:
                        deps[r] = False
        for b in list(writes) + list(pw):
            for r in b.rs.values():
                if r not in deps:
                    deps[r] = False
        for b in writes:
            for w in b.ws.values():
                if w not in deps:
                    deps[w] = False
        op.deps = deps
        if dma:
            db = sigbuf if sigbuf is not None else (list(writes) + list(pw))[0]
            if db.sem is None:
                self.dma_bufs.append(db)
                db.sem = True
            db.cnt += 16 * ninc
            op.val = db.cnt
            op.dbuf = db
            if any(b.arena for b in list(reads) + list(writes) + list(pw)):
                self.arena_dmas.append(op)
        elif cc:
            self.cc_cnt += 1
            op.val = self.cc_cnt
        else:
            self.last_compute[eng] = op
        wkey = ("dma", id(op.dbuf)) if dma else ("cc" if cc else eng)
        for b in reads:
            if dma or cc:
                b.rs[("dma", id(op.dbuf) if dma else "cc")] = op
            else:
                b.rs[eng] = op
        for b in writes:
            b.ws = {wkey: op}
            b.rs = {}
        for b in pw:
            b.ws[wkey] = op
        self.ops[eng].append(op)
        return op

    def new_phase(self):
        self.fence_ops = list(self.last_compute.values()) + list(self.arena_dmas)
        self.arena_dmas = []

    def abuf(self, name):
        if not hasattr(self, "_ab"):
            self._ab = {}
        if name not in self._ab:
            self._ab[name] = Buf(name, arena=True)
        return self.fence(self._ab[name])

    def fence(self, buf):
        buf.ws = {}
        buf.rs = {("f", i): o for i, o in enumerate(self.fence_ops)}
        return buf

    def finalize(self, nc, stack):
        engsem = {e: stack.enter_context(nc.semaphore("s_" + e)) for e in ENGS}
        ccsem = stack.enter_context(nc.semaphore("s_cc"))
        for i, b in enumerate(self.dma_bufs):
            b.sem = stack.enter_context(nc.semaphore("d%d" % i))
        for e in ENGS:
            for op in self.ops[e]:
                for d, raw in op.deps.items():
                    if d.dma or d.cc:
                        continue
                    if d.eng == op.eng and not op.dma and not op.cc:
                        if e == "pe":
                            continue
                    d.sig = True
        for e in ENGS:
            c = 0
            for op in self.ops[e]:
                if not op.dma and not op.cc and op.sig:
                    c += 1
                    op.val = c
        block = stack.enter_context(nc.Block())

        def emit(e, eng):
            waited = {}
            for op in self.ops[e]:
                for d, raw in op.deps.items():
                    if d.dma:
                        sem, key, v = d.dbuf.sem, ("d", id(d.dbuf)), d.val
                    elif d.cc:
                        sem, key, v = ccsem, "cc", d.val
                    else:
                        if d.eng == e and not op.dma and not op.cc and e == "pe":
                            continue
                        sem, key, v = engsem[d.eng], d.eng, d.val
                    if waited.get(key, 0) >= v:
                        continue
                    waited[key] = v
                    eng.wait_ge(sem, v)
                if op.fn is None:
                    continue
                res = op.fn(eng)
                if op.dma:
                    if not isinstance(res, (list, tuple)):
                        res = [res]
                    assert len(res) == op.ninc
                    for r in res:
                        r.then_inc(op.dbuf.sem, 16)
                elif op.cc:
                    res.then_inc(ccsem, 1)
                elif op.sig:
                    res.then_inc(engsem[e], 1)

        @block.tensor
        def _(eng):
            emit("pe", eng)

        @block.scalar
        def _(eng):
            emit("act", eng)

        @block.vector
        def _(eng):
            emit("dve", eng)

        @block.gpsimd
        def _(eng):
            emit("pool", eng)

        @block.sync
        def _(eng):
            emit("sp", eng)


def _mk_raw(fn):
    op = Op()
    op.eng = "pool"
    op.fn = fn
    op.deps = {}
    op.dma = False
    op.cc = False
    op.sig = False
    op.val = None
    op.dbuf = None
    op.ninc = 0
    return op


class Ring:
    def __init__(self, items):
        self.items = items
        self.i = 0

    def next(self):
        it = self.items[self.i % len(self.items)]
        self.i += 1
        return it


def build_program(seg_lo, seg_hi, fused, debug=False):
    nc = bass.Bass("TRN2", target_bir_lowering=False)
    P = Prog()
    DYN = {}
    stack = contextlib.ExitStack()
    first_seg, last_seg = seg_lo == 0, seg_hi == 2
    layers = sorted({0 if s <= 1 else 1 for s in range(seg_lo, seg_hi + 1)} | {1 if s >= 1 else 0 for s in range(seg_lo, seg_hi + 1)})

    in_names = []

    def din(name, shape, dt=F32):
        in_names.append(name)
        return nc.dram_tensor(name, list(shape), dt, kind="ExternalInput").ap()

    def dout(name, shape, dt=F32):
        return nc.dram_tensor(name, list(shape), dt, kind="ExternalOutput").ap()

    class _LazyW(dict):
        def __init__(self, l):
            super().__init__()
            self.l = l

        def __missing__(self, key):
            shp = {"w_in": [D, INC], "w_out": [D, D], "gate": [D, FF], "up": [D, FF], "down": [FF, D], "pw": [512, 512],
                   "vec": [128, NV], "lamb": [128, 256], "bias": [128, 28 * 128]}[key]
            v = din("%s%d" % (key, self.l), shp)
            self[key] = v
            return v
    W = {l: _LazyW(l) for l in layers}
    tabs_d = din("tabs", [128, 4 * T])
    dmask_d = din("dmask", [128, 8 * 7 * 128])
    perm_d = din("perm", [128, 128])
    halo_d = din("halo", [128, 2])
    if first_seg:
        xT_d = din("xT", [D, T])
    else:
        stx_in = din("stx_in", [128, KC * T])
        stq_in = din("stq_in", [128, 12 * T], BF16)
    if last_seg:
        out_d = dout("outT", [D, T])
    KSEG = [("kA", 0, 512), ("kC", 512, 256), ("kD", 768, 512), ("uu", 1280, 512)]
    VSEG = [("vA", 0, 512), ("vC", 512, 256), ("vD", 768, 512)]
    NAMES = [k[0] for k in KSEG] + [v[0] for v in VSEG]

    def kloc(row):
        for name, r0, n in KSEG:
            if r0 <= row < r0 + n:
                return name, row - r0, n
        raise ValueError(row)

    def vloc(col):
        for name, c0, n in VSEG:
            if c0 <= col < c0 + n:
                return name, col - c0, n
        raise ValueError(col)

    def snd_shape(name):
        for nm, _, n in KSEG:
            if nm == name:
                return [n, T]
        for nm, _, n in VSEG:
            if nm == name:
                return [T, n]

    def gth_shape(name):
        sh = snd_shape(name)
        return [4 * sh[0], sh[1]]
    SND, GTH, SNDH, GTHH = {}, {}, {}, {}
    if not fused:
        if not last_seg:
            stx_out = dout("stx_out", [128, KC * T])
            stq_out = dout("stq_out", [128, 12 * T], BF16)
            SND[seg_lo] = {n_: dout("snd_" + n_, snd_shape(n_), BF16) for n_ in NAMES}
        if not first_seg:
            GTH[seg_lo - 1] = {n_: din("gth_" + n_, gth_shape(n_), BF16) for n_ in NAMES}
    else:
        for s_ in (0, 1):
            SNDH[s_] = {n_: nc.dram_tensor("snd%d_%s" % (s_, n_), snd_shape(n_), BF16) for n_ in NAMES}
            GTHH[s_] = {n_: nc.dram_tensor("gth%d_%s" % (s_, n_), gth_shape(n_), BF16) for n_ in NAMES}
            SND[s_] = {n_: SNDH[s_][n_].ap() for n_ in NAMES}
            GTH[s_] = {n_: GTHH[s_][n_].ap() for n_ in NAMES}
    B_snd = {s_: {n_: Buf("snd%d%s" % (s_, n_)) for n_ in NAMES} for s_ in (0, 1)}
    B_gth = {s_: {n_: Buf("gth%d%s" % (s_, n_)) for n_ in NAMES} for s_ in (0, 1)}
    B_out = Buf("out")
    B_st = Buf("stout")
    if debug:
        dbg_cat = dout("dbg_cat", [NTG, 128, KC * TG], BF16)
        dbg_xmid = dout("dbg_xmid", [128, KC * T])
        dbg_y = dout("dbg_y", [128, 4 * TG])
        dbg_zs = dout("dbg_zs", [128, 4 * TG], BF16)
    B_dbg = Buf("dbg")
    winK = {s_: nc.dram_tensor("winK%d" % s_, [1024, 1792], BF16) for s_ in (0, 1)}
    winV = {s_: nc.dram_tensor("winV%d" % s_, [1792, 512], BF16) for s_ in (0, 1)}
    B_winK = {s_: Buf("winK%d" % s_) for s_ in (0, 1)}
    B_winV = {s_: Buf("winV%d" % s_) for s_ in (0, 1)}

    def sb(name, shape, dt):
        return stack.enter_context(nc.sbuf_tensor("t_" + name, list(shape), dt))

    xT = sb("xT", [128, KC, T], F32)
    qT = sb("qT", [128, 12, T], BF16)
    ain = sb("ain", [128, KC, TG], BF16)
    B_x = [Buf("x%d" % i) for i in range(NTG)]
    B_q = [Buf("q%d" % i) for i in range(NTG)]
    B_ain = Buf("ain")
    wbufs = Ring([(sb("wb%d" % i, [128, 4096], BF16), Buf("wb%d" % i)) for i in range(3)])
    ptp = Ring([(sb("pt%d" % i, [128, TG], BF16), Buf("pt%d" % i)) for i in range(3)])
    sqp = Ring([(sb("sq%d" % i, [128, TG], BF16), Buf("sq%d" % i)) for i in range(3)])
    f32p = Ring([(sb("f%d" % i, [128, TG], F32), Buf("f%d" % i)) for i in range(4)])
    stgp = Ring([(sb("stg%d" % i, [128, TG], BF16), Buf("stg%d" % i)) for i in range(2)])
    vec = {l: sb("vec%d" % l, [128, NV], F32) for l in layers}
    lamb = {l: sb("lamb%d" % l, [128, 256], F32) for l in layers}
    lamv = {l: sb("lamv%d" % l, [128, 8], F32) for l in layers}
    B_vec = {l: Buf("vec%d" % l) for l in layers}
    B_lamv = {l: Buf("lamv%d" % l) for l in layers}
    perm = sb("perm", [128, 128], BF16)
    onesD = sb("onesD", [128, 128], BF16)
    ones128 = sb("ones128", [128, 128], BF16)
    ones512 = sb("ones512", [128, 128], BF16)
    ones1 = sb("ones1", [128, 128], BF16)
    halo = sb("halo", [128, 2], F32)
    epsc = sb("epsc", [128, 1], F32)
    B_const = Buf("const")
    B_perm = Buf("perm")
    B_halo = Buf("halo")
    ARENA_F32 = 13824
    arena = sb("arena", [128, ARENA_F32], F32)

    def a32(off, n):
        return arena[:, off:off + n]

    def a16(off, n):
        return arena[:, off:off + n].bitcast(BF16)

    ps = [stack.enter_context(nc.psum_tensor("ps%d" % i, [128, 512], F32)) for i in range(8)]
    B_ps = [Buf("ps%d" % i, psum=True) for i in range(8)]

    def mm(out, pairs, first=True, last=True):
        def fn(e):
            n = len(pairs)
            ins = None
            for i, (l_, r_) in enumerate(pairs):
                ins = e.matmul(out, l_, r_, start=(first and i == 0), stop=(last and i == n - 1))
            return ins
        return fn

    def dma(out, in_):
        return lambda e: e.dma_start(out=out, in_=in_)

    def act(out, in_, func, scale=None, bias=None):
        kw = {}
        if scale is not None:
            kw["scale"] = scale
        if bias is not None:
            kw["bias"] = bias
        return lambda e: e.activation(out=out, in_=in_, func=func, **kw)

    def tt(out, a, b, op):
        return lambda e: e.tensor_tensor(out=out, in0=a, in1=b, op=op)

    def ts(out, a, s1, s2, op0, op1):
        return lambda e: e.tensor_scalar(out=out, in0=a, scalar1=s1, scalar2=s2, op0=op0, op1=op1)

    def stt(out, a, s, b, op0, op1):
        return lambda e: e.scalar_tensor_tensor(out=out, in0=a, scalar=s, in1=b, op0=op0, op1=op1)

    def tgs(tg):
        return slice(tg * TG, (tg + 1) * TG)

    def recip(out, bout, in_, bin_):
        P.add("act", act(out, in_, AF.Ln), reads=[bin_], writes=[bout])
        P.add("act", act(out, out, AF.Exp, scale=-1.0), reads=[bout], writes=[bout])

    def rsqrt(out, bout, in_, bin_):
        P.add("act", act(out, in_, AF.Ln, bias=epsc[:, 0:1]), reads=[bin_, B_const], writes=[bout])
        P.add("act", act(out, out, AF.Exp, scale=-0.5), reads=[bout], writes=[bout])

    def setup_consts():
        for t_, v in ((onesD, 1.0 / D), (ones128, 1.0 / 128), (ones512, 1.0 / 512), (ones1, 1.0)):
            P.add("dve", (lambda e, t_=t_, v=v: e.memset(t_[:], v)), pw=[B_const])
        P.add("dve", (lambda e: e.memset(epsc[:], EPS)), pw=[B_const])
        P.add("pool", dma(perm[:], perm_d), writes=[B_perm], dma=True)
        P.add("sp", dma(halo[:], halo_d), writes=[B_halo], dma=True)
        for l in layers:
            P.add("sp", dma(vec[l][:], W[l]["vec"]), writes=[B_vec[l]], dma=True)
            bl = Buf("lamb")
            P.add("sp", dma(lamb[l][:], W[l]["lamb"]), writes=[bl], dma=True)
            lv = lamv[l]
            lam_init = 0.8 - 0.6 * float(np.exp(-0.3 * l))
            tmpf, btmp = f32p.next()
            P.add("dve", tt(tmpf[:, 0:64], lamb[l][:, 0:64], lamb[l][:, 64:128], ALU.mult), reads=[bl], writes=[btmp])
            P.add("dve", (lambda e, lv=lv, tmpf=tmpf: e.reduce_sum(out=lv[:, 0:1], in_=tmpf[:, 0:64], axis=mybir.AxisListType.X)),
                  reads=[btmp], pw=[B_lamv[l]])
            tmpf2, btmp2 = f32p.next()
            P.add("dve", tt(tmpf2[:, 0:64], lamb[l][:, 128:192], lamb[l][:, 192:256], ALU.mult), reads=[bl], writes=[btmp2])
            P.add("dve", (lambda e, lv=lv, tmpf2=tmpf2: e.reduce_sum(out=lv[:, 1:2], in_=tmpf2[:, 0:64], axis=mybir.AxisListType.X)),
                  reads=[btmp2], pw=[B_lamv[l]])
            P.add("act", act(lv[:, 2:4], lv[:, 0:2], AF.Exp), reads=[B_lamv[l]], pw=[B_lamv[l]])
            P.add("dve", stt(lv[:, 4:5], lv[:, 3:4], -lam_init, lv[:, 2:3], ALU.add, ALU.subtract), reads=[B_lamv[l]], pw=[B_lamv[l]])
            P.add("dve", (lambda e, lv=lv, l=l, li=lam_init: e.tensor_scalar_mul(out=lv[:, 5:6], in0=vec[l][:, 204:205], scalar1=1.0 - li)),
                  reads=[B_vec[l]], pw=[B_lamv[l]])

    def rmsnorm_to_ain(l, tg, gcol):
        for c in range(KC):
            sq, bsq = sqp.next()
            P.add("act", act(sq[:], xT[:, c, tgs(tg)], AF.Square), reads=[B_x[tg]], writes=[bsq])
            P.add("pe", mm(ps[6][:], [(onesD[:], sq[:])], first=(c == 0), last=(c == KC - 1)), reads=[bsq, B_const],
                  writes=[B_ps[6]] if c == 0 else [], pw=[] if c == 0 else [B_ps[6]])
        rstd, brstd = f32p.next()
        rsqrt(rstd[:], brstd, ps[6][:], B_ps[6])
        for c in range(KC):
            P.add("dve", stt(ain[:, c, :], xT[:, c, tgs(tg)], vec[l][:, gcol + c:gcol + c + 1], rstd[:], ALU.mult, ALU.mult),
                  reads=[B_x[tg], brstd, B_vec[l]], pw=[B_ain])

    def load_w(segs, kchunks, krow0=0):
        wt, bw = wbufs.next()
        tot = sum(s[2] for s in segs)
        assert kchunks * tot <= 4096
        view = wt[:, 0:kchunks * tot].rearrange("p (k c) -> p k c", c=tot)
        off = 0
        for i, (Wd, c0, n) in enumerate(segs):
            src = Wd[krow0:krow0 + kchunks * 128, c0:c0 + n].rearrange("(k p) c -> p k c", p=128)
            P.add("pool", dma(view[:, :, off:off + n], src), writes=[bw] if i == 0 else [], pw=[] if i == 0 else [bw], dma=True)
            off += n
        return view, bw

    gen_ps = Ring([0, 1, 2, 3])

    def phase1(l, seg, tg):
        P.new_phase()
        tabs = a32(0, 4 * TG).rearrange("p (f t) -> p f t", f=4)
        B_tabs = P.abuf("tabs")
        P.add("sp", dma(tabs, tabs_d.rearrange("p (f t) -> p f t", f=4)[:, :, tgs(tg)]), writes=[B_tabs], dma=True)
        if KCUT == 1:
            return
        rmsnorm_to_ain(l, tg, 0)
        if KCUT == 2:
            return
        w_in = W[l]["w_in"]

        def rope(psb, kind, dest, dest_bufs_pw, gcol=None):
            if kind == "plain":
                P.add("act", act(dest, ps[psb][:], AF.Copy), reads=[B_ps[psb]], pw=dest_bufs_pw)
                return
            ci, si = (0, 1) if kind == "A" else (2, 3)
            if kind == "C":
                sq, bsq = sqp.next()
                P.add("act", act(sq[:], ps[psb][:], AF.Square), reads=[B_ps[psb]], writes=[bsq])
                P.add("pe", mm(ps[7][:], [(ones128[:], sq[:])]), reads=[bsq, B_const], writes=[B_ps[7]])
                rstd2, brstd2 = f32p.next()
                rsqrt(rstd2[:], brstd2, ps[7][:], B_ps[7])
            xb, bxb = sqp.next()
            if kind == "C":
                P.add("act", act(xb[:], ps[psb][:], AF.Copy, scale=vec[l][:, gcol:gcol + 1]), reads=[B_ps[psb], B_vec[l]], writes=[bxb])
            else:
                P.add("act", act(xb[:], ps[psb][:], AF.Copy), reads=[B_ps[psb]], writes=[bxb])
            P.add("pe", mm(ps[5][:], [(perm[:], xb[:])]), reads=[bxb, B_perm], writes=[B_ps[5]])
            t1, bt1 = f32p.next()
            if kind == "C":
                P.add("dve", stt(t1[:], ps[psb][:], vec[l][:, gcol:gcol + 1], tabs[:, ci, :], ALU.mult, ALU.mult),
                      reads=[B_ps[psb], B_vec[l], B_tabs], writes=[bt1])
            else:
                P.add("dve", tt(t1[:], ps[psb][:], tabs[:, ci, :], ALU.mult), reads=[B_ps[psb], B_tabs], writes=[bt1])
            t2, bt2 = f32p.next()
            P.add("dve", tt(t2[:], ps[5][:], tabs[:, si, :], ALU.mult), reads=[B_ps[5], B_tabs], writes=[bt2])
            if kind == "C":
                P.add("dve", tt(t1[:], t1[:], t2[:], ALU.add), reads=[bt1, bt2], writes=[bt1])
                P.add("dve", tt(dest, t1[:], rstd2[:], ALU.mult), reads=[bt1, brstd2], pw=dest_bufs_pw)
            else:
                P.add("dve", tt(dest, t1[:], t2[:], ALU.add), reads=[bt1, bt2], pw=dest_bufs_pw)

        def to_sendK(kind, psb, row0, gcol=None):
            stg, bstg = stgp.next()
            rope(psb, kind, stg[:], [bstg], gcol)
            nm_, lr_, _n = kloc(row0)
            P.add("sp", dma(SND[seg][nm_][lr_:lr_ + 128, tgs(tg)], stg[:]), reads=[bstg], pw=[B_snd[seg][nm_]], dma=True, sigbuf=bstg)

        groups = [
            ((0, 1), ("qA", 0)), ((2, 3), ("qA", 2)), ((4, 5), ("kA", 0)), ((6, 7), ("kA", 256)),
            ((12, 16), ("glu", 0)), ((13, 17), ("glu", 1)), ((14, 18), ("glu", 2)), ((15, 19), ("glu", 3)),
            ((20, 21), ("qC", 4)), ((22, 23), ("qC", 6)), ((24, 25), ("kC", 512)),
            ((28, 29), ("qD", 8)), ((30, 31), ("qD", 10)), ((32, 33), ("kD", 768)), ((34, 35), ("kD", 1024)),
        ]
        if KCUT == 3:
            groups = groups[:1]
        if KCUT == 4:
            groups = groups[:5]
        for (c0, c1), (kind, arg) in groups:
            if c1 == c0 + 1:
                view, bw = load_w([(w_in, c0 * 128, 256)], KC)
            else:
                view, bw = load_w([(w_in, c0 * 128, 128), (w_in, c1 * 128, 128)], KC)
            banks = []
            for j in range(2):
                pb = gen_ps.next()
                banks.append(pb)
                P.add("pe", mm(ps[pb][:], [(view[:, kc, j * 128:(j + 1) * 128], ain[:, kc, :]) for kc in range(KC)]),
                      reads=[bw, B_ain], writes=[B_ps[pb]])
            if kind == "glu":
                sg, bsg = f32p.next()
                P.add("act", act(sg[:], ps[banks[1]][:], AF.Sigmoid), reads=[B_ps[banks[1]]], writes=[bsg])
                stg, bstg = stgp.next()
                P.add("dve", tt(stg[:], ps[banks[0]][:], sg[:], ALU.mult), reads=[B_ps[banks[0]], bsg], writes=[bstg])
                P.add("sp", dma(SND[seg]["uu"][arg * 128:(arg + 1) * 128, tgs(tg)], stg[:]), reads=[bstg], pw=[B_snd[seg]["uu"]], dma=True, sigbuf=bstg)
                continue
            for j in range(2):
                pb = banks[j]
                if kind == "qA":
                    rope(pb, "A", qT[:, arg + j, tgs(tg)], [B_q[tg]])
                elif kind == "qC":
                    rope(pb, "C", qT[:, arg + j, tgs(tg)], [B_q[tg]], gcol=205)
                elif kind == "qD":
                    rope(pb, "plain", qT[:, arg + j, tgs(tg)], [B_q[tg]])
                elif kind == "kA":
                    to_sendK("A", pb, arg + j * 128)
                elif kind == "kC":
                    to_sendK("C", pb, arg + j * 128, gcol=206)
                elif kind == "kD":
                    to_sendK("plain", pb, arg + j * 128)
        if KCUT in (3, 4, 5):
            return
        for wc0, vc0 in ((1024, 0), (1280, 256), (3328, 512), (4608, 768), (4864, 1024)):
            view, bw = load_w([(w_in, wc0, 256)], KC)
            for t4 in range(4):
                pb = gen_ps.next()
                P.add("pe", mm(ps[pb][:, 0:256], [(ain[:, kc, t4 * 128:(t4 + 1) * 128], view[:, kc, :]) for kc in range(KC)]),
                      reads=[bw, B_ain], writes=[B_ps[pb]])
                stg, bstg = stgp.next()
                P.add("act", act(stg[:, 0:256], ps[pb][:, 0:256], AF.Copy), reads=[B_ps[pb]], writes=[bstg])
                r0 = tg * TG + t4 * 128
                nm_, lc_, _n = vloc(vc0)
                P.add("sp", dma(SND[seg][nm_][r0:r0 + 128, lc_:lc_ + 256], stg[:, 0:256]), reads=[bstg], pw=[B_snd[seg][nm_]], dma=True, sigbuf=bstg)

    def exchange(seg):
        if not fused:
            return
        for n_ in NAMES:
            P.add("pool", (lambda e, n_=n_: e.collective_compute("AllGather", ALU.bypass, replica_groups=[[0, 1, 2, 3], [4, 5, 6, 7]],
                                                                 ins=[SNDH[seg][n_].ap().opt()], outs=[GTHH[seg][n_].ap().opt()])),
                  reads=[B_snd[seg][n_]], writes=[B_gth[seg][n_]], cc=True)

    def make_windows(seg):
        wK, wV = winK[seg], winV[seg]
        gkD = GTH[seg]["kD"].rearrange("(r k) t -> r k t", r=4)
        guu = GTH[seg]["uu"].rearrange("(r k) t -> r k t", r=4)
        gvD = GTH[seg]["vD"].rearrange("(r k) t -> r k t", r=4)

        def kcopies(e):
            return [
                e.dma_start(out=wK[0:512, 0:384], in_=gkD[bass.ds(DYN["lb"], 1), :, 640:1024].rearrange("o k t -> (o k) t")),
                e.dma_start(out=wK[0:512, 1408:1792], in_=gkD[bass.ds(DYN["rb"], 1), :, 0:384].rearrange("o k t -> (o k) t")),
                e.dma_start(out=wK[512:1024, 0:384], in_=guu[bass.ds(DYN["lb"], 1), :, 640:1024].rearrange("o k t -> (o k) t")),
                e.dma_start(out=wK[512:1024, 1408:1792], in_=guu[bass.ds(DYN["rb"], 1), :, 0:384].rearrange("o k t -> (o k) t")),
            ]
        P.add("pool", kcopies, reads=[B_gth[seg]["kD"], B_gth[seg]["uu"]], writes=[B_winK[seg]], dma=True, ninc=4)
        if fused:
            P.add("sp", dma(wK[0:512, 384:1408], SND[seg]["kD"]), reads=[B_snd[seg]["kD"]], pw=[B_winK[seg]], dma=True)
            P.add("sp", dma(wK[512:1024, 384:1408], SND[seg]["uu"]), reads=[B_snd[seg]["uu"]], pw=[B_winK[seg]], dma=True)
        else:
            def kown(e):
                return [
                    e.dma_start(out=wK[0:512, 384:1408], in_=gkD[bass.ds(DYN["rk"], 1), :, :].rearrange("o k t -> (o k) t")),
                    e.dma_start(out=wK[512:1024, 384:1408], in_=guu[bass.ds(DYN["rk"], 1), :, :].rearrange("o k t -> (o k) t")),
                ]
            P.add("pool", kown, reads=[B_gth[seg]["kD"], B_gth[seg]["uu"]], pw=[B_winK[seg]], dma=True, ninc=2)

        def vcopies(e):
            return [
                e.dma_start(out=wV[0:384, :], in_=gvD[bass.ds(DYN["lb"], 1), 640:1024, :].rearrange("o k t -> (o k) t")),
                e.dma_start(out=wV[1408:1792, :], in_=gvD[bass.ds(DYN["rb"], 1), 0:384, :].rearrange("o k t -> (o k) t")),
            ]
        P.add("pool", vcopies, reads=[B_gth[seg]["vD"]], writes=[B_winV[seg]], dma=True, ninc=2)
        if fused:
            P.add("sp", dma(wV[384:1408, :], SND[seg]["vD"]), reads=[B_snd[seg]["vD"]], pw=[B_winV[seg]], dma=True)
        else:
            P.add("pool", (lambda e: e.dma_start(out=wV[384:1408, :], in_=gvD[bass.ds(DYN["rk"], 1), :, :].rearrange("o k t -> (o k) t"))),
                  reads=[B_gth[seg]["vD"]], pw=[B_winV[seg]], dma=True)

    def phase2a(l, seg, tg):
        cat = ain
        P.new_phase()
        UW = 542
        uw = a16(0, 4 * UW // 2).rearrange("p (c t) -> p c t", c=4)
        y = a32(1088, 4 * TG).rearrange("p (c t) -> p c t", c=4)
        acc1 = a32(3136, TG)
        zs = a16(3648, 4 * TG // 2).rearrange("p (c t) -> p c t", c=4)
        mean = a32(4672, TG)
        B_uw, B_y, B_acc1, B_zs, B_mean = (P.abuf(n) for n in ("uw", "y", "acc1", "zs", "mean"))
        U0 = 1280

        c0w = 369 if tg == 0 else 881
        P.add("sp", dma(uw, winK[seg][512:1024, c0w:c0w + 542].rearrange("(c p) t -> p c t", p=128)), reads=[B_winK[seg]], writes=[B_uw], dma=True)
        hs = (slice(0, 15), 0) if tg == 0 else (slice(527, 542), 1)
        P.add("dve", (lambda e: e.tensor_scalar_mul(out=uw[:, :, hs[0]], in0=uw[:, :, hs[0]], scalar1=halo[:, hs[1]:hs[1] + 1])),
              reads=[B_uw, B_halo], writes=[B_uw])
        V = vec[l]
        for c in range(4):
            P.add("dve", (lambda e, c=c: e.tensor_scalar_mul(out=y[:, c, :], in0=uw[:, c, 0:TG], scalar1=V[:, 64 + c * 31:65 + c * 31])),
                  reads=[B_uw, B_vec[l]], pw=[B_y])
            P.add("dve", (lambda e, c=c: e.tensor_scalar_mul(out=acc1, in0=uw[:, c, 1:1 + TG], scalar1=V[:, 65 + c * 31:66 + c * 31])),
                  reads=[B_uw, B_vec[l]], writes=[B_acc1])
            for j in range(2, 31):
                dst, bd = (y[:, c, :], B_y) if j % 2 == 0 else (acc1, B_acc1)
                P.add("dve", stt(dst, uw[:, c, j:j + TG], V[:, 64 + c * 31 + j:65 + c * 31 + j], dst, ALU.mult, ALU.add),
                      reads=[B_uw, B_vec[l], bd], pw=[bd])
            P.add("dve", stt(y[:, c, :], acc1, V[:, 188 + c:189 + c], y[:, c, :], ALU.add, ALU.add), reads=[B_acc1, B_y, B_vec[l]], pw=[B_y])
            yb, byb = sqp.next()
            P.add("act", act(yb[:], y[:, c, :], AF.Copy), reads=[B_y], writes=[byb])
            P.add("pe", mm(ps[4][:], [(ones512[:], yb[:])], first=(c == 0), last=(c == 3)), reads=[byb, B_const],
                  writes=[B_ps[4]] if c == 0 else [], pw=[] if c == 0 else [B_ps[4]])
            ysq, bysq = sqp.next()
            P.add("act", act(ysq[:], y[:, c, :], AF.Square), reads=[B_y], writes=[bysq])
            P.add("pe", mm(ps[5][:], [(ones512[:], ysq[:])], first=(c == 0), last=(c == 3)), reads=[bysq, B_const],
                  writes=[B_ps[5]] if c == 0 else [], pw=[] if c == 0 else [B_ps[5]])
        if debug and seg == 0 and tg == 0:
            P.add("sp", dma(dbg_y.rearrange("p (c t) -> p c t", c=4), y), reads=[B_y], pw=[B_dbg], dma=True, sigbuf=B_y)
        P.add("act", act(mean, ps[4][:], AF.Copy), reads=[B_ps[4]], writes=[B_mean])
        var, bvar = a32(5184, TG), P.abuf("cvar")
        P.add("dve", stt(var[:], mean, -1.0, mean, ALU.mult, ALU.mult), reads=[B_mean], writes=[bvar])
        P.add("dve", tt(var[:], var[:], ps[5][:], ALU.add), reads=[bvar, B_ps[5]], writes=[bvar])
        rsqrt(var[:], bvar, var[:], bvar)
        for c in range(4):
            t1, bt1 = f32p.next()
            P.add("dve", tt(t1[:], y[:, c, :], mean, ALU.subtract), reads=[B_y, B_mean], writes=[bt1])
            P.add("dve", tt(t1[:], t1[:], var[:], ALU.mult), reads=[bt1, bvar], writes=[bt1])
            P.add("act", act(zs[:, c, :], t1[:], AF.Silu, scale=V[:, 192 + c:193 + c], bias=V[:, 196 + c:197 + c]),
                  reads=[bt1, B_vec[l]], pw=[B_zs])
        if debug and seg == 0 and tg == 0:
            P.add("sp", dma(dbg_zs.rearrange("p (c t) -> p c t", c=4), zs), reads=[B_zs], pw=[B_dbg], dma=True, sigbuf=B_zs)
        view, bw = load_w([(W[l]["pw"], 0, 512)], 4)
        for oc in range(4):
            pb = gen_ps.next()
            P.add("pe", mm(ps[pb][:], [(view[:, kc, oc * 128:(oc + 1) * 128], zs[:, kc, :]) for kc in range(4)]),
                  reads=[bw, B_zs], writes=[B_ps[pb]])
            P.add("act", act(cat[:, 4 + oc, :], ps[pb][:], AF.Identity, bias=V[:, 200 + oc:201 + oc]), reads=[B_ps[pb], B_vec[l]], pw=[B_ain])

        P.new_phase()
        kq = [a16(r * 512, 512) for r in range(4)]
        vq = [a16(2048 + r * 512, 512).rearrange("p (t d) -> p t d", d=128) for r in range(4)]
        bkq = [P.abuf("kq%d" % r) for r in range(4)]
        bvq = [P.abuf("vq%d" % r) for r in range(4)]
        biasT = a32(4096, 28 * 128).rearrange("p (n q) -> p n q", q=128)
        dm = a16(7680, 4 * 7 * 64).rearrange("p (i j q) -> p i j q", i=4, j=7)
        B_bias = P.abuf("biasT")
        B_dm = P.abuf("dm")
        P.add("sp", dma(biasT, W[l]["bias"].rearrange("p (n q) -> p n q", q=128)), writes=[B_bias], dma=True)
        P.add("pool", dma(dm, dmask_d.rearrange("p (i j q) -> p i j q", i=8, j=7)[:, tg * 4:(tg + 1) * 4]), writes=[B_dm], dma=True)
        st_ring = Ring([0, 1])

        def load_kv(krow0, vcol0):
            for r in range(4):
                kn_, lr_, kn = kloc(krow0)
                vn_, lc_, _vn = vloc(vcol0)
                P.add("sp", dma(kq[r], GTH[seg][kn_][r * kn + lr_:r * kn + lr_ + 128, :]), reads=[B_gth[seg][kn_]], writes=[bkq[r]], dma=True)
                P.add("sp", dma(vq[r], GTH[seg][vn_][r * T:(r + 1) * T, lc_:lc_ + 128].rearrange("(t p) c -> p t c", p=128)),
                      reads=[B_gth[seg][vn_]], writes=[bvq[r]], dma=True)

        def attn(q_ap, kpart, scale, ob, sb_):
            pend = None
            for kt in range(33):
                cur = None
                if kt < 32:
                    r, kl = kt // 8, kt % 8
                    sbk = st_ring.next()
                    P.add("pe", mm(ps[sbk][:], [(kq[r][kpart, kl * 128:(kl + 1) * 128], q_ap)]), reads=[bkq[r], B_q[tg]], writes=[B_ps[sbk]])
                    pT, bpt = ptp.next()
                    P.add("act", act(pT[:], ps[sbk][:], AF.Exp, scale=scale), reads=[B_ps[sbk]], writes=[bpt])
                    cur = (kt, r, kl, pT, bpt)
                if pend is not None:
                    k0, r0_, kl0, pT0, bpt0 = pend
                    w_, p_ = ([B_ps[ob]], []) if k0 == 0 else ([], [B_ps[ob]])
                    P.add("pe", mm(ps[ob][:], [(vq[r0_][:, kl0, :], pT0[:])], first=(k0 == 0), last=(k0 == 31)), reads=[bvq[r0_], bpt0], writes=w_, pw=p_)
                    w_, p_ = ([B_ps[sb_]], []) if k0 == 0 else ([], [B_ps[sb_]])
                    P.add("pe", mm(ps[sb_][:], [(ones1[:], pT0[:])], first=(k0 == 0), last=(k0 == 31)), reads=[bpt0, B_const], writes=w_, pw=p_)
                pend = cur

        lv = lamv[l]
        for h in range(4):
            load_kv(h * 128, h * 128)
            for m in range(2):
                kp = slice(m * 64, (m + 1) * 64)
                attn(qT[kp, h, tgs(tg)], kp, 0.125, 2 + m, 4 + m)
            r0, br0 = f32p.next()
            recip(r0[:], br0, ps[4][:], B_ps[4])
            P.add("dve", tt(r0[:], ps[2][:], r0[:], ALU.mult), reads=[B_ps[2], br0], writes=[br0])
            r1, br1 = f32p.next()
            recip(r1[:], br1, ps[5][:], B_ps[5])
            P.add("dve", tt(r1[:], ps[3][:], r1[:], ALU.mult), reads=[B_ps[3], br1], writes=[br1])
            P.add("dve", stt(r1[:], r1[:], lv[:, 4:5], r0[:], ALU.mult, ALU.add), reads=[br0, br1, B_lamv[l]], writes=[br1])
            sq, bsq = sqp.next()
            P.add("act", act(sq[:], r1[:], AF.Square), reads=[br1], writes=[bsq])
            P.add("pe", mm(ps[6][:], [(ones128[:], sq[:])]), reads=[bsq, B_const], writes=[B_ps[6]])
            rs_, brs = f32p.next()
            rsqrt(rs_[:], brs, ps[6][:], B_ps[6])
            P.add("dve", stt(cat[:, h, :], r1[:], lv[:, 5:6], rs_[:], ALU.mult, ALU.mult), reads=[br1, brs, B_lamv[l]], pw=[B_ain])
        sC = 128 ** -0.5
        for g in range(2):
            load_kv(512 + g * 128, 512 + g * 128)
            for rr in range(2):
                hq = 2 * g + rr
                attn(qT[:, 4 + hq, tgs(tg)], slice(0, 128), sC, 2, 4)
                rc, brc = f32p.next()
                recip(rc[:], brc, ps[4][:], B_ps[4])
                P.add("dve", tt(cat[:, 8 + hq, :], ps[2][:], rc[:], ALU.mult), reads=[B_ps[2], brc], pw=[B_ain])
        for h in range(4):
            wK, wV = winK[seg], winV[seg]
            for part, (c_lo, c_hi) in enumerate(((0, 384), (384, 1408), (1408, 1792))):
                n_ = c_hi - c_lo
                P.add("sp", dma(kq[part][:, 0:n_], wK[h * 128:(h + 1) * 128, c_lo:c_hi]), reads=[B_winK[seg]], writes=[bkq[part]], dma=True)
                P.add("sp", dma(vq[part][:, 0:n_ // 128, :], wV[c_lo:c_hi, h * 128:(h + 1) * 128].rearrange("(t p) c -> p t c", p=128)),
                      reads=[B_winV[seg]], writes=[bvq[part]], dma=True)
            def kwin(wt):
                if wt < 3:
                    return kq[0][:, wt * 128:(wt + 1) * 128]
                if wt < 11:
                    return kq[1][:, (wt - 3) * 128:(wt - 2) * 128]
                return kq[2][:, (wt - 11) * 128:(wt - 10) * 128]

            def vwin(wt):
                if wt < 3:
                    return vq[0][:, wt, :]
                if wt < 11:
                    return vq[1][:, wt - 3, :]
                return vq[2][:, wt - 11, :]
            for i in range(4):
                lt = tg * 4 + i
                pend = []
                for j in range(7 + 2):
                    if j < 7:
                        wt = lt + j
                        sbk = st_ring.next()
                        P.add("pe", mm(ps[sbk][:, 0:128], [(kwin(wt), qT[:, 8 + h, tg * TG + i * 128:tg * TG + (i + 1) * 128])]),
                              reads=[bkq[0], bkq[1], bkq[2], B_q[tg]], writes=[B_ps[sbk]])
                        t_, bt_ = f32p.next()
                        P.add("dve", stt(t_[:, 0:128], ps[sbk][:, 0:128], sC, biasT[:, h * 7 + j, :], ALU.mult, ALU.add),
                              reads=[B_ps[sbk], B_bias], writes=[bt_])
                        e_, be_ = sqp.next()
                        P.add("act", act(e_[:, 0:128], t_[:, 0:128], AF.Exp), reads=[bt_], writes=[be_])
                        pT, bpt = ptp.next()
                        P.add("dve", tt(pT[:, 0:128], e_[:, 0:128], dm[:, i, j, :], ALU.mult), reads=[be_, B_dm], writes=[bpt])
                        pend.append((j, wt, pT, bpt))
                    if j >= 2:
                        j0, wt0, pT0, bpt0 = pend.pop(0)
                        w_, p_ = ([B_ps[3]], []) if j0 == 0 else ([], [B_ps[3]])
                        P.add("pe", mm(ps[3][:, 0:128], [(vwin(wt0), pT0[:, 0:128])], first=(j0 == 0), last=(j0 == 6)),
                              reads=[bvq[0], bvq[1], bvq[2], bpt0], writes=w_, pw=p_)
                        w_, p_ = ([B_ps[5]], []) if j0 == 0 else ([], [B_ps[5]])
                        P.add("pe", mm(ps[5][:, 0:128], [(ones1[:], pT0[:, 0:128])], first=(j0 == 0), last=(j0 == 6)), reads=[bpt0, B_const], writes=w_, pw=p_)
                rc, brc = f32p.next()
                recip(rc[:, 0:128], brc, ps[5][:, 0:128], B_ps[5])
                P.add("dve", tt(cat[:, 12 + h, i * 128:(i + 1) * 128], ps[3][:, 0:128], rc[:, 0:128], ALU.mult), reads=[B_ps[3], brc], pw=[B_ain])

    def phase2b(l, tg):
        P.new_phase()
        mixed = a32(0, KC * TG).rearrange("p (c t) -> p c t", c=KC)
        B_mixed = P.abuf("mixed")
        cat = ain
        for g in range(8):
            view, bw = load_w([(W[l]["w_out"], g * 256, 256)], KC)
            for j in range(2):
                oc = g * 2 + j
                pb = gen_ps.next()
                P.add("pe", mm(ps[pb][:], [(view[:, kc, j * 128:(j + 1) * 128], cat[:, kc, :]) for kc in range(KC)]),
                      reads=[bw, B_ain], writes=[B_ps[pb]])
                P.add("dve", (lambda e, oc=oc, pb=pb: e.tensor_copy(out=mixed[:, oc, :], in_=ps[pb][:])), reads=[B_ps[pb]], pw=[B_mixed])
                sq, bsq = sqp.next()
                P.add("act", act(sq[:], ps[pb][:], AF.Square), reads=[B_ps[pb]], writes=[bsq])
                P.add("pe", mm(ps[6][:], [(onesD[:], sq[:])], first=(oc == 0), last=(oc == KC - 1)), reads=[bsq, B_const],
                      writes=[B_ps[6]] if oc == 0 else [], pw=[] if oc == 0 else [B_ps[6]])
        rstd, brstd = f32p.next()
        rsqrt(rstd[:], brstd, ps[6][:], B_ps[6])
        for oc in range(KC):
            P.add("dve", stt(mixed[:, oc, :], mixed[:, oc, :], vec[l][:, 16 + oc:17 + oc], rstd[:], ALU.mult, ALU.mult),
                  reads=[B_mixed, brstd, B_vec[l]], pw=[B_mixed])
            P.add("dve", tt(xT[:, oc, tgs(tg)], xT[:, oc, tgs(tg)], mixed[:, oc, :], ALU.add), reads=[B_mixed, B_x[tg]], pw=[B_x[tg]])

    def phase2c(l, tg):
        P.new_phase()
        fT = a32(0, KC * TG).rearrange("p (c t) -> p c t", c=KC)
        actT = a16(8192, 22 * TG // 2).rearrange("p (c t) -> p c t", c=22)
        B_fT = P.abuf("fT")
        B_act = P.abuf("act")
        rmsnorm_to_ain(l, tg, 32)
        for hf in range(2):
            for pr in range(11):
                fl0 = 2 * pr
                fc0 = hf * 22 + fl0
                viewg, bwg = load_w([(W[l]["gate"], fc0 * 128, 256)], KC)
                pg = [gen_ps.next(), gen_ps.next()]
                for j in range(2):
                    P.add("pe", mm(ps[pg[j]][:], [(viewg[:, kc, j * 128:(j + 1) * 128], ain[:, kc, :]) for kc in range(KC)]),
                          reads=[bwg, B_ain], writes=[B_ps[pg[j]]])
                sgs = []
                for j in range(2):
                    sg, bsg = f32p.next()
                    P.add("act", act(sg[:], ps[pg[j]][:], AF.Silu), reads=[B_ps[pg[j]]], writes=[bsg])
                    sgs.append((sg, bsg))
                viewu, bwu = load_w([(W[l]["up"], fc0 * 128, 256)], KC)
                pu = [gen_ps.next(), gen_ps.next()]
                for j in range(2):
                    P.add("pe", mm(ps[pu[j]][:], [(viewu[:, kc, j * 128:(j + 1) * 128], ain[:, kc, :]) for kc in range(KC)]),
                          reads=[bwu, B_ain], writes=[B_ps[pu[j]]])
                for j in range(2):
                    P.add("dve", tt(actT[:, fl0 + j, :], ps[pu[j]][:], sgs[j][0][:], ALU.mult), reads=[B_ps[pu[j]], sgs[j][1]], pw=[B_act])
            for oc in range(KC):
                view, bw = load_w([(W[l]["down"], oc * 128, 128)], 22, krow0=hf * 22 * 128)
                for j in range(1):
                    pb = gen_ps.next()
                    P.add("pe", mm(ps[pb][:], [(view[:, fl, :], actT[:, fl, :]) for fl in range(22)]), reads=[bw, B_act], writes=[B_ps[pb]])
                    if hf == 0:
                        P.add("act", act(fT[:, oc, :], ps[pb][:], AF.Copy), reads=[B_ps[pb]], pw=[B_fT])
                    else:
                        P.add("dve", tt(fT[:, oc, :], fT[:, oc, :], ps[pb][:], ALU.add), reads=[B_ps[pb], B_fT], pw=[B_fT])
                        sq, bsq = sqp.next()
                        P.add("act", act(sq[:], fT[:, oc, :], AF.Square), reads=[B_fT], writes=[bsq])
                        P.add("pe", mm(ps[6][:], [(onesD[:], sq[:])], first=(oc == 0), last=(oc == KC - 1)), reads=[bsq, B_const],
                              writes=[B_ps[6]] if oc == 0 else [], pw=[] if oc == 0 else [B_ps[6]])
        rstd, brstd = f32p.next()
        rsqrt(rstd[:], brstd, ps[6][:], B_ps[6])
        for oc in range(KC):
            P.add("dve", stt(fT[:, oc, :], fT[:, oc, :], vec[l][:, 48 + oc:49 + oc], rstd[:], ALU.mult, ALU.mult),
                  reads=[B_fT, brstd, B_vec[l]], pw=[B_fT])
            P.add("dve", tt(xT[:, oc, tgs(tg)], xT[:, oc, tgs(tg)], fT[:, oc, :], ALU.add), reads=[B_fT, B_x[tg]], pw=[B_x[tg]])

    def dyn_init(e):
        pid = e.partition_id()
        rk = e.snap(pid % 4)
        DYN["rk"] = rk
        DYN["lb"] = e.snap((rk + 3) % 4)
        DYN["rb"] = e.snap((rk + 1) % 4)
        return None
    P.ops["pool"].append(_mk_raw(dyn_init))
    setup_consts()
    if first_seg:
        for tg in range(NTG):
            P.add("sp", dma(xT[:, :, tgs(tg)], xT_d.rearrange("(c p) t -> p c t", p=128)[:, :, tgs(tg)]), writes=[B_x[tg]], dma=True)
    else:
        for tg in range(NTG):
            P.add("sp", dma(xT[:, :, tgs(tg)], stx_in.rearrange("p (c t) -> p c t", c=KC)[:, :, tgs(tg)]), writes=[B_x[tg]], dma=True)
            P.add("sp", dma(qT[:, :, tgs(tg)], stq_in.rearrange("p (c t) -> p c t", c=12)[:, :, tgs(tg)]), writes=[B_q[tg]], dma=True)
    for seg in range(seg_lo, seg_hi + 1):
        if seg >= 1:
            lprev = seg - 1
            make_windows(seg - 1)
            for tg in range(NTG):
                phase2a(lprev, seg - 1, tg)
                if debug and seg == 1:
                    P.add("sp", dma(dbg_cat[tg], ain[:].rearrange("p c t -> p (c t)")), reads=[B_ain], pw=[B_dbg], dma=True, sigbuf=B_ain)
                phase2b(lprev, tg)
                if debug and seg == 1:
                    P.add("sp", dma(dbg_xmid.rearrange("p (c t) -> p c t", c=KC)[:, :, tgs(tg)], xT[:, :, tgs(tg)]), reads=[B_x[tg]], pw=[B_dbg], dma=True, sigbuf=B_x[tg])
                phase2c(lprev, tg)
        if seg <= 1:
            for tg in range(NTG):
                phase1(seg, seg, tg)
            exchange(seg)
    finals = []
    if last_seg:
        for tg in range(NTG):
            P.add("sp", dma(out_d.rearrange("(c p) t -> p c t", p=128)[:, :, tgs(tg)], xT[:, :, tgs(tg)]), reads=[B_x[tg]], pw=[B_out], dma=True, sigbuf=B_x[tg])
        finals = [B_out]
    elif not fused:
        for tg in range(NTG):
            P.add("sp", dma(stx_out.rearrange("p (c t) -> p c t", c=KC)[:, :, tgs(tg)], xT[:, :, tgs(tg)]), reads=[B_x[tg]], pw=[B_st], dma=True, sigbuf=B_x[tg])
            P.add("sp", dma(stq_out.rearrange("p (c t) -> p c t", c=12)[:, :, tgs(tg)], qT[:, :, tgs(tg)]), reads=[B_q[tg]], pw=[B_st], dma=True, sigbuf=B_q[tg])
        finals = [B_st, B_dbg] + list(B_snd[seg_lo].values())
    P.add("sp", None, reads=finals)
    P.finalize(nc, stack)
    stack.close()
    nc._in_names = in_names
    return nc


def _host_consts():
    theta = 10000.0
    inv = np.power(theta, -np.arange(0, 64, 2, dtype=np.float32) / 64).astype(np.float32)
    p = np.arange(128)
    j = p % 32
    sign = np.where((p % 64) < 32, -1.0, 1.0).astype(np.float32)
    tabs, dmasks, halos = [], [], []
    for rank in range(4):
        s = (rank * T + np.arange(T))
        posA = s.astype(np.float32)
        angA = posA[None, :] * inv[j][:, None]
        row = (s // 64).astype(np.float32)
        col = (s % 64).astype(np.float32)
        posC = np.where((p < 64)[:, None], row[None, :], col[None, :]).astype(np.float32)
        angC = posC * inv[j][:, None]
        tab = np.stack([np.cos(angA), np.sin(angA) * sign[:, None], np.cos(angC), np.sin(angC) * sign[:, None]], axis=1)
        tabs.append(np.ascontiguousarray(tab.reshape(128, 4 * T).astype(np.float32)))
        dmk = np.zeros((128, 8, 7, 128), np.float32)
        kk = np.arange(128)
        kr_par, kc = kk // 64, kk % 64
        qq = np.arange(128)
        qr_l, qc = qq // 64, qq % 64
        for lt in range(8):
            b = rank * 8 + lt
            qr = 2 * b + qr_l
            win_r = np.clip(qr - 4, 0, 56)
            win_c = np.clip(qc - 8, 0, 48)
            for jj in range(7):
                kr = 2 * b + 2 * (jj - 3) + kr_par
                ok = ((kr[:, None] >= 0) & (kr[:, None] < 64) & (kr[:, None] >= win_r[None, :]) & (kr[:, None] < win_r[None, :] + 8)
                      & (kc[:, None] >= win_c[None, :]) & (kc[:, None] < win_c[None, :] + 16))
                dmk[:, lt, jj, :] = ok
        dmasks.append(np.ascontiguousarray(dmk.reshape(128, 8 * 7 * 128)))
        hl = np.zeros((128, 2), np.float32)
        hl[:, 0] = 0.0 if rank == 0 else 1.0
        hl[:, 1] = 0.0 if rank == 3 else 1.0
        halos.append(hl)
    m = np.arange(128)
    perm = np.zeros((128, 128), np.float32)
    perm[m ^ 32, m] = 1.0
    return tabs, dmasks, halos, perm


def _layer_inputs(inp, l):
    def pc(v, n):
        return np.ascontiguousarray(np.asarray(v, np.float32).reshape(n, 128).T)
    vec = np.zeros((128, NV), np.float32)
    vec[:, 0:16] = pc(inp["norm_mix_pre"][l], 16)
    vec[:, 16:32] = pc(inp["norm_mix_post"][l], 16)
    vec[:, 32:48] = pc(inp["norm_ffn_pre"][l], 16)
    vec[:, 48:64] = pc(inp["norm_ffn_post"][l], 16)
    dw = np.asarray(inp["conv_dw"][l], np.float32)
    vec[:, 64:188] = dw.reshape(31, 4, 128).transpose(2, 1, 0).reshape(128, 124)
    vec[:, 188:192] = pc(inp["conv_dw_b"][l], 4)
    vec[:, 192:196] = pc(inp["conv_ln_g"][l], 4)
    vec[:, 196:200] = pc(inp["conv_ln_b"][l], 4)
    vec[:, 200:204] = pc(inp["conv_pw_b"][l], 4)
    vec[:, 204] = np.asarray(inp["diff_subln"][l], np.float32)
    vec[:, 205] = np.asarray(inp["gqa_q_norm"][l], np.float32)
    vec[:, 206] = np.asarray(inp["gqa_k_norm"][l], np.float32)
    lamb = np.ascontiguousarray(np.broadcast_to(np.asarray(inp["diff_lambda"][l], np.float32).reshape(1, 256), (128, 256)))
    rpb = np.asarray(inp["na_rpb"][l], np.float32)
    kk = np.arange(128)
    kr_par, kc = kk // 64, kk % 64
    qq = np.arange(128)
    qr_l, qc = qq // 64, qq % 64
    bias = np.zeros((128, 4, 7, 128), np.float32)
    for jj in range(7):
        dr = 2 * (jj - 3) + kr_par[:, None] - qr_l[None, :]
        ir = np.clip(dr + 7, 0, 14)
        ic = np.clip(kc[:, None] - qc[None, :] + 15, 0, 30)
        for h in range(4):
            bias[:, h, jj, :] = rpb[h][ir, ic]
    d = {
        "w_in%d" % l: np.ascontiguousarray(inp["w_in"][l], np.float32), "w_out%d" % l: np.ascontiguousarray(inp["w_out"][l], np.float32),
        "gate%d" % l: np.ascontiguousarray(inp["ffn_gate"][l], np.float32), "up%d" % l: np.ascontiguousarray(inp["ffn_up"][l], np.float32),
        "down%d" % l: np.ascontiguousarray(inp["ffn_down"][l], np.float32), "pw%d" % l: np.ascontiguousarray(inp["conv_pw"][l], np.float32),
        "vec%d" % l: vec, "lamb%d" % l: lamb, "bias%d" % l: np.ascontiguousarray(bias.reshape(128, 28 * 128)),
    }
    return d


_NC_CACHE = {}


def _get_nc(lo, hi, fused):
    key = (lo, hi, fused)
    if key not in _NC_CACHE:
        _NC_CACHE[key] = build_program(lo, hi, fused)
    return _NC_CACHE[key]


def kernel(**inp):
    inp = {k: np.asarray(v) for k, v in inp.items()}
    x = inp["x"].astype(np.float32, copy=False)
    tabs, dmasks, halos, perm = _host_consts()
    lay = {l: _layer_inputs(inp, l) for l in range(L)}
    cores = list(range(8))

    def common(c):
        r = c % 4
        return {"tabs": tabs[r], "dmask": dmasks[r], "halo": halos[r], "perm": perm}

    def xT_of(c):
        b, r = c // 4, c % 4
        return np.ascontiguousarray(x[b, r * T:(r + 1) * T, :].T)

    if FUSED:
        nc = _get_nc(0, 2, True)
        maps = []
        for c in cores:
            m = common(c)
            m.update(lay[0])
            m.update(lay[1])
            m["xT"] = xT_of(c)
            maps.append(m)
        maps = [{k: m[k] for k in nc._in_names} for m in maps]
        res = run_bass_kernel_spmd(nc, maps, core_ids=cores)
        outs = [np.asarray(res.results[c]["outT"]) for c in cores]
    else:
        state = None
        outs = None
        for seg in range(3):
            nc = _get_nc(seg, seg, False)
            maps = []
            for c in cores:
                m = common(c)
                for l in sorted({0 if seg <= 1 else 1, 1 if seg >= 1 else 0}):
                    m.update(lay[l])
                if seg == 0:
                    m["xT"] = xT_of(c)
                else:
                    g0 = (c // 4) * 4
                    m["stx_in"] = state[c]["stx_out"]
                    m["stq_in"] = state[c]["stq_out"]
                    for n_ in ("kA", "kC", "kD", "uu", "vA", "vC", "vD"):
                        m["gth_" + n_] = np.concatenate([state[g0 + r]["snd_" + n_] for r in range(4)], axis=0)
                maps.append(m)
            maps = [{k: m[k] for k in nc._in_names} for m in maps]
            res = run_bass_kernel_spmd(nc, maps, core_ids=cores)
            if seg < 2:
                state = [{k: np.asarray(v) for k, v in res.results[c].items()} for c in cores]
            else:
                outs = [np.asarray(res.results[c]["outT"]) for c in cores]
    out = np.zeros((2, 4096, D), np.float32)
    for c in cores:
        b, r = c // 4, c % 4
        out[b, r * T:(r + 1) * T, :] = outs[c].T
    return out
```
